# Optimizing a Trainium2 kernel written in Bass

```python
import math
import jax, jax.numpy as jnp
from jax import lax
import numpy as np

D_MODEL = 1024
BATCH = 4
SEQ = 8192
DEPTH = 4

CHUNK = 64
N_PREV = 8
BAND = (N_PREV + 1) * CHUNK
N_HEADS = 16
HEAD_DIM = D_MODEL // N_HEADS
E_MIX = N_HEADS * HEAD_DIM
REL_CLIP = 128
N_REL = 2 * REL_CLIP + 1
CONV_W = 3
N_MEM = 256
MEM_HEADS = 4
MEM_HEAD_DIM = 128
E_MEM = MEM_HEADS * MEM_HEAD_DIM
E_BRANCH = E_MIX + E_MEM
N_IN = 3 * E_MIX + E_MEM + E_BRANCH
N_MIXERS = 2
N_ATTN_LAYERS = (DEPTH + 1) // 2
N_CONV_LAYERS = DEPTH // 2
DN_ALPHA = (2.0 * DEPTH) ** 0.25
DN_BETA = (8.0 * DEPTH) ** -0.25
LN_EPS = 1e-5

kernel_name = "hybrid_chunk_attn_shortconv_mem_deepnorm"


def layer_norm(x, g, b):
    xf = x.astype(jnp.float32)
    mu = jnp.mean(xf, axis=-1, keepdims=True)
    var = jnp.mean(jnp.square(xf - mu), axis=-1, keepdims=True)
    y = (xf - mu) * lax.rsqrt(var + LN_EPS) * g.astype(jnp.float32) + b.astype(jnp.float32)
    return y.astype(x.dtype)


def rel_bias_band(table):
    i = jnp.arange(CHUNK)[:, None]
    m = jnp.arange(BAND)[None, :]
    rel = N_PREV * CHUNK + i - m
    idx = jnp.clip(rel, -REL_CLIP, REL_CLIP) + REL_CLIP
    return table[:, idx]


def chunked_attention(q, k, v, bias_table):
    b, s, h, dh = q.shape
    n_chunks = s // CHUNK
    pad = ((0, 0), (N_PREV * CHUNK, 0), (0, 0), (0, 0))
    k_pad = jnp.pad(k, pad)
    v_pad = jnp.pad(v, pad)
    bias = rel_bias_band(bias_table).astype(jnp.float32)
    scale = 1.0 / math.sqrt(dh)
    q_blocks = jnp.moveaxis(q.reshape(b, n_chunks, CHUNK, h, dh), 1, 0)
    neg = jnp.finfo(jnp.float32).min

    def one_chunk(args):
        q_blk, c = args
        k_band = lax.dynamic_slice_in_dim(k_pad, c * CHUNK, BAND, axis=1)
        v_band = lax.dynamic_slice_in_dim(v_pad, c * CHUNK, BAND, axis=1)
        sc = jnp.einsum('bqhd,bkhd->bhqk', q_blk, k_band).astype(jnp.float32) * scale + bias[None]
        key_pos = (c - N_PREV) * CHUNK + jnp.arange(BAND)
        sc = jnp.where((key_pos >= 0)[None, None, None, :], sc, neg)
        p = jax.nn.softmax(sc, axis=-1).astype(v_band.dtype)
        return jnp.einsum('bhqk,bkhd->bqhd', p, v_band)

    out = lax.map(one_chunk, (q_blocks, jnp.arange(n_chunks)))
    return jnp.moveaxis(out, 0, 1).reshape(b, s, h * dh)


def causal_dwconv(u, w):
    c = u.shape[-1]
    return lax.conv_general_dilated(
        u, w[:, None, :].astype(u.dtype), window_strides=(1,),
        padding=[(CONV_W - 1, 0)], dimension_numbers=('NWC', 'WIO', 'NWC'),
        feature_group_count=c)


def short_gated_conv(bg, cg, u, w):
    return bg * causal_dwconv(cg * u, w)


def memory_attention(q_mem, kv_mem):
    b, s, _ = q_mem.shape
    q = q_mem.reshape(b, s, MEM_HEADS, MEM_HEAD_DIM)
    k, v = jnp.split(kv_mem, 2, axis=-1)
    k = k.reshape(b, -1, MEM_HEADS, MEM_HEAD_DIM)
    v = v.reshape(b, -1, MEM_HEADS, MEM_HEAD_DIM)
    sc = jnp.einsum('bshd,bmhd->bhsm', q, k).astype(jnp.float32) / math.sqrt(MEM_HEAD_DIM)
    p = jax.nn.softmax(sc, axis=-1).astype(v.dtype)
    return jnp.einsum('bhsm,bmhd->bshd', p, v).reshape(b, s, E_MEM)


def setup_inputs(seed: int = 0) -> dict:
    key = jax.random.key(seed)
    ks = jax.random.split(key, 10)
    x = jax.random.normal(ks[0], (BATCH, SEQ, D_MODEL), jnp.float32)
    mem = jax.random.normal(ks[1], (BATCH, N_MEM, D_MODEL), jnp.float32)
    w_in = jax.random.normal(ks[2], (DEPTH, D_MODEL, N_IN), jnp.float32) * D_MODEL ** -0.5
    w_mem_kv = jax.random.normal(ks[3], (DEPTH, D_MODEL, 2 * E_MEM), jnp.float32) * D_MODEL ** -0.5
    w_out = jax.random.normal(ks[4], (DEPTH, E_BRANCH, D_MODEL), jnp.float32) * (E_BRANCH ** -0.5 * DN_BETA)
    rel_bias = jax.random.normal(ks[5], (N_ATTN_LAYERS, N_HEADS, N_REL), jnp.float32) * 0.5
    conv_w = jax.random.normal(ks[6], (N_CONV_LAYERS, CONV_W, E_MIX), jnp.float32) * CONV_W ** -0.5
    ln_g = 1.0 + 0.05 * jax.random.normal(ks[7], (DEPTH, D_MODEL), jnp.float32)
    ln_b = 0.02 * jax.random.normal(ks[8], (DEPTH, D_MODEL), jnp.float32)
    return {"x": x, "mem": mem, "w_in": w_in, "w_mem_kv": w_mem_kv, "w_out": w_out,
            "rel_bias": rel_bias, "conv_w": conv_w, "ln_g": ln_g, "ln_b": ln_b}


def reference(x, mem, w_in, w_mem_kv, w_out, rel_bias, conv_w, ln_g, ln_b):
    b, s, _ = x.shape
    for layer in range(DEPTH):
        h = jnp.einsum('bsd,de->bse', x, w_in[layer])
        mix_in = h[..., :3 * E_MIX]
        q_mem = h[..., 3 * E_MIX:3 * E_MIX + E_MEM]
        z = h[..., 3 * E_MIX + E_MEM:]
        p0, p1, p2 = jnp.split(mix_in, 3, axis=-1)
        if layer % N_MIXERS == 0:
            q = p0.reshape(b, s, N_HEADS, HEAD_DIM)
            k = p1.reshape(b, s, N_HEADS, HEAD_DIM)
            v = p2.reshape(b, s, N_HEADS, HEAD_DIM)
            mix_out = chunked_attention(q, k, v, rel_bias[layer // N_MIXERS])
        else:
            mix_out = short_gated_conv(p0, p1, p2, conv_w[layer // N_MIXERS])
        kv_mem = jnp.einsum('bmd,de->bme', mem, w_mem_kv[layer])
        mem_out = memory_attention(q_mem, kv_mem)
        y = jnp.concatenate([mix_out, mem_out], axis=-1) * jax.nn.silu(z)
        out = jnp.einsum('bse,ed->bsd', y, w_out[layer])
        x = layer_norm(DN_ALPHA * x + out, ln_g[layer], ln_b[layer])
    return x
```

```python
import math
from contextlib import ExitStack

import numpy as np
import concourse.bass as bass
import concourse.mybir as mybir
from concourse.bass_utils import run_bass_kernel_spmd

F32 = mybir.dt.float32
BF16 = mybir.dt.bfloat16
AF = mybir.ActivationFunctionType
ALU = mybir.AluOpType

D = 1024
NIN = 5120
NT = 42
HALO = 10
NMAIN = 32
DEPTH = 4
ALPHA = (2.0 * DEPTH) ** 0.25
EPS = 1e-5
NXS = 4

ENG_ATTR = {'pe': 'tensor', 'act': 'scalar', 'dve': 'vector', 'pool': 'gpsimd', 'sp': 'sync'}


class Op:
    __slots__ = ('eng', 'fn', 'reads', 'writes', 'dma', 'waits', 'signal', 'ev', 'vc', 'sigcount')

    def __init__(self, eng, fn, reads, writes, dma):
        self.eng = eng
        self.fn = fn
        self.reads = reads
        self.writes = writes
        self.dma = dma
        self.waits = []
        self.signal = False
        self.ev = None
        self.vc = None
        self.sigcount = 0


class Sched:
    def __init__(self, nc, same_engine_sync=('act', 'dve', 'pool')):
        self.nc = nc
        self.ops = []
        self.same_sync = set(same_engine_sync)

    def op(self, eng, fn, reads=(), writes=(), dma=None):
        self.ops.append(Op(eng, fn, tuple(reads), tuple(writes), dma))

    def fence(self):
        shared = {}
        for e in ENG_ATTR:
            self.ops.append(Op(e, None, ('__fence__', shared), (), None))

    def finalize(self, stack):
        nc = self.nc
        state = {}
        vcs = {e: {} for e in ENG_ATTR}
        eng_count = {e: 0 for e in ENG_ATTR}
        dma_count = {}
        evop = {}
        dmavc = {}
        for op in self.ops:
            e = op.eng
            deps = {}
            if len(op.reads) == 2 and op.reads[0] == '__fence__':
                shared = op.reads[1]
                if not shared:
                    for e2, c2 in eng_count.items():
                        if c2 > 0:
                            shared[('e', e2)] = c2 - 1
                    for dk2, c2 in dma_count.items():
                        shared[dk2] = c2
                deps.update(shared)
                op.reads = ()
            for r in op.reads:
                st = state.get(r)
                if st:
                    for (k, i) in st[0]:
                        if deps.get(k, -1) < i:
                            deps[k] = i
            for w in op.writes:
                st = state.get(w)
                if st:
                    for (k, i) in st[0]:
                        if deps.get(k, -1) < i:
                            deps[k] = i
                    for (k, i) in st[1]:
                        if deps.get(k, -1) < i:
                            deps[k] = i
            vc = vcs[e]
            myk = ('e', e)
            for k, i in deps.items():
                if k == myk and e not in self.same_sync:
                    continue
                if vc.get(k, -1) >= i:
                    continue
                op.waits.append((k, i))
            for (k, i) in op.waits:
                if k[0] == 'e':
                    src = evop[(k, i)]
                    src.signal = True
                    svc = src.vc
                else:
                    svc = dmavc[(k, i)]
                for kk, ii in svc.items():
                    if vc.get(kk, -1) < ii:
                        vc[kk] = ii
                if vc.get(k, -1) < i:
                    vc[k] = i
            if op.fn is None:
                op.ev = None
                continue
            if op.dma is None:
                idx = eng_count[e]
                eng_count[e] = idx + 1
                op.ev = (myk, idx)
                evop[op.ev] = op
                snap = dict(vc)
                snap[myk] = idx
                op.vc = snap
                if e not in self.same_sync:
                    vc[myk] = idx
            else:
                dk = ('d', op.dma)
                c = dma_count.get(dk, 0) + 1
                dma_count[dk] = c
                op.ev = (dk, c)
                dmavc[op.ev] = dict(vc)
            for r in op.reads:
                st = state.setdefault(r, [[], []])
                rl = [x for x in st[1] if x[0] != op.ev[0]]
                rl.append(op.ev)
                st[1] = rl
            for w in op.writes:
                state[w] = [[op.ev], []]
        sems = {}
        for e in ENG_ATTR:
            sems[('e', e)] = stack.enter_context(nc.semaphore('sem_' + e))
        for dk in dma_count:
            sems[dk] = stack.enter_context(nc.semaphore('dsem_%d' % len(sems)))
        cnt = {e: 0 for e in ENG_ATTR}
        for op in self.ops:
            if op.dma is None and op.signal:
                cnt[op.eng] += 1
                op.sigcount = cnt[op.eng]
        for op in self.ops:
            engobj = getattr(nc, ENG_ATTR[op.eng])
            for (k, i) in op.waits:
                val = evop[(k, i)].sigcount if k[0] == 'e' else 16 * i
                engobj.wait_ge(sems[k], val)
            if op.fn is None:
                continue
            ins = op.fn()
            if op.dma is not None:
                ins.then_inc(sems[op.ev[0]], 16)
            elif op.signal:
                ins.then_inc(sems[('e', op.eng)], 1)
        self.stats = dict(nops=len(self.ops), sig=cnt, nsems=len(sems))


def layer_plan(l):
    if l == 0:
        blocks = [((0, 1), 'kv'), ((2, 3), 'kv')] + [((t, t + 1), 'full') for t in range(4, NT, 2)]
    elif l == 1:
        blocks = [((4,), 'cu'), ((5,), 'full')] + [((t, t + 1), 'full') for t in range(6, NT, 2)]
    elif l == 2:
        blocks = [((5, 6), 'kv'), ((7, 8), 'kv'), ((9,), 'full')] + \
                 [((t, t + 1), 'full') for t in range(10, NT, 2)]
    else:
        blocks = [((9,), 'cu')] + [((t, t + 1), 'full') for t in range(10, NT, 2)]
    return blocks


def build_program(layers, same_engine_sync=('act', 'dve', 'pool'), max_blocks=None):
    nc = bass.Bass("TRN2", target_bir_lowering=False, dynamic_dma_scratch_size=8192)
    last = layers[-1]
    final = (last == DEPTH - 1)
    dr = {}

    def din(name, shape):
        dr[name] = nc.dram_tensor(name, list(shape), F32, kind="ExternalInput").ap()

    din("xin", (NT * 128, D))
    din("valid", (128, NT))
    din("vrow", (128, NT, 2))
    din("mem", (256, D))
    din("ident", (128, 128))
    din("w_in", (len(layers), D, NIN))
    din("w_mem_kv", (len(layers), D, D))
    din("w_out", (len(layers), 1536, D))
    din("bt", (2, 128, 16, 256))
    din("chb", (2, 128, 16))
    din("convw", (2, 128, 8, 3))
    din("ln_g", (len(layers), D))
    din("ln_b", (len(layers), D))
    if final:
        xout = nc.dram_tensor("xout", [NMAIN * 128, D], F32, kind="ExternalOutput").ap()
    else:
        xout = nc.dram_tensor("xout", [NT * 128, D], F32, kind="ExternalOutput").ap()
    scratch = {}
    for l in layers[:-1]:
        scratch[l + 1] = nc.dram_tensor("xs%d" % (l + 1), [NT * 128, D], F32).ap()

    S = Sched(nc, same_engine_sync)
    with ExitStack() as st:
        def sb(name, shape, dt):
            return st.enter_context(nc.sbuf_tensor("sb_" + name, list(shape), dt))

        w_in_sb = sb("w_in_sb", (128, 8, NIN), BF16)
        w_out_sb = sb("w_out_sb", (128, 12, D), BF16)
        kmemT = sb("kmemT", (128, 4, 256), BF16)
        vmem = sb("vmem", (128, 2, 4, 129), BF16)
        valid_sb = sb("valid_sb", (128, NT), F32)
        vrow_sb = sb("vrow_sb", (128, NT, 2), F32)
        ident = sb("ident", (128, 128), F32)
        ident_bf = sb("ident_bf", (128, 128), BF16)
        lng = sb("lng", (128, D), F32)
        lnb = sb("lnb", (128, D), F32)
        xs = sb("xs", (128, NXS, D), F32)
        xT = sb("xT", (128, 8, 256), BF16)
        tb = sb("tb", (128, 1, D), F32)
        siluz = sb("siluz", (128, 2, 2, 1536), BF16)
        yT = sb("yT", (128, 12, 128), BF16)
        QmT = sb("QmT", (128, 2, 4, 256), BF16)
        PTm = sb("PTm", (128, 2, 256), BF16)
        small = sb("small", (128, 64), F32)
        stats = sb("stats", (128, 2, 2, 6), F32)
        mv = sb("mv", (128, 2, 4), F32)
        gtmp = sb("gtmp", (128, 2, 256), F32)
        chb_sb = sb("chb_sb", (128, 16), F32)
        import os as _os
        dbg_out = nc.dram_tensor("dbg", [128, 1536], BF16, kind="ExternalOutput").ap() if _os.environ.get("KDBG") else None
        dbg2 = nc.dram_tensor("dbg2", [128, 2048], BF16, kind="ExternalOutput").ap() if _os.environ.get("KDBG2") else None
        cw_sb = sb("cw_sb", (128, 8, 3), F32)
        MIXB = 51968
        MIX = sb("MIX", (128, MIXB // 2), BF16)

        def mview(off, shape, dt):
            esz = 4 if dt == F32 else 2
            n = 1
            for d_ in shape:
                n *= d_
            v = MIX[:, off // 2: off // 2 + (n * esz) // 2]
            if dt == F32:
                v = v.bitcast(F32)
            if len(shape) == 1:
                return v
            names = "abcd"[:len(shape)]
            pat = "p (%s) -> p %s" % (" ".join(names), " ".join(names))
            kw = {names[i]: shape[i] for i in range(len(shape) - 1)}
            return v.rearrange(pat, **kw)

        QT = mview(0, (2, 8, 256), BF16)
        KTr = mview(8192, (8, 1024), BF16)
        Vr = mview(24576, (8, 16, 65), BF16)
        PT = mview(41216, (2, 640), BF16)
        Ep = mview(43776, (16, 256), BF16)
        cu = mview(0, (2, 8, 258), F32)
        p0s = mview(16512, (2, 8, 256), F32)
        p1s = mview(32896, (2, 256), F32)
        cacc = mview(34944, (2, 256), F32)
        szT = mview(36992, (2, 8, 256), BF16)
        memx = mview(0, (2, D), F32)
        memT = mview(8192, (8, 256), BF16)
        wkv = mview(12288, (8, D), BF16)
        btst = tb[:, :, :].rearrange("p a (h c) -> p (a h) c", c=256)

        PS = st.enter_context(nc.psum_tensor("PS", [128, 8 * 512], F32))
        print("SBUF bytes remaining per partition:", nc.sbuf_bytes_remaining)

        def bank(b, n=512, off=0):
            return PS[:, b * 512 + off: b * 512 + off + n]

        T, A, V, G, SP = nc.tensor, nc.scalar, nc.vector, nc.gpsimd, nc.sync
        BK = lambda b: ('bank', b)

        rot = {'i': 0}

        def gen_bank():
            b = 4 + (rot['i'] % 4)
            rot['i'] += 1
            return b

        S.op('sp', lambda: SP.dma_start(out=ident[:, :], in_=dr["ident"]), writes=['ident'], dma='ident')
        S.op('sp', lambda: SP.dma_start(out=valid_sb[:, :], in_=dr["valid"]), writes=['valid'], dma='valid')
        S.op('sp', lambda: SP.dma_start(out=vrow_sb[:, :, :], in_=dr["vrow"]), writes=['vrow'], dma='vrow')
        S.op('dve', lambda: V.tensor_copy(out=ident_bf[:, :], in_=ident[:, :]), reads=['ident'], writes=['ident_bf'])
        S.op('pool', lambda: G.memset(small[:, 16:17], EPS), writes=['epsc'])

        WCH = ((0, 2048), (2048, 4096), (4096, 5120))

        def wkey(col):
            return ('win', 0 if col < 2048 else (1 if col < 4096 else 2))

        def load_weights(l):
            l = layers.index(l)
            for c, (c0, c1) in enumerate(WCH):
                S.op('pool', lambda l=l, c0=c0, c1=c1: G.dma_start(
                    out=w_in_sb[:, :, c0:c1],
                    in_=dr["w_in"][l, :, c0:c1].rearrange("(k p) n -> p k n", p=128)),
                    writes=[('win', c)], dma=('win', c))

        def load_w_out(l):
            l = layers.index(l)
            S.op('pool', lambda l=l: G.dma_start(
                out=w_out_sb[:, :, :], in_=dr["w_out"][l, :, :].rearrange("(k p) n -> p k n", p=128)),
                writes=[('wout', 0), ('wout', 1)], dma=('wout', 0))

        import os as _os2
        _PROBE2 = _os2.environ.get('KPROBE2', '')

        def kv_prologue(l):
            l = layers.index(l)
            S.op('sp', lambda: SP.dma_start(out=memx, in_=dr["mem"].rearrange("(j p) d -> p j d", p=128)),
                 writes=['memx'], dma='memx')
            S.op('pool', lambda l=l: G.dma_start(
                out=wkv[:, :, :], in_=dr["w_mem_kv"][l, :, :].rearrange("(k p) n -> p k n", p=128)),
                writes=[('wkv', 0), ('wkv', 1)], dma=('wkv', 0))
            if _PROBE2 == 'c1':
                return
            for j in range(2):
                for half in range(2):
                    b = gen_bank()
                    for q in range(4):
                        kc = half * 4 + q
                        S.op('pe', lambda b=b, q=q, kc=kc, j=j: T.transpose(
                            out=bank(b, 128, q * 128), in_=memx[:, j, kc * 128:(kc + 1) * 128], identity=ident[:, :]),
                            reads=['memx', 'ident'], writes=[BK(b)])
                    S.op('act', lambda b=b, half=half, j=j: A.copy(
                        out=memT[:, half * 4:half * 4 + 4, j * 128:(j + 1) * 128],
                        in_=bank(b).rearrange("p (c t) -> p c t", c=4)),
                        writes=[BK(b), ('memT', j, half)])
            if _PROBE2 == 'c2':
                return
            memT_keys = [('memT', j, half) for j in range(2) for half in range(2)]
            for h in range(4):
                b = gen_bank()
                for kc in range(8):
                    S.op('pe', lambda b=b, h=h, kc=kc: T.matmul(
                        bank(b, 256), lhsT=wkv[:, kc, h * 128:(h + 1) * 128], rhs=memT[:, kc, :],
                        start=(kc == 0), stop=(kc == 7)),
                        reads=[('wkv', 0)] + memT_keys, writes=[BK(b)])
                S.op('act', lambda b=b, h=h: A.copy(out=kmemT[:, h, :], in_=bank(b, 256)),
                     writes=[BK(b), 'kmemT'])
            if _PROBE2 == 'c3':
                return
            for mb in range(2):
                b = gen_bank()
                for kc in range(8):
                    S.op('pe', lambda b=b, mb=mb, kc=kc: T.matmul(
                        bank(b), lhsT=memT[:, kc, mb * 128:(mb + 1) * 128], rhs=wkv[:, kc, 512:1024],
                        start=(kc == 0), stop=(kc == 7)),
                        reads=[('wkv', 1)] + memT_keys, writes=[BK(b)])
                S.op('dve', lambda b=b, mb=mb: V.tensor_copy(
                    out=vmem[:, mb, :, 0:128], in_=bank(b).rearrange("p (h d) -> p h d", h=4)),
                    writes=[BK(b), 'vmem'])
                S.op('pool', lambda mb=mb: G.memset(vmem[:, mb, :, 128:129], 1.0), writes=['vmem'])

        import os as _os
        PROBE = _os.environ.get("KPROBE", "")
        if PROBE != "a":
            load_weights(layers[0])
            load_w_out(layers[0])

        for li, l in enumerate(layers if PROBE not in ("a", "b") else []):
            attn = (l % 2 == 0)
            la = l // 2
            src = dr["xin"] if li == 0 else scratch[l]
            dst = xout if l == last else scratch[l + 1]

            def dst_rows(t, l=l):
                if l == DEPTH - 1:
                    return (t - HALO) * 128
                return t * 128

            kv_prologue(l)
            S.fence()
            if PROBE == "c":
                break

            S.op('sp', lambda l=li: SP.dma_start(out=lng[:, :], in_=dr["ln_g"][l:l + 1, :].to_broadcast([128, D])),
                 writes=['lng'], dma='lng')
            S.op('sp', lambda l=li: SP.dma_start(out=lnb[:, :], in_=dr["ln_b"][l:l + 1, :].to_broadcast([128, D])),
                 writes=['lnb'], dma='lnb')
            if attn:
                S.op('sp', lambda la=la: SP.dma_start(out=chb_sb[:, :], in_=dr["chb"][la]), writes=['chb'], dma='chb')
                S.op('dve', lambda: V.tensor_scalar(out=small[:, 32:48], in0=chb_sb[:, :], scalar1=-1.0,
                                                    scalar2=None, op0=ALU.mult),
                     reads=['chb'], writes=['negchb'])
                for g in range(4):
                    S.op('sp', lambda la=la, g=g: SP.dma_start(out=btst, in_=dr["bt"][la, :, 4 * g:4 * g + 4, :]),
                         writes=[('tb', 0)], dma='btst')
                    for hh in range(4):
                        h = 4 * g + hh
                        S.op('act', lambda h=h, hh=hh: A.activation(
                            out=Ep[:, h, :], in_=btst[:, hh, :], func=AF.Exp, bias=small[:, 32 + h:33 + h], scale=1.0),
                            reads=[('tb', 0), 'negchb'], writes=['Ep'])
                S.op('pool', lambda: G.memset(Ep[64:128, :, 128:192], 0.0), writes=['Ep'])
            else:
                S.op('sp', lambda la=la: SP.dma_start(out=cw_sb[:, :, :], in_=dr["convw"][la]), writes=['cw'], dma='cw')

            blocks = layer_plan(l)
            if max_blocks is not None:
                blocks = blocks[:max_blocks]
            nb = len(blocks)
            st_l = {'cu_prev': None}

            def phase1(bi, l=l, attn=attn, blocks=blocks, src=src, st_l=st_l):
                tiles, mode = blocks[bi]
                n = len(tiles)
                N = 128 * n
                pb = bi % 2
                for ti, t in enumerate(tiles):
                    slot = t % NXS
                    S.op('sp', lambda slot=slot, t=t: SP.dma_start(out=xs[:, slot, :], in_=src[t * 128:(t + 1) * 128, :]),
                         reads=[('xd', l, t)], writes=[('xs', slot)], dma=('xs', slot))
                    for half in range(2):
                        b = gen_bank()
                        for q in range(4):
                            kc = half * 4 + q
                            S.op('pe', lambda b=b, q=q, kc=kc, slot=slot: T.transpose(
                                out=bank(b, 128, q * 128), in_=xs[:, slot, kc * 128:(kc + 1) * 128], identity=ident[:, :]),
                                reads=[('xs', slot), 'ident'], writes=[BK(b)])
                        if half == 0:
                            S.op('act', lambda b=b, half=half, ti=ti: A.copy(
                                out=xT[:, half * 4:half * 4 + 4, ti * 128:(ti + 1) * 128],
                                in_=bank(b).rearrange("p (c t) -> p c t", c=4)),
                                writes=[BK(b), ('xT', ti, half)])
                        else:
                            S.op('dve', lambda b=b, half=half, ti=ti: V.tensor_copy(
                                out=xT[:, half * 4:half * 4 + 4, ti * 128:(ti + 1) * 128],
                                in_=bank(b).rearrange("p (c t) -> p c t", c=4)),
                                writes=[BK(b), ('xT', ti, half)])
                xTk = [('xT', ti, half) for ti in range(n) for half in range(2)]

                def tokmajor(ti, col0, evac):
                    b = gen_bank()
                    c = col0 // 512
                    for kc in range(8):
                        S.op('pe', lambda b=b, kc=kc, ti=ti, col0=col0: T.matmul(
                            bank(b), lhsT=xT[:, kc, ti * 128:(ti + 1) * 128], rhs=w_in_sb[:, kc, col0:col0 + 512],
                            start=(kc == 0), stop=(kc == 7)),
                            reads=[('xT', ti, 0), ('xT', ti, 1), wkey(col0)], writes=[BK(b)])
                    evac(b)

                def featmajor(col0, evac):
                    b = gen_bank()
                    c = col0 // 512
                    for kc in range(8):
                        S.op('pe', lambda b=b, kc=kc, col0=col0: T.matmul(
                            bank(b, N), lhsT=w_in_sb[:, kc, col0:col0 + 128], rhs=xT[:, kc, 0:N],
                            start=(kc == 0), stop=(kc == 7)),
                            reads=xTk + [wkey(col0)], writes=[BK(b)])
                    evac(b)

                if attn:
                    for ti, t in enumerate(tiles):
                        vs = t % 8
                        for hb in range(2):
                            def ev(b, t=t, vs=vs, hb=hb):
                                S.op('dve', lambda: V.tensor_scalar(
                                    out=Vr[:, vs, hb * 8:(hb + 1) * 8, 0:64],
                                    in0=bank(b).rearrange("p (h d) -> p h d", h=8),
                                    scalar1=valid_sb[:, t:t + 1], scalar2=None, op0=ALU.mult),
                                    reads=['valid'], writes=[BK(b), ('V', vs)])
                            tokmajor(ti, 2048 + hb * 512, ev)
                        S.op('pool', lambda vs=vs, t=t: G.tensor_copy(
                            out=Vr[:, vs, :, 64:65], in_=valid_sb[:, t:t + 1].unsqueeze(1).to_broadcast([128, 16, 1])),
                            reads=['valid'], writes=[('V', vs)])
                    for j in range(8):
                        def ev(b, j=j):
                            for ti, t in enumerate(tiles):
                                ks = t % 8
                                if (j + ti) % 2 == 0:
                                    S.op('act', lambda ti=ti, ks=ks: A.copy(
                                        out=KTr[:, j, ks * 128:(ks + 1) * 128], in_=bank(b, 128, ti * 128)),
                                        writes=[BK(b), ('K', ks)])
                                else:
                                    S.op('dve', lambda ti=ti, ks=ks: V.tensor_copy(
                                        out=KTr[:, j, ks * 128:(ks + 1) * 128], in_=bank(b, 128, ti * 128)),
                                        writes=[BK(b), ('K', ks)])
                        featmajor(1024 + j * 128, ev)
                    if mode == 'full':
                        for j in range(8):
                            def ev(b, j=j):
                                S.op('act', lambda: A.mul(out=QT[:, pb, j, 0:N], in_=bank(b, N), mul=0.125),
                                     writes=[BK(b), ('QT', pb)])
                            featmajor(j * 128, ev)
                else:
                    cb = bi % 2
                    prev = st_l['cu_prev']
                    if prev is not None:
                        pcb, pN, pt = prev
                        S.op('pool', lambda cb=cb, pcb=pcb, pN=pN, pt=pt: G.tensor_tensor(
                            out=cu[:, cb, :, 0:2], in0=cu[:, pcb, :, pN:pN + 2],
                            in1=vrow_sb[:, pt:pt + 1, :].to_broadcast([128, 8, 2]), op=ALU.mult),
                            reads=[('cu', pcb), 'vrow'], writes=[('cu', cb)])
                    else:
                        S.op('pool', lambda cb=cb: G.memset(cu[:, cb, :, 0:2], 0.0), writes=[('cu', cb)])
                    for j in range(8):
                        pj = j % 2

                        def ev1(b, pj=pj):
                            S.op('act', lambda: A.copy(out=p1s[:, pj, 0:N], in_=bank(b, N)),
                                 writes=[BK(b), ('p1s', pj)])
                        featmajor(1024 + j * 128, ev1)

                        def ev2(b, j=j, pj=pj, cb=cb):
                            S.op('dve', lambda: V.tensor_tensor(out=cu[:, cb, j, 2:2 + N], in0=bank(b, N),
                                                                in1=p1s[:, pj, 0:N], op=ALU.mult),
                                 reads=[('p1s', pj)], writes=[BK(b), ('cu', cb)])
                        featmajor(2048 + j * 128, ev2)
                    st_l['cu_prev'] = (cb, N, tiles[-1])
                    if mode == 'full':
                        for j in range(8):
                            def ev(b, j=j):
                                S.op('act', lambda: A.copy(out=p0s[:, pb, j, 0:N], in_=bank(b, N)),
                                     writes=[BK(b), ('p0s', pb, j)])
                            featmajor(j * 128, ev)

                            def evz(b, j=j):
                                S.op('act', lambda: A.activation(out=szT[:, pb, j, 0:N], in_=bank(b, N), func=AF.Silu),
                                     writes=[BK(b), ('szT', pb, j)])
                            featmajor(3584 + j * 128, evz)
                if mode == 'full':
                    for j in range(4):
                        def ev(b, j=j):
                            S.op('act', lambda: A.mul(out=QmT[:, pb, j, 0:N], in_=bank(b, N), mul=1.0 / math.sqrt(128.0)),
                                 writes=[BK(b), ('QmT', pb)])
                        featmajor(3072 + j * 128, ev)
                    zblocks = (0, 1, 2) if attn else (2,)
                    for ti, t in enumerate(tiles):
                        for zb in zblocks:
                            def ev(b, ti=ti, zb=zb):
                                S.op('act', lambda: A.activation(out=siluz[:, pb, ti, zb * 512:(zb + 1) * 512],
                                                                 in_=bank(b), func=AF.Silu),
                                     writes=[BK(b), ('siluz', pb, ti)])
                            tokmajor(ti, 3584 + zb * 512, ev)

            def phase2(bi, l=l, attn=attn, blocks=blocks, dst=dst):
                tiles, mode = blocks[bi]
                if mode != 'full':
                    return
                _d2 = _os.environ.get("KDBG2")
                if _d2 and int(_os.environ["KDBG"]) == tiles[0]:
                    if _d2 == 'xT':
                        S.op('sp', lambda: SP.dma_start(out=dbg2, in_=xT[:, :, :].rearrange("p a b -> p (a b)")),
                             reads=[('xT', 0, 0), ('xT', 0, 1), ('xT', 1, 0), ('xT', 1, 1)], writes=[('xd', l + 1, 1)], dma='gdbg2')
                    elif _d2 == 'w':
                        S.op('sp', lambda: SP.dma_start(out=dbg2, in_=w_in_sb[:, 0, 3072:5120]),
                             reads=[('win', 1), ('win', 2)], writes=[('xd', l + 1, 1)], dma='gdbg2')
                    elif _d2 == 'QT':
                        S.op('sp', lambda: SP.dma_start(out=dbg2.rearrange("p (a b) -> p a b", a=8), in_=QT[:, bi % 2, :, :]),
                             reads=[('QT', bi % 2)], writes=[('xd', l + 1, 1)], dma='gdbg2')
                    elif _d2 == 'K':
                        S.op('sp', lambda: SP.dma_start(out=dbg2[:, 0:1024].rearrange("p (a b) -> p a b", a=8),
                                                        in_=KTr[:, :, (tiles[0] % 8) * 128:(tiles[0] % 8) * 128 + 128]),
                             reads=[('K', tiles[0] % 8)], writes=[('xd', l + 1, 1)], dma='gdbg2')
                    elif _d2 == 'QmT':
                        S.op('sp', lambda: SP.dma_start(out=dbg2[:, 0:1024].rearrange("p (a b) -> p a b", a=4), in_=QmT[:, bi % 2, :, :]),
                             reads=[('QmT', bi % 2)], writes=[('xd', l + 1, 1)], dma='gdbg2')
                    elif _d2 == 'V':
                        S.op('sp', lambda: SP.dma_start(out=dbg2[:, 0:1040].rearrange("p (a b) -> p a b", a=16), in_=Vr[:, tiles[0] % 8, :, :]),
                             reads=[('V', tiles[0] % 8)], writes=[('xd', l + 1, 1)], dma='gdbg2')
                    elif _d2 == 'sz':
                        S.op('sp', lambda: SP.dma_start(out=dbg2[:, 0:1536], in_=siluz[:, bi % 2, 0, :]),
                             reads=[('siluz', bi % 2, 0)], writes=[('xd', l + 1, 1)], dma='gdbg2')
                n = len(tiles)
                N = 128 * n
                pb = bi % 2
                if not attn:
                    cb = bi % 2
                    for j in range(8):
                        aj = j % 2
                        S.op('dve', lambda j=j, aj=aj: V.tensor_scalar(
                            out=cacc[:, aj, 0:N], in0=cu[:, cb, j, 0:N], scalar1=cw_sb[:, j, 0:1], scalar2=None,
                            op0=ALU.mult), reads=[('cu', cb), 'cw'], writes=[('cacc', aj)])
                        for k in (1, 2):
                            S.op('dve', lambda j=j, aj=aj, k=k: V.scalar_tensor_tensor(
                                out=cacc[:, aj, 0:N], in0=cu[:, cb, j, k:k + N], scalar=cw_sb[:, j, k:k + 1],
                                in1=cacc[:, aj, 0:N], op0=ALU.mult, op1=ALU.add),
                                reads=[('cu', cb), 'cw'], writes=[('cacc', aj)])
                        S.op('pool', lambda j=j, aj=aj: G.tensor_tensor(
                            out=cacc[:, aj, 0:N], in0=cacc[:, aj, 0:N], in1=p0s[:, pb, j, 0:N], op=ALU.mult),
                            reads=[('p0s', pb, j)], writes=[('cacc', aj)])
                        S.op('pool', lambda j=j, aj=aj: G.tensor_tensor(
                            out=szT[:, pb, j, 0:N], in0=cacc[:, aj, 0:N], in1=szT[:, pb, j, 0:N], op=ALU.mult),
                            reads=[('cacc', aj)], writes=[('szT', pb, j)])
                def tile_body(ti, t):
                    xslot = t % NXS
                    if attn:
                        def scores(h):
                            sbuf = h % 2
                            hp = h % 2
                            ch = h // 2
                            for j in range(5):
                                ks = (t - 4 + j) % 8
                                S.op('pe', lambda sbuf=sbuf, j=j, ks=ks, hp=hp, ch=ch: T.matmul(
                                    PS[:, sbuf * 1024 + j * 128: sbuf * 1024 + (j + 1) * 128],
                                    lhsT=KTr[hp * 64:(hp + 1) * 64, ch, ks * 128:(ks + 1) * 128],
                                    rhs=QT[hp * 64:(hp + 1) * 64, pb, ch, ti * 128:(ti + 1) * 128],
                                    start=True, stop=True),
                                    reads=[('K', ks), ('QT', pb)], writes=[BK(2 * sbuf), BK(2 * sbuf + 1)])
                            pbuf = h % 2
                            S.op('act', lambda sbuf=sbuf, pbuf=pbuf: A.activation(
                                out=PT[:, pbuf, :], in_=PS[:, sbuf * 1024: sbuf * 1024 + 640], func=AF.Exp),
                                writes=[BK(2 * sbuf), BK(2 * sbuf + 1), ('PT', pbuf)])
                            S.op('dve', lambda pbuf=pbuf, h=h: V.tensor_tensor(
                                out=PT[:, pbuf, 384:640], in0=PT[:, pbuf, 384:640], in1=Ep[:, h, :], op=ALU.mult),
                                reads=['Ep'], writes=[('PT', pbuf)])
                            S.op('pool', lambda pbuf=pbuf: G.memset(PT[0:64, pbuf, 64:128], 0.0),
                                 writes=[('PT', pbuf)])

                        pvb = {'b': None}

                        def pv(h):
                            hg = h % 4
                            if hg == 0:
                                pvb['b'] = gen_bank()
                            b = pvb['b']
                            pbuf = h % 2
                            for j in range(5):
                                vs = (t - 4 + j) % 8
                                S.op('pe', lambda b=b, hg=hg, j=j, vs=vs, pbuf=pbuf, h=h: T.matmul(
                                    bank(b, 65, hg * 65), lhsT=PT[:, pbuf, j * 128:(j + 1) * 128],
                                    rhs=Vr[:, vs, h, :], start=(j == 0), stop=(j == 4)),
                                    reads=[('PT', pbuf), ('V', vs)], writes=[BK(b)])
                            if hg == 3:
                                h0 = h - 3
                                gb = (h // 4) % 2
                                pv4 = bank(b, 260).rearrange("p (h d) -> p h d", h=4)
                                S.op('dve', lambda pv4=pv4: V.tensor_scalar(
                                    out=small[:, 0:4].unsqueeze(2), in0=pv4[:, :, 64:65], scalar1=1e-30, scalar2=None,
                                    op0=ALU.add), writes=[BK(b), 'rs'])
                                S.op('dve', lambda: V.reciprocal(out=small[:, 0:4], in_=small[:, 0:4]), writes=['rs'])
                                S.op('dve', lambda pv4=pv4, gb=gb: V.tensor_tensor(
                                    out=gtmp[:, gb, :].rearrange("p (h d) -> p h d", h=4), in0=pv4[:, :, 0:64],
                                    in1=small[:, 0:4].unsqueeze(2).to_broadcast([128, 4, 64]), op=ALU.mult),
                                    reads=['rs'], writes=[BK(b), ('gtmp', gb)])
                                S.op('pool', lambda gb=gb, h0=h0: G.tensor_tensor(
                                    out=siluz[:, pb, ti, h0 * 64:(h0 + 4) * 64], in0=gtmp[:, gb, :],
                                    in1=siluz[:, pb, ti, h0 * 64:(h0 + 4) * 64], op=ALU.mult),
                                    reads=[('gtmp', gb)], writes=[('siluz', pb, ti)])

                        scores(0)
                        for h in range(16):
                            if h + 1 < 16:
                                scores(h + 1)
                            pv(h)
                    for hm in range(4):
                        b = gen_bank()
                        for mb in range(2):
                            S.op('pe', lambda b=b, mb=mb, hm=hm: T.matmul(
                                bank(b, 128, mb * 128), lhsT=kmemT[:, hm, mb * 128:(mb + 1) * 128],
                                rhs=QmT[:, pb, hm, ti * 128:(ti + 1) * 128], start=True, stop=True),
                                reads=['kmemT', ('QmT', pb)], writes=[BK(b)])
                        mbuf = hm % 2
                        S.op('act', lambda b=b, mbuf=mbuf: A.activation(out=PTm[:, mbuf, :], in_=bank(b, 256), func=AF.Exp),
                             writes=[BK(b), ('PTm', mbuf)])
                        for mb in range(2):
                            S.op('pe', lambda b=b, mb=mb, hm=hm, mbuf=mbuf: T.matmul(
                                bank(b, 129, 256), lhsT=PTm[:, mbuf, mb * 128:(mb + 1) * 128],
                                rhs=vmem[:, mb, hm, :], start=(mb == 0), stop=(mb == 1)),
                                reads=[('PTm', mbuf), 'vmem'], writes=[BK(b)])
                        S.op('dve', lambda b=b: V.tensor_scalar(
                            out=small[:, 8:9], in0=bank(b, 1, 256 + 128), scalar1=1e-30, scalar2=None, op0=ALU.add),
                            writes=[BK(b), 'rsm'])
                        S.op('dve', lambda: V.reciprocal(out=small[:, 8:9], in_=small[:, 8:9]), writes=['rsm'])
                        S.op('dve', lambda b=b, hm=hm: V.scalar_tensor_tensor(
                            out=siluz[:, pb, ti, 1024 + hm * 128:1024 + (hm + 1) * 128], in0=bank(b, 128, 256),
                            scalar=small[:, 8:9], in1=siluz[:, pb, ti, 1024 + hm * 128:1024 + (hm + 1) * 128],
                            op0=ALU.mult, op1=ALU.mult),
                            reads=['rsm'], writes=[BK(b), ('siluz', pb, ti)])
                    if _os.environ.get("KDBG") and int(_os.environ["KDBG"]) == t:
                        S.op('sp', lambda: SP.dma_start(out=dbg_out, in_=siluz[:, pb, ti, :]),
                             reads=[('siluz', pb, ti)], writes=[('xd', l + 1, 0)], dma='gdbg')
                    if _os.environ.get("KDBG2") == 'PT' and int(_os.environ["KDBG"]) == t:
                        S.op('sp', lambda: SP.dma_start(out=dbg2[:, 0:1280].rearrange("p (a b) -> p a b", a=2), in_=PT[:, :, :]),
                             reads=[('PT', 0), ('PT', 1)], writes=[('xd', l + 1, 1)], dma='gdbg2')
                        S.op('sp', lambda: SP.dma_start(out=dbg2[:, 1280:1792].rearrange("p (a b) -> p a b", a=2), in_=PTm[:, :, :]),
                             reads=[('PTm', 0), ('PTm', 1)], writes=[('xd', l + 1, 2)], dma='gdbg3')
                    chunks = list(range(12)) if attn else list(range(8, 12))
                    for g0 in range(0, len(chunks), 4):
                        grp = chunks[g0:g0 + 4]
                        b = gen_bank()
                        psb = bank(b).bitcast(BF16)
                        for q, j in enumerate(grp):
                            S.op('pe', lambda q=q, j=j, psb=psb: T.transpose(
                                out=psb[:, q * 128:(q + 1) * 128], in_=siluz[:, pb, ti, j * 128:(j + 1) * 128],
                                identity=ident_bf[:, :]),
                                reads=[('siluz', pb, ti), 'ident_bf'], writes=[BK(b)])
                        j0 = grp[0]
                        S.op('dve', lambda psb=psb, j0=j0: V.tensor_copy(
                            out=yT[:, j0:j0 + 4, :], in_=psb[:, 0:512].rearrange("p (c t) -> p c t", c=4)),
                            writes=[BK(b), ('yT', j0 // 4)])
                    tslot = 0
                    for half in range(2):
                        b = gen_bank()
                        for ec in range(12):
                            if attn or ec >= 8:
                                lh = yT[:, ec, :]
                                rk = [('yT', ec // 4)]
                            else:
                                lh = szT[:, pb, ec, ti * 128:(ti + 1) * 128]
                                rk = [('szT', pb, ec)]
                            S.op('pe', lambda b=b, ec=ec, half=half, lh=lh: T.matmul(
                                bank(b), lhsT=lh, rhs=w_out_sb[:, ec, half * 512:(half + 1) * 512],
                                start=(ec == 0), stop=(ec == 11)),
                                reads=rk + [('wout', half)], writes=[BK(b)])
                        S.op('dve', lambda b=b, half=half, xslot=xslot, tslot=tslot: V.scalar_tensor_tensor(
                            out=tb[:, tslot, half * 512:(half + 1) * 512], in0=xs[:, xslot, half * 512:(half + 1) * 512],
                            scalar=ALPHA, in1=bank(b), op0=ALU.mult, op1=ALU.add),
                            reads=[('xs', xslot)], writes=[BK(b), ('tb', tslot)])
                        S.op('dve', lambda half=half, tslot=tslot: V.bn_stats(
                            out=stats[:, tslot, half, :], in_=tb[:, tslot, half * 512:(half + 1) * 512]),
                            reads=[('tb', tslot)], writes=[('stats', tslot)])
                    S.op('dve', lambda tslot=tslot: V.bn_aggr(
                        out=mv[:, tslot, 0:2], in_=stats[:, tslot, :, :].rearrange("p a b -> p (a b)")),
                        reads=[('stats', tslot)], writes=[('mv', tslot)])
                    S.op('act', lambda tslot=tslot: A.activation(
                        out=mv[:, tslot, 3:4], in_=mv[:, tslot, 1:2], func=AF.Ln, bias=small[:, 16:17], scale=1.0),
                        reads=[('mv', tslot), 'epsc'], writes=[('mvb', tslot)])
                    S.op('act', lambda tslot=tslot: A.activation(
                        out=mv[:, tslot, 2:3], in_=mv[:, tslot, 3:4], func=AF.Exp, scale=-0.5),
                        reads=[('mvb', tslot)], writes=[('mvc', tslot)])
                    S.op('pool', lambda tslot=tslot: G.tensor_scalar(
                        out=tb[:, tslot, :], in0=tb[:, tslot, :], scalar1=mv[:, tslot, 0:1], scalar2=mv[:, tslot, 2:3],
                        op0=ALU.subtract, op1=ALU.mult), reads=[('mv', tslot), ('mvc', tslot)], writes=[('tb', tslot)])
                    S.op('pool', lambda tslot=tslot: G.tensor_tensor(
                        out=tb[:, tslot, :], in0=tb[:, tslot, :], in1=lng[:, :], op=ALU.mult),
                        reads=['lng'], writes=[('tb', tslot)])
                    S.op('pool', lambda tslot=tslot: G.tensor_tensor(
                        out=tb[:, tslot, :], in0=tb[:, tslot, :], in1=lnb[:, :], op=ALU.add),
                        reads=['lnb'], writes=[('tb', tslot)])
                    if l == DEPTH - 1 and t < HALO:
                        return
                    r0 = dst_rows(t)
                    S.op('sp', lambda tslot=tslot, r0=r0: SP.dma_start(out=dst[r0:r0 + 128, :], in_=tb[:, tslot, :]),
                         reads=[('tb', tslot)], writes=[('xd', l + 1, t)], dma=('tbo', tslot))

                for ti_, t_ in enumerate(tiles):
                    tile_body(ti_, t_)

            if nb > 0:
                phase1(0)
            for bi in range(nb):
                if bi + 1 < nb:
                    phase1(bi + 1)
                elif li + 1 < len(layers):
                    load_weights(layers[li + 1])
                phase2(bi)
            if li + 1 < len(layers):
                load_w_out(layers[li + 1])
            S.fence()

        outkeys = [('xd', last + 1, t) for t in range(NT)]
        S.op('sp', None, reads=outkeys)
        S.finalize(st)
    return nc, S


def _host_layout(inputs):
    x = np.asarray(inputs["x"], dtype=np.float32)
    mem = np.asarray(inputs["mem"], dtype=np.float32)
    rel_bias = np.asarray(inputs["rel_bias"], dtype=np.float32)
    conv_w = np.asarray(inputs["conv_w"], dtype=np.float32)
    k = np.arange(128)[:, None]
    q = np.arange(128)[None, :]
    idx3 = np.minimum(q - k + 128, 128) + 128
    idx4 = (q - k) + 128
    idx = np.concatenate([idx3, idx4], axis=1)
    bt = np.ascontiguousarray(np.transpose(rel_bias[:, :, idx], (0, 2, 1, 3)))
    chb = np.ascontiguousarray(np.broadcast_to(rel_bias[:, None, :, 256], (2, 128, 16)))
    convw = np.ascontiguousarray(np.transpose(conv_w.reshape(2, 3, 8, 128), (0, 3, 2, 1)))
    common = dict(
        ident=np.eye(128, dtype=np.float32),
        w_in=np.asarray(inputs["w_in"], dtype=np.float32),
        w_mem_kv=np.asarray(inputs["w_mem_kv"], dtype=np.float32),
        w_out=np.asarray(inputs["w_out"], dtype=np.float32),
        bt=bt, chb=chb, convw=convw,
        ln_g=np.asarray(inputs["ln_g"], dtype=np.float32),
        ln_b=np.asarray(inputs["ln_b"], dtype=np.float32),
    )
    per_core = []
    for c in range(8):
        b, half = c // 2, c % 2
        s0 = half * 4096
        w0 = s0 - HALO * 128
        xw = np.zeros((NT * 128, D), np.float32)
        lo = max(w0, 0)
        xw[lo - w0:] = x[b, lo:s0 + 4096]
        vt = np.zeros((NT * 128,), np.float32)
        vt[lo - w0:] = 1.0
        valid = np.ascontiguousarray(vt.reshape(NT, 128).T)
        vrow = np.ascontiguousarray(np.broadcast_to(vt.reshape(NT, 128)[None, :, 126:128], (128, NT, 2)))
        m = dict(common)
        m.update(xin=xw, valid=valid, vrow=vrow, mem=np.ascontiguousarray(mem[b]))
        per_core.append(m)
    return per_core


_CACHE = {}


def _get_program(layers):
    key = tuple(layers)
    if key not in _CACHE:
        _CACHE[key] = build_program(list(layers))[0]
    return _CACHE[key]


FUSED = True


def kernel(**inputs):
    per_core = _host_layout(inputs)
    groups = [[0, 1, 2, 3]] if FUSED else [[0], [1], [2], [3]]
    full = {k: per_core[0][k] for k in ("w_in", "w_mem_kv", "w_out", "ln_g", "ln_b")}
    for grp in groups:
        nc = _get_program(grp)
        sl = {k: np.ascontiguousarray(v[grp[0]:grp[-1] + 1]) for k, v in full.items()}
        maps = []
        for c in range(8):
            m = dict(per_core[c])
            m.update(sl)
            maps.append(m)
        res = run_bass_kernel_spmd(nc, maps, core_ids=list(range(8)))
        outs = [r["xout"] for r in res.results]
        if grp[-1] != DEPTH - 1:
            for c in range(8):
                per_core[c]["xin"] = outs[c]
    out = np.zeros((4, 8192, D), np.float32)
    for c in range(8):
        b, half = c // 2, c % 2
        out[b, half * 4096:(half + 1) * 4096] = outs[c]
    return out
```

```python
import math
from contextlib import ExitStack

import numpy as np
import concourse.bass as bass
import concourse.mybir as mybir
from concourse.bass_utils import run_bass_kernel_spmd

F32 = mybir.dt.float32
BF16 = mybir.dt.bfloat16
AF = mybir.ActivationFunctionType
ALU = mybir.AluOpType

D = 1024
NIN = 5120
NT = 42
HALO = 10
NMAIN = 32
DEPTH = 4
ALPHA = (2.0 * DEPTH) ** 0.25
EPS = 1e-5
NXS = 4

ENG_ATTR = {'pe': 'tensor', 'act': 'scalar', 'dve': 'vector', 'pool': 'gpsimd', 'sp': 'sync'}


class Op:
    __slots__ = ('eng', 'fn', 'reads', 'writes', 'dma', 'waits', 'signal', 'ev', 'vc', 'sigcount')

    def __init__(self, eng, fn, reads, writes, dma):
        self.eng = eng
        self.fn = fn
        self.reads = reads
        self.writes = writes
        self.dma = dma
        self.waits = []
        self.signal = False
        self.ev = None
        self.vc = None
        self.sigcount = 0


class Sched:
    def __init__(self, nc, same_engine_sync=('act', 'dve', 'pool')):
        self.nc = nc
        self.ops = []
        self.same_sync = set(same_engine_sync)

    def op(self, eng, fn, reads=(), writes=(), dma=None):
        self.ops.append(Op(eng, fn, tuple(reads), tuple(writes), dma))

    def fence(self):
        shared = {}
        for e in ENG_ATTR:
            self.ops.append(Op(e, None, ('__fence__', shared), (), None))

    def finalize(self, stack):
        nc = self.nc
        state = {}
        vcs = {e: {} for e in ENG_ATTR}
        eng_count = {e: 0 for e in ENG_ATTR}
        dma_count = {}
        evop = {}
        dmavc = {}
        for op in self.ops:
            e = op.eng
            deps = {}
            if len(op.reads) == 2 and op.reads[0] == '__fence__':
                shared = op.reads[1]
                if not shared:
                    for e2, c2 in eng_count.items():
                        if c2 > 0:
                            shared[('e', e2)] = c2 - 1
                    for dk2, c2 in dma_count.items():
                        shared[dk2] = c2
                deps.update(shared)
                op.reads = ()
            for r in op.reads:
                st = state.get(r)
                if st:
                    for (k, i) in st[0]:
                        if deps.get(k, -1) < i:
                            deps[k] = i
            for w in op.writes:
                st = state.get(w)
                if st:
                    for (k, i) in st[0]:
                        if deps.get(k, -1) < i:
                            deps[k] = i
                    for (k, i) in st[1]:
                        if deps.get(k, -1) < i:
                            deps[k] = i
            vc = vcs[e]
            myk = ('e', e)
            for k, i in deps.items():
                if k == myk and e not in self.same_sync:
                    continue
                if vc.get(k, -1) >= i:
                    continue
                op.waits.append((k, i))
            for (k, i) in op.waits:
                if k[0] == 'e':
                    src = evop[(k, i)]
                    src.signal = True
                    svc = src.vc
                else:
                    svc = dmavc[(k, i)]
                for kk, ii in svc.items():
                    if vc.get(kk, -1) < ii:
                        vc[kk] = ii
                if vc.get(k, -1) < i:
                    vc[k] = i
            if op.fn is None:
                op.ev = None
                continue
            if op.dma is None:
                idx = eng_count[e]
                eng_count[e] = idx + 1
                op.ev = (myk, idx)
                evop[op.ev] = op
                snap = dict(vc)
                snap[myk] = idx
                op.vc = snap
                if e not in self.same_sync:
                    vc[myk] = idx
            else:
                dk = ('d', op.dma)
                c = dma_count.get(dk, 0) + 1
                dma_count[dk] = c
                op.ev = (dk, c)
                dmavc[op.ev] = dict(vc)
            for r in op.reads:
                st = state.setdefault(r, [[], []])
                rl = [x for x in st[1] if x[0] != op.ev[0]]
                rl.append(op.ev)
                st[1] = rl
            for w in op.writes:
                state[w] = [[op.ev], []]
        sems = {}
        for e in ENG_ATTR:
            sems[('e', e)] = stack.enter_context(nc.semaphore('sem_' + e))
        for dk in dma_count:
            sems[dk] = stack.enter_context(nc.semaphore('dsem_%d' % len(sems)))
        cnt = {e: 0 for e in ENG_ATTR}
        for op in self.ops:
            if op.dma is None and op.signal:
                cnt[op.eng] += 1
                op.sigcount = cnt[op.eng]
        for op in self.ops:
            engobj = getattr(nc, ENG_ATTR[op.eng])
            for (k, i) in op.waits:
                val = evop[(k, i)].sigcount if k[0] == 'e' else 16 * i
                engobj.wait_ge(sems[k], val)
            if op.fn is None:
                continue
            ins = op.fn()
            if op.dma is not None:
                ins.then_inc(sems[op.ev[0]], 16)
            elif op.signal:
                ins.then_inc(sems[('e', op.eng)], 1)
        self.stats = dict(nops=len(self.ops), sig=cnt, nsems=len(sems))


def layer_plan(l):
    if l == 0:
        blocks = [((0, 1), 'kv'), ((2, 3), 'kv')] + [((t, t + 1), 'full') for t in range(4, NT, 2)]
    elif l == 1:
        blocks = [((4,), 'cu'), ((5,), 'full')] + [((t, t + 1), 'full') for t in range(6, NT, 2)]
    elif l == 2:
        blocks = [((5, 6), 'kv'), ((7, 8), 'kv'), ((9,), 'full')] + \
                 [((t, t + 1), 'full') for t in range(10, NT, 2)]
    else:
        blocks = [((9,), 'cu')] + [((t, t + 1), 'full') for t in range(10, NT, 2)]
    return blocks


def build_program(layers, same_engine_sync=('act', 'dve', 'pool'), max_blocks=None):
    nc = bass.Bass("TRN2", target_bir_lowering=False, dynamic_dma_scratch_size=8192)
    last = layers[-1]
    final = (last == DEPTH - 1)
    dr = {}

    def din(name, shape):
        dr[name] = nc.dram_tensor(name, list(shape), F32, kind="ExternalInput").ap()

    din("xin", (NT * 128, D))
    din("valid", (128, NT))
    din("vrow", (128, NT, 2))
    din("mem", (256, D))
    din("ident", (128, 128))
    din("w_in", (len(layers), D, NIN))
    din("w_mem_kv", (len(layers), D, D))
    din("w_out", (len(layers), 1536, D))
    din("bt", (2, 128, 16, 256))
    din("chb", (2, 128, 16))
    din("convw", (2, 128, 8, 3))
    din("ln_g", (len(layers), D))
    din("ln_b", (len(layers), D))
    if final:
        xout = nc.dram_tensor("xout", [NMAIN * 128, D], F32, kind="ExternalOutput").ap()
    else:
        xout = nc.dram_tensor("xout", [NT * 128, D], F32, kind="ExternalOutput").ap()
    scratch = {}
    for l in layers[:-1]:
        scratch[l + 1] = nc.dram_tensor("xs%d" % (l + 1), [NT * 128, D], F32).ap()

    S = Sched(nc, same_engine_sync)
    with ExitStack() as st:
        def sb(name, shape, dt):
            return st.enter_context(nc.sbuf_tensor("sb_" + name, list(shape), dt))

        w_in_sb = sb("w_in_sb", (128, 8, NIN), BF16)
        w_out_sb = sb("w_out_sb", (128, 12, D), BF16)
        kmemT = sb("kmemT", (128, 4, 256), BF16)
        vmem = sb("vmem", (128, 2, 4, 129), BF16)
        valid_sb = sb("valid_sb", (128, NT), F32)
        vrow_sb = sb("vrow_sb", (128, NT, 2), F32)
        ident = sb("ident", (128, 128), F32)
        ident_bf = sb("ident_bf", (128, 128), BF16)
        lng = sb("lng", (128, D), F32)
        lnb = sb("lnb", (128, D), F32)
        xs = sb("xs", (128, NXS, D), F32)
        xT = sb("xT", (128, 8, 256), BF16)
        tb = sb("tb", (128, 1, D), F32)
        siluz = sb("siluz", (128, 2, 2, 1536), BF16)
        yT = sb("yT", (128, 12, 128), BF16)
        QmT = sb("QmT", (128, 2, 4, 256), BF16)
        PTm = sb("PTm", (128, 2, 256), BF16)
        small = sb("small", (128, 64), F32)
        stats = sb("stats", (128, 2, 2, 6), F32)
        mv = sb("mv", (128, 2, 4), F32)
        gtmp = sb("gtmp", (128, 2, 256), F32)
        chb_sb = sb("chb_sb", (128, 16), F32)
        mask0 = sb("mask0", (128, 128), BF16)
        import os as _os
        dbg_out = nc.dram_tensor("dbg", [128, 1536], BF16, kind="ExternalOutput").ap() if _os.environ.get("KDBG") else None
        dbg2 = nc.dram_tensor("dbg2", [128, 2048], BF16, kind="ExternalOutput").ap() if _os.environ.get("KDBG2") else None
        cw_sb = sb("cw_sb", (128, 8, 3), F32)
        MIXB = 51968
        MIX = sb("MIX", (128, MIXB // 2), BF16)

        def mview(off, shape, dt):
            esz = 4 if dt == F32 else 2
            n = 1
            for d_ in shape:
                n *= d_
            v = MIX[:, off // 2: off // 2 + (n * esz) // 2]
            if dt == F32:
                v = v.bitcast(F32)
            if len(shape) == 1:
                return v
            names = "abcd"[:len(shape)]
            pat = "p (%s) -> p %s" % (" ".join(names), " ".join(names))
            kw = {names[i]: shape[i] for i in range(len(shape) - 1)}
            return v.rearrange(pat, **kw)

        QT = mview(0, (2, 8, 256), BF16)
        KTr = mview(8192, (8, 1024), BF16)
        Vr = mview(24576, (8, 16, 65), BF16)
        PT = mview(41216, (2, 640), BF16)
        Ep = mview(43776, (16, 256), BF16)
        cu = mview(0, (2, 8, 258), F32)
        p0s = mview(16512, (2, 8, 256), F32)
        p1s = mview(32896, (2, 256), F32)
        cacc = mview(34944, (2, 256), F32)
        szT = mview(36992, (2, 8, 256), BF16)
        memx = mview(0, (2, D), F32)
        memT = mview(8192, (8, 256), BF16)
        wkv = mview(12288, (8, D), BF16)
        btst = tb[:, :, :].rearrange("p a (h c) -> p (a h) c", c=256)

        PS = st.enter_context(nc.psum_tensor("PS", [128, 8 * 512], F32))
        print("SBUF bytes remaining per partition:", nc.sbuf_bytes_remaining)

        def bank(b, n=512, off=0):
            return PS[:, b * 512 + off: b * 512 + off + n]

        T, A, V, G, SP = nc.tensor, nc.scalar, nc.vector, nc.gpsimd, nc.sync
        BK = lambda b: ('bank', b)

        rot = {'i': 0}

        def gen_bank():
            b = 4 + (rot['i'] % 4)
            rot['i'] += 1
            return b

        S.op('sp', lambda: SP.dma_start(out=ident[:, :], in_=dr["ident"]), writes=['ident'], dma='ident')
        S.op('sp', lambda: SP.dma_start(out=valid_sb[:, :], in_=dr["valid"]), writes=['valid'], dma='valid')
        S.op('sp', lambda: SP.dma_start(out=vrow_sb[:, :, :], in_=dr["vrow"]), writes=['vrow'], dma='vrow')
        S.op('dve', lambda: V.tensor_copy(out=ident_bf[:, :], in_=ident[:, :]), reads=['ident'], writes=['ident_bf'])
        S.op('pool', lambda: G.memset(small[:, 16:17], EPS), writes=['epsc'])

        WCH = ((0, 2048), (2048, 4096), (4096, 5120))

        def wkey(col):
            return ('win', 0 if col < 2048 else (1 if col < 4096 else 2))

        def load_weights(l):
            l = layers.index(l)
            for c, (c0, c1) in enumerate(WCH):
                S.op('pool', lambda l=l, c0=c0, c1=c1: G.dma_start(
                    out=w_in_sb[:, :, c0:c1],
                    in_=dr["w_in"][l, :, c0:c1].rearrange("(k p) n -> p k n", p=128)),
                    writes=[('win', c)], dma=('win', c))

        def load_w_out(l):
            l = layers.index(l)
            S.op('pool', lambda l=l: G.dma_start(
                out=w_out_sb[:, :, :], in_=dr["w_out"][l, :, :].rearrange("(k p) n -> p k n", p=128)),
                writes=[('wout', 0), ('wout', 1)], dma=('wout', 0))

        import os as _os2
        _PROBE2 = _os2.environ.get('KPROBE2', '')

        def kv_prologue(l):
            l = layers.index(l)
            S.op('sp', lambda: SP.dma_start(out=memx, in_=dr["mem"].rearrange("(j p) d -> p j d", p=128)),
                 writes=['memx'], dma='memx')
            S.op('pool', lambda l=l: G.dma_start(
                out=wkv[:, :, :], in_=dr["w_mem_kv"][l, :, :].rearrange("(k p) n -> p k n", p=128)),
                writes=[('wkv', 0), ('wkv', 1)], dma=('wkv', 0))
            if _PROBE2 == 'c1':
                return
            for j in range(2):
                for half in range(2):
                    b = gen_bank()
                    for q in range(4):
                        kc = half * 4 + q
                        S.op('pe', lambda b=b, q=q, kc=kc, j=j: T.transpose(
                            out=bank(b, 128, q * 128), in_=memx[:, j, kc * 128:(kc + 1) * 128], identity=ident[:, :]),
                            reads=['memx', 'ident'], writes=[BK(b)])
                    S.op('act', lambda b=b, half=half, j=j: A.copy(
                        out=memT[:, half * 4:half * 4 + 4, j * 128:(j + 1) * 128],
                        in_=bank(b).rearrange("p (c t) -> p c t", c=4)),
                        writes=[BK(b), ('memT', j, half)])
            if _PROBE2 == 'c2':
                return
            memT_keys = [('memT', j, half) for j in range(2) for half in range(2)]
            for h in range(4):
                b = gen_bank()
                for kc in range(8):
                    S.op('pe', lambda b=b, h=h, kc=kc: T.matmul(
                        bank(b, 256), lhsT=wkv[:, kc, h * 128:(h + 1) * 128], rhs=memT[:, kc, :],
                        start=(kc == 0), stop=(kc == 7)),
                        reads=[('wkv', 0)] + memT_keys, writes=[BK(b)])
                S.op('act', lambda b=b, h=h: A.copy(out=kmemT[:, h, :], in_=bank(b, 256)),
                     writes=[BK(b), 'kmemT'])
            if _PROBE2 == 'c3':
                return
            for mb in range(2):
                b = gen_bank()
                for kc in range(8):
                    S.op('pe', lambda b=b, mb=mb, kc=kc: T.matmul(
                        bank(b), lhsT=memT[:, kc, mb * 128:(mb + 1) * 128], rhs=wkv[:, kc, 512:1024],
                        start=(kc == 0), stop=(kc == 7)),
                        reads=[('wkv', 1)] + memT_keys, writes=[BK(b)])
                S.op('dve', lambda b=b, mb=mb: V.tensor_copy(
                    out=vmem[:, mb, :, 0:128], in_=bank(b).rearrange("p (h d) -> p h d", h=4)),
                    writes=[BK(b), 'vmem'])
                S.op('pool', lambda mb=mb: G.memset(vmem[:, mb, :, 128:129], 1.0), writes=['vmem'])

        import os as _os
        PROBE = _os.environ.get("KPROBE", "")
        if PROBE != "a":
            load_weights(layers[0])
            load_w_out(layers[0])

        for li, l in enumerate(layers if PROBE not in ("a", "b") else []):
            attn = (l % 2 == 0)
            la = l // 2
            src = dr["xin"] if li == 0 else scratch[l]
            dst = xout if l == last else scratch[l + 1]

            def dst_rows(t, l=l):
                if l == DEPTH - 1:
                    return (t - HALO) * 128
                return t * 128

            kv_prologue(l)
            S.fence()
            if PROBE == "c":
                break

            S.op('sp', lambda l=li: SP.dma_start(out=lng[:, :], in_=dr["ln_g"][l:l + 1, :].to_broadcast([128, D])),
                 writes=['lng'], dma='lng')
            S.op('sp', lambda l=li: SP.dma_start(out=lnb[:, :], in_=dr["ln_b"][l:l + 1, :].to_broadcast([128, D])),
                 writes=['lnb'], dma='lnb')
            if attn:
                S.op('sp', lambda la=la: SP.dma_start(out=chb_sb[:, :], in_=dr["chb"][la]), writes=['chb'], dma='chb')
                S.op('dve', lambda: V.tensor_scalar(out=small[:, 32:48], in0=chb_sb[:, :], scalar1=-1.0,
                                                    scalar2=None, op0=ALU.mult),
                     reads=['chb'], writes=['negchb'])
                for g in range(4):
                    S.op('sp', lambda la=la, g=g: SP.dma_start(out=btst, in_=dr["bt"][la, :, 4 * g:4 * g + 4, :]),
                         writes=[('tb', 0)], dma='btst')
                    for hh in range(4):
                        h = 4 * g + hh
                        S.op('act', lambda h=h, hh=hh: A.activation(
                            out=Ep[:, h, :], in_=btst[:, hh, :], func=AF.Identity, bias=small[:, 32 + h:33 + h], scale=1.0),
                            reads=[('tb', 0), 'negchb'], writes=['Ep'])
                S.op('pool', lambda: G.memset(Ep[64:128, :, 128:192], -30000.0), writes=['Ep'])
                S.op('pool', lambda: G.memset(mask0[:, :], 0.0), writes=['mask0'])
                S.op('pool', lambda: G.memset(mask0[0:64, 64:128], -30000.0), writes=['mask0'])
            else:
                S.op('sp', lambda la=la: SP.dma_start(out=cw_sb[:, :, :], in_=dr["convw"][la]), writes=['cw'], dma='cw')

            blocks = layer_plan(l)
            if max_blocks is not None:
                blocks = blocks[:max_blocks]
            nb = len(blocks)
            st_l = {'cu_prev': None}

            def phase1(bi, l=l, attn=attn, blocks=blocks, src=src, st_l=st_l):
                tiles, mode = blocks[bi]
                n = len(tiles)
                N = 128 * n
                pb = bi % 2
                for ti, t in enumerate(tiles):
                    slot = t % NXS
                    S.op('sp', lambda slot=slot, t=t: SP.dma_start(out=xs[:, slot, :], in_=src[t * 128:(t + 1) * 128, :]),
                         reads=[('xd', l, t)], writes=[('xs', slot)], dma=('xs', slot))
                    for half in range(2):
                        b = gen_bank()
                        for q in range(4):
                            kc = half * 4 + q
                            S.op('pe', lambda b=b, q=q, kc=kc, slot=slot: T.transpose(
                                out=bank(b, 128, q * 128), in_=xs[:, slot, kc * 128:(kc + 1) * 128], identity=ident[:, :]),
                                reads=[('xs', slot), 'ident'], writes=[BK(b)])
                        if half == 0:
                            S.op('act', lambda b=b, half=half, ti=ti: A.copy(
                                out=xT[:, half * 4:half * 4 + 4, ti * 128:(ti + 1) * 128],
                                in_=bank(b).rearrange("p (c t) -> p c t", c=4)),
                                writes=[BK(b), ('xT', ti, half)])
                        else:
                            S.op('dve', lambda b=b, half=half, ti=ti: V.tensor_copy(
                                out=xT[:, half * 4:half * 4 + 4, ti * 128:(ti + 1) * 128],
                                in_=bank(b).rearrange("p (c t) -> p c t", c=4)),
                                writes=[BK(b), ('xT', ti, half)])
                xTk = [('xT', ti, half) for ti in range(n) for half in range(2)]

                def tokmajor(ti, col0, evac):
                    b = gen_bank()
                    c = col0 // 512
                    for kc in range(8):
                        S.op('pe', lambda b=b, kc=kc, ti=ti, col0=col0: T.matmul(
                            bank(b), lhsT=xT[:, kc, ti * 128:(ti + 1) * 128], rhs=w_in_sb[:, kc, col0:col0 + 512],
                            start=(kc == 0), stop=(kc == 7)),
                            reads=[('xT', ti, 0), ('xT', ti, 1), wkey(col0)], writes=[BK(b)])
                    evac(b)

                def featmajor(col0, evac):
                    b = gen_bank()
                    c = col0 // 512
                    for kc in range(8):
                        S.op('pe', lambda b=b, kc=kc, col0=col0: T.matmul(
                            bank(b, N), lhsT=w_in_sb[:, kc, col0:col0 + 128], rhs=xT[:, kc, 0:N],
                            start=(kc == 0), stop=(kc == 7)),
                            reads=xTk + [wkey(col0)], writes=[BK(b)])
                    evac(b)

                if attn:
                    for ti, t in enumerate(tiles):
                        vs = t % 8
                        for hb in range(2):
                            def ev(b, t=t, vs=vs, hb=hb):
                                S.op('dve', lambda: V.tensor_scalar(
                                    out=Vr[:, vs, hb * 8:(hb + 1) * 8, 0:64],
                                    in0=bank(b).rearrange("p (h d) -> p h d", h=8),
                                    scalar1=valid_sb[:, t:t + 1], scalar2=None, op0=ALU.mult),
                                    reads=['valid'], writes=[BK(b), ('V', vs)])
                            tokmajor(ti, 2048 + hb * 512, ev)
                        S.op('pool', lambda vs=vs, t=t: G.tensor_copy(
                            out=Vr[:, vs, :, 64:65], in_=valid_sb[:, t:t + 1].unsqueeze(1).to_broadcast([128, 16, 1])),
                            reads=['valid'], writes=[('V', vs)])
                    for j in range(8):
                        def ev(b, j=j):
                            for ti, t in enumerate(tiles):
                                ks = t % 8
                                if (j + ti) % 2 == 0:
                                    S.op('act', lambda ti=ti, ks=ks: A.copy(
                                        out=KTr[:, j, ks * 128:(ks + 1) * 128], in_=bank(b, 128, ti * 128)),
                                        writes=[BK(b), ('K', ks)])
                                else:
                                    S.op('dve', lambda ti=ti, ks=ks: V.tensor_copy(
                                        out=KTr[:, j, ks * 128:(ks + 1) * 128], in_=bank(b, 128, ti * 128)),
                                        writes=[BK(b), ('K', ks)])
                        featmajor(1024 + j * 128, ev)
                    if mode == 'full':
                        for j in range(8):
                            def ev(b, j=j):
                                S.op('act', lambda: A.mul(out=QT[:, pb, j, 0:N], in_=bank(b, N), mul=0.125),
                                     writes=[BK(b), ('QT', pb)])
                            featmajor(j * 128, ev)
                else:
                    cb = bi % 2
                    prev = st_l['cu_prev']
                    if prev is not None:
                        pcb, pN, pt = prev
                        S.op('pool', lambda cb=cb, pcb=pcb, pN=pN, pt=pt: G.tensor_tensor(
                            out=cu[:, cb, :, 0:2], in0=cu[:, pcb, :, pN:pN + 2],
                            in1=vrow_sb[:, pt:pt + 1, :].to_broadcast([128, 8, 2]), op=ALU.mult),
                            reads=[('cu', pcb), 'vrow'], writes=[('cu', cb)])
                    else:
                        S.op('pool', lambda cb=cb: G.memset(cu[:, cb, :, 0:2], 0.0), writes=[('cu', cb)])
                    for j in range(8):
                        pj = j % 2

                        def ev1(b, pj=pj):
                            S.op('act', lambda: A.copy(out=p1s[:, pj, 0:N], in_=bank(b, N)),
                                 writes=[BK(b), ('p1s', pj)])
                        featmajor(1024 + j * 128, ev1)

                        def ev2(b, j=j, pj=pj, cb=cb):
                            S.op('dve', lambda: V.tensor_tensor(out=cu[:, cb, j, 2:2 + N], in0=bank(b, N),
                                                                in1=p1s[:, pj, 0:N], op=ALU.mult),
                                 reads=[('p1s', pj)], writes=[BK(b), ('cu', cb)])
                        featmajor(2048 + j * 128, ev2)
                    st_l['cu_prev'] = (cb, N, tiles[-1])
                    if mode == 'full':
                        for j in range(8):
                            def ev(b, j=j):
                                S.op('act', lambda: A.copy(out=p0s[:, pb, j, 0:N], in_=bank(b, N)),
                                     writes=[BK(b), ('p0s', pb, j)])
                            featmajor(j * 128, ev)

                            def evz(b, j=j):
                                S.op('act', lambda: A.activation(out=szT[:, pb, j, 0:N], in_=bank(b, N), func=AF.Silu),
                                     writes=[BK(b), ('szT', pb, j)])
                            featmajor(3584 + j * 128, evz)
                if mode == 'full':
                    for j in range(4):
                        def ev(b, j=j):
                            S.op('act', lambda: A.mul(out=QmT[:, pb, j, 0:N], in_=bank(b, N), mul=1.0 / math.sqrt(128.0)),
                                 writes=[BK(b), ('QmT', pb)])
                        featmajor(3072 + j * 128, ev)
                    zblocks = (0, 1, 2) if attn else (2,)
                    for ti, t in enumerate(tiles):
                        for zb in zblocks:
                            def ev(b, ti=ti, zb=zb):
                                S.op('act', lambda: A.activation(out=siluz[:, pb, ti, zb * 512:(zb + 1) * 512],
                                                                 in_=bank(b), func=AF.Silu),
                                     writes=[BK(b), ('siluz', pb, ti)])
                            tokmajor(ti, 3584 + zb * 512, ev)

            def phase2(bi, l=l, attn=attn, blocks=blocks, dst=dst):
                tiles, mode = blocks[bi]
                if mode != 'full':
                    return
                _d2 = _os.environ.get("KDBG2")
                if _d2 and int(_os.environ["KDBG"]) == tiles[0]:
                    if _d2 == 'xT':
                        S.op('sp', lambda: SP.dma_start(out=dbg2, in_=xT[:, :, :].rearrange("p a b -> p (a b)")),
                             reads=[('xT', 0, 0), ('xT', 0, 1), ('xT', 1, 0), ('xT', 1, 1)], writes=[('xd', l + 1, 1)], dma='gdbg2')
                    elif _d2 == 'w':
                        S.op('sp', lambda: SP.dma_start(out=dbg2, in_=w_in_sb[:, 0, 3072:5120]),
                             reads=[('win', 1), ('win', 2)], writes=[('xd', l + 1, 1)], dma='gdbg2')
                    elif _d2 == 'QT':
                        S.op('sp', lambda: SP.dma_start(out=dbg2.rearrange("p (a b) -> p a b", a=8), in_=QT[:, bi % 2, :, :]),
                             reads=[('QT', bi % 2)], writes=[('xd', l + 1, 1)], dma='gdbg2')
                    elif _d2 == 'K':
                        S.op('sp', lambda: SP.dma_start(out=dbg2[:, 0:1024].rearrange("p (a b) -> p a b", a=8),
                                                        in_=KTr[:, :, (tiles[0] % 8) * 128:(tiles[0] % 8) * 128 + 128]),
                             reads=[('K', tiles[0] % 8)], writes=[('xd', l + 1, 1)], dma='gdbg2')
                    elif _d2 == 'QmT':
                        S.op('sp', lambda: SP.dma_start(out=dbg2[:, 0:1024].rearrange("p (a b) -> p a b", a=4), in_=QmT[:, bi % 2, :, :]),
                             reads=[('QmT', bi % 2)], writes=[('xd', l + 1, 1)], dma='gdbg2')
                    elif _d2 == 'V':
                        S.op('sp', lambda: SP.dma_start(out=dbg2[:, 0:1040].rearrange("p (a b) -> p a b", a=16), in_=Vr[:, tiles[0] % 8, :, :]),
                             reads=[('V', tiles[0] % 8)], writes=[('xd', l + 1, 1)], dma='gdbg2')
                    elif _d2 == 'sz':
                        S.op('sp', lambda: SP.dma_start(out=dbg2[:, 0:1536], in_=siluz[:, bi % 2, 0, :]),
                             reads=[('siluz', bi % 2, 0)], writes=[('xd', l + 1, 1)], dma='gdbg2')
                n = len(tiles)
                N = 128 * n
                pb = bi % 2
                if not attn:
                    cb = bi % 2
                    for j in range(8):
                        aj = j % 2
                        S.op('dve', lambda j=j, aj=aj: V.tensor_scalar(
                            out=cacc[:, aj, 0:N], in0=cu[:, cb, j, 0:N], scalar1=cw_sb[:, j, 0:1], scalar2=None,
                            op0=ALU.mult), reads=[('cu', cb), 'cw'], writes=[('cacc', aj)])
                        for k in (1, 2):
                            S.op('dve', lambda j=j, aj=aj, k=k: V.scalar_tensor_tensor(
                                out=cacc[:, aj, 0:N], in0=cu[:, cb, j, k:k + N], scalar=cw_sb[:, j, k:k + 1],
                                in1=cacc[:, aj, 0:N], op0=ALU.mult, op1=ALU.add),
                                reads=[('cu', cb), 'cw'], writes=[('cacc', aj)])
                        S.op('pool', lambda j=j, aj=aj: G.tensor_tensor(
                            out=cacc[:, aj, 0:N], in0=cacc[:, aj, 0:N], in1=p0s[:, pb, j, 0:N], op=ALU.mult),
                            reads=[('p0s', pb, j)], writes=[('cacc', aj)])
                        S.op('pool', lambda j=j, aj=aj: G.tensor_tensor(
                            out=szT[:, pb, j, 0:N], in0=cacc[:, aj, 0:N], in1=szT[:, pb, j, 0:N], op=ALU.mult),
                            reads=[('cacc', aj)], writes=[('szT', pb, j)])
                def tile_body(ti, t):
                    xslot = t % NXS
                    if attn:
                        def scores(h):
                            sbuf = h % 2
                            hp = h % 2
                            ch = h // 2
                            for j in range(5):
                                ks = (t - 4 + j) % 8
                                hasb = j in (0, 3, 4)
                                S.op('pe', lambda sbuf=sbuf, j=j, ks=ks, hp=hp, ch=ch, hasb=hasb: T.matmul(
                                    PS[:, sbuf * 1024 + j * 128: sbuf * 1024 + (j + 1) * 128],
                                    lhsT=KTr[hp * 64:(hp + 1) * 64, ch, ks * 128:(ks + 1) * 128],
                                    rhs=QT[hp * 64:(hp + 1) * 64, pb, ch, ti * 128:(ti + 1) * 128],
                                    start=True, stop=(not hasb)),
                                    reads=[('K', ks), ('QT', pb)], writes=[BK(2 * sbuf), BK(2 * sbuf + 1)])
                                if hasb:
                                    if j == 0:
                                        brhs = mask0[:, :]
                                        bk = 'mask0'
                                    else:
                                        brhs = Ep[:, h, (j - 3) * 128:(j - 2) * 128]
                                        bk = 'Ep'
                                    S.op('pe', lambda sbuf=sbuf, j=j, brhs=brhs: T.matmul(
                                        PS[:, sbuf * 1024 + j * 128: sbuf * 1024 + (j + 1) * 128],
                                        lhsT=ident_bf[:, :], rhs=brhs, start=False, stop=True),
                                        reads=[bk, 'ident_bf'], writes=[BK(2 * sbuf), BK(2 * sbuf + 1)])
                            pbuf = h % 2
                            S.op('act', lambda sbuf=sbuf, pbuf=pbuf: A.activation(
                                out=PT[:, pbuf, :], in_=PS[:, sbuf * 1024: sbuf * 1024 + 640], func=AF.Exp),
                                writes=[BK(2 * sbuf), BK(2 * sbuf + 1), ('PT', pbuf)])

                        pvb = {'b': None}

                        def pv(h):
                            hg = h % 4
                            if hg == 0:
                                pvb['b'] = gen_bank()
                            b = pvb['b']
                            pbuf = h % 2
                            for j in range(5):
                                vs = (t - 4 + j) % 8
                                S.op('pe', lambda b=b, hg=hg, j=j, vs=vs, pbuf=pbuf, h=h: T.matmul(
                                    bank(b, 65, hg * 65), lhsT=PT[:, pbuf, j * 128:(j + 1) * 128],
                                    rhs=Vr[:, vs, h, :], start=(j == 0), stop=(j == 4)),
                                    reads=[('PT', pbuf), ('V', vs)], writes=[BK(b)])
                            if hg == 3:
                                h0 = h - 3
                                gb = (h // 4) % 2
                                pv4 = bank(b, 260).rearrange("p (h d) -> p h d", h=4)
                                S.op('dve', lambda pv4=pv4: V.tensor_scalar(
                                    out=small[:, 0:4].unsqueeze(2), in0=pv4[:, :, 64:65], scalar1=1e-30, scalar2=None,
                                    op0=ALU.add), writes=[BK(b), 'rs'])
                                S.op('dve', lambda: V.reciprocal(out=small[:, 0:4], in_=small[:, 0:4]), writes=['rs'])
                                S.op('dve', lambda pv4=pv4, gb=gb: V.tensor_tensor(
                                    out=gtmp[:, gb, :].rearrange("p (h d) -> p h d", h=4), in0=pv4[:, :, 0:64],
                                    in1=small[:, 0:4].unsqueeze(2).to_broadcast([128, 4, 64]), op=ALU.mult),
                                    reads=['rs'], writes=[BK(b), ('gtmp', gb)])
                                S.op('pool', lambda gb=gb, h0=h0: G.tensor_tensor(
                                    out=siluz[:, pb, ti, h0 * 64:(h0 + 4) * 64], in0=gtmp[:, gb, :],
                                    in1=siluz[:, pb, ti, h0 * 64:(h0 + 4) * 64], op=ALU.mult),
                                    reads=[('gtmp', gb)], writes=[('siluz', pb, ti)])

                        scores(0)
                        for h in range(16):
                            if h + 1 < 16:
                                scores(h + 1)
                            pv(h)
                    for hm in range(4):
                        b = gen_bank()
                        for mb in range(2):
                            S.op('pe', lambda b=b, mb=mb, hm=hm: T.matmul(
                                bank(b, 128, mb * 128), lhsT=kmemT[:, hm, mb * 128:(mb + 1) * 128],
                                rhs=QmT[:, pb, hm, ti * 128:(ti + 1) * 128], start=True, stop=True),
                                reads=['kmemT', ('QmT', pb)], writes=[BK(b)])
                        mbuf = hm % 2
                        S.op('act', lambda b=b, mbuf=mbuf: A.activation(out=PTm[:, mbuf, :], in_=bank(b, 256), func=AF.Exp),
                             writes=[BK(b), ('PTm', mbuf)])
                        for mb in range(2):
                            S.op('pe', lambda b=b, mb=mb, hm=hm, mbuf=mbuf: T.matmul(
                                bank(b, 129, 256), lhsT=PTm[:, mbuf, mb * 128:(mb + 1) * 128],
                                rhs=vmem[:, mb, hm, :], start=(mb == 0), stop=(mb == 1)),
                                reads=[('PTm', mbuf), 'vmem'], writes=[BK(b)])
                        S.op('dve', lambda b=b: V.tensor_scalar(
                            out=small[:, 8:9], in0=bank(b, 1, 256 + 128), scalar1=1e-30, scalar2=None, op0=ALU.add),
                            writes=[BK(b), 'rsm'])
                        S.op('dve', lambda: V.reciprocal(out=small[:, 8:9], in_=small[:, 8:9]), writes=['rsm'])
                        S.op('dve', lambda b=b, hm=hm: V.scalar_tensor_tensor(
                            out=siluz[:, pb, ti, 1024 + hm * 128:1024 + (hm + 1) * 128], in0=bank(b, 128, 256),
                            scalar=small[:, 8:9], in1=siluz[:, pb, ti, 1024 + hm * 128:1024 + (hm + 1) * 128],
                            op0=ALU.mult, op1=ALU.mult),
                            reads=['rsm'], writes=[BK(b), ('siluz', pb, ti)])
                    if _os.environ.get("KDBG") and int(_os.environ["KDBG"]) == t:
                        S.op('sp', lambda: SP.dma_start(out=dbg_out, in_=siluz[:, pb, ti, :]),
                             reads=[('siluz', pb, ti)], writes=[('xd', l + 1, 0)], dma='gdbg')
                    if _os.environ.get("KDBG2") == 'PT' and int(_os.environ["KDBG"]) == t:
                        S.op('sp', lambda: SP.dma_start(out=dbg2[:, 0:1280].rearrange("p (a b) -> p a b", a=2), in_=PT[:, :, :]),
                             reads=[('PT', 0), ('PT', 1)], writes=[('xd', l + 1, 1)], dma='gdbg2')
                        S.op('sp', lambda: SP.dma_start(out=dbg2[:, 1280:1792].rearrange("p (a b) -> p a b", a=2), in_=PTm[:, :, :]),
                             reads=[('PTm', 0), ('PTm', 1)], writes=[('xd', l + 1, 2)], dma='gdbg3')
                    chunks = list(range(12)) if attn else list(range(8, 12))
                    for g0 in range(0, len(chunks), 4):
                        grp = chunks[g0:g0 + 4]
                        b = gen_bank()
                        psb = bank(b).bitcast(BF16)
                        for q, j in enumerate(grp):
                            S.op('pe', lambda q=q, j=j, psb=psb: T.transpose(
                                out=psb[:, q * 128:(q + 1) * 128], in_=siluz[:, pb, ti, j * 128:(j + 1) * 128],
                                identity=ident_bf[:, :]),
                                reads=[('siluz', pb, ti), 'ident_bf'], writes=[BK(b)])
                        j0 = grp[0]
                        S.op('dve', lambda psb=psb, j0=j0: V.tensor_copy(
                            out=yT[:, j0:j0 + 4, :], in_=psb[:, 0:512].rearrange("p (c t) -> p c t", c=4)),
                            writes=[BK(b), ('yT', j0 // 4)])
                    tslot = 0
                    for half in range(2):
                        b = gen_bank()
                        for ec in range(12):
                            if attn or ec >= 8:
                                lh = yT[:, ec, :]
                                rk = [('yT', ec // 4)]
                            else:
                                lh = szT[:, pb, ec, ti * 128:(ti + 1) * 128]
                                rk = [('szT', pb, ec)]
                            S.op('pe', lambda b=b, ec=ec, half=half, lh=lh: T.matmul(
                                bank(b), lhsT=lh, rhs=w_out_sb[:, ec, half * 512:(half + 1) * 512],
                                start=(ec == 0), stop=(ec == 11)),
                                reads=rk + [('wout', half)], writes=[BK(b)])
                        S.op('dve', lambda b=b, half=half, xslot=xslot, tslot=tslot: V.scalar_tensor_tensor(
                            out=tb[:, tslot, half * 512:(half + 1) * 512], in0=xs[:, xslot, half * 512:(half + 1) * 512],
                            scalar=ALPHA, in1=bank(b), op0=ALU.mult, op1=ALU.add),
                            reads=[('xs', xslot)], writes=[BK(b), ('tb', tslot)])
                        S.op('dve', lambda half=half, tslot=tslot: V.bn_stats(
                            out=stats[:, tslot, half, :], in_=tb[:, tslot, half * 512:(half + 1) * 512]),
                            reads=[('tb', tslot)], writes=[('stats', tslot)])
                    S.op('dve', lambda tslot=tslot: V.bn_aggr(
                        out=mv[:, tslot, 0:2], in_=stats[:, tslot, :, :].rearrange("p a b -> p (a b)")),
                        reads=[('stats', tslot)], writes=[('mv', tslot)])
                    S.op('act', lambda tslot=tslot: A.activation(
                        out=small[:, 20:21], in_=mv[:, tslot, 1:2], func=AF.Ln, bias=small[:, 16:17], scale=1.0),
                        reads=[('mv', tslot), 'epsc'], writes=[('mvb', tslot)])
                    S.op('act', lambda tslot=tslot: A.activation(
                        out=mv[:, tslot, 2:3], in_=small[:, 20:21], func=AF.Exp, scale=-0.5),
                        reads=[('mvb', tslot)], writes=[('mvc', tslot)])
                    S.op('dve', lambda tslot=tslot: V.scalar_tensor_tensor(
                        out=mv[:, tslot, 3:4], in0=mv[:, tslot, 0:1], scalar=-1.0, in1=mv[:, tslot, 2:3],
                        op0=ALU.mult, op1=ALU.mult), reads=[('mv', tslot), ('mvc', tslot)], writes=[('mvd', tslot)])
                    S.op('act', lambda tslot=tslot: A.activation(
                        out=tb[:, tslot, :], in_=tb[:, tslot, :], func=AF.Identity,
                        bias=mv[:, tslot, 3:4], scale=mv[:, tslot, 2:3]),
                        reads=[('mvd', tslot), ('mvc', tslot)], writes=[('tb', tslot)])
                    S.op('pool', lambda tslot=tslot: G.tensor_tensor(
                        out=tb[:, tslot, :], in0=tb[:, tslot, :], in1=lng[:, :], op=ALU.mult),
                        reads=['lng'], writes=[('tb', tslot)])
                    S.op('pool', lambda tslot=tslot: G.tensor_tensor(
                        out=tb[:, tslot, :], in0=tb[:, tslot, :], in1=lnb[:, :], op=ALU.add),
                        reads=['lnb'], writes=[('tb', tslot)])
                    if l == DEPTH - 1 and t < HALO:
                        return
                    r0 = dst_rows(t)
                    S.op('sp', lambda tslot=tslot, r0=r0: SP.dma_start(out=dst[r0:r0 + 128, :], in_=tb[:, tslot, :]),
                         reads=[('tb', tslot)], writes=[('xd', l + 1, t)], dma=('tbo', tslot))

                for ti_, t_ in enumerate(tiles):
                    tile_body(ti_, t_)

            if nb > 0:
                phase1(0)
            for bi in range(nb):
                if bi + 1 < nb:
                    phase1(bi + 1)
                elif li + 1 < len(layers):
                    load_weights(layers[li + 1])
                phase2(bi)
            if li + 1 < len(layers):
                load_w_out(layers[li + 1])
            S.fence()

        outkeys = [('xd', last + 1, t) for t in range(NT)]
        S.op('sp', None, reads=outkeys)
        S.finalize(st)
    return nc, S


def _host_layout(inputs):
    x = np.asarray(inputs["x"], dtype=np.float32)
    mem = np.asarray(inputs["mem"], dtype=np.float32)
    rel_bias = np.asarray(inputs["rel_bias"], dtype=np.float32)
    conv_w = np.asarray(inputs["conv_w"], dtype=np.float32)
    k = np.arange(128)[:, None]
    q = np.arange(128)[None, :]
    idx3 = np.minimum(q - k + 128, 128) + 128
    idx4 = (q - k) + 128
    idx = np.concatenate([idx3, idx4], axis=1)
    bt = np.ascontiguousarray(np.transpose(rel_bias[:, :, idx], (0, 2, 1, 3)))
    chb = np.ascontiguousarray(np.broadcast_to(rel_bias[:, None, :, 256], (2, 128, 16)))
    convw = np.ascontiguousarray(np.transpose(conv_w.reshape(2, 3, 8, 128), (0, 3, 2, 1)))
    common = dict(
        ident=np.eye(128, dtype=np.float32),
        w_in=np.asarray(inputs["w_in"], dtype=np.float32),
        w_mem_kv=np.asarray(inputs["w_mem_kv"], dtype=np.float32),
        w_out=np.asarray(inputs["w_out"], dtype=np.float32),
        bt=bt, chb=chb, convw=convw,
        ln_g=np.asarray(inputs["ln_g"], dtype=np.float32),
        ln_b=np.asarray(inputs["ln_b"], dtype=np.float32),
    )
    per_core = []
    for c in range(8):
        b, half = c // 2, c % 2
        s0 = half * 4096
        w0 = s0 - HALO * 128
        xw = np.zeros((NT * 128, D), np.float32)
        lo = max(w0, 0)
        xw[lo - w0:] = x[b, lo:s0 + 4096]
        vt = np.zeros((NT * 128,), np.float32)
        vt[lo - w0:] = 1.0
        valid = np.ascontiguousarray(vt.reshape(NT, 128).T)
        vrow = np.ascontiguousarray(np.broadcast_to(vt.reshape(NT, 128)[None, :, 126:128], (128, NT, 2)))
        m = dict(common)
        m.update(xin=xw, valid=valid, vrow=vrow, mem=np.ascontiguousarray(mem[b]))
        per_core.append(m)
    return per_core


_CACHE = {}


def _get_program(layers):
    key = tuple(layers)
    if key not in _CACHE:
        _CACHE[key] = build_program(list(layers))[0]
    return _CACHE[key]


FUSED = True


def kernel(**inputs):
    per_core = _host_layout(inputs)
    groups = [[0, 1, 2, 3]] if FUSED else [[0], [1], [2], [3]]
    full = {k: per_core[0][k] for k in ("w_in", "w_mem_kv", "w_out", "ln_g", "ln_b")}
    for grp in groups:
        nc = _get_program(grp)
        sl = {k: np.ascontiguousarray(v[grp[0]:grp[-1] + 1]) for k, v in full.items()}
        maps = []
        for c in range(8):
            m = dict(per_core[c])
            m.update(sl)
            maps.append(m)
        res = run_bass_kernel_spmd(nc, maps, core_ids=list(range(8)))
        outs = [r["xout"] for r in res.results]
        if grp[-1] != DEPTH - 1:
            for c in range(8):
                per_core[c]["xin"] = outs[c]
    out = np.zeros((4, 8192, D), np.float32)
    for c in range(8):
        b, half = c // 2, c % 2
        out[b, half * 4096:(half + 1) * 4096] = outs[c]
    return out
```

```python
import math
from contextlib import ExitStack

import numpy as np
import concourse.bass as bass
import concourse.mybir as mybir
from concourse.bass_utils import run_bass_kernel_spmd

F32 = mybir.dt.float32
BF16 = mybir.dt.bfloat16
AF = mybir.ActivationFunctionType
ALU = mybir.AluOpType

D = 1024
NIN = 5120
NT = 42
HALO = 10
NMAIN = 32
DEPTH = 4
ALPHA = (2.0 * DEPTH) ** 0.25
EPS = 1e-5
NXS = 4

ENG_ATTR = {'pe': 'tensor', 'act': 'scalar', 'dve': 'vector', 'pool': 'gpsimd', 'sp': 'sync'}


class Op:
    __slots__ = ('eng', 'fn', 'reads', 'writes', 'dma', 'waits', 'signal', 'ev', 'vc', 'sigcount')

    def __init__(self, eng, fn, reads, writes, dma):
        self.eng = eng
        self.fn = fn
        self.reads = reads
        self.writes = writes
        self.dma = dma
        self.waits = []
        self.signal = False
        self.ev = None
        self.vc = None
        self.sigcount = 0


class Sched:
    def __init__(self, nc, same_engine_sync=('act', 'dve', 'pool')):
        self.nc = nc
        self.ops = []
        self.same_sync = set(same_engine_sync)

    def op(self, eng, fn, reads=(), writes=(), dma=None):
        self.ops.append(Op(eng, fn, tuple(reads), tuple(writes), dma))

    def fence(self):
        shared = {}
        for e in ENG_ATTR:
            self.ops.append(Op(e, None, ('__fence__', shared), (), None))

    def finalize(self, stack):
        nc = self.nc
        state = {}
        vcs = {e: {} for e in ENG_ATTR}
        eng_count = {e: 0 for e in ENG_ATTR}
        dma_count = {}
        evop = {}
        dmavc = {}
        for op in self.ops:
            e = op.eng
            deps = {}
            if len(op.reads) == 2 and op.reads[0] == '__fence__':
                shared = op.reads[1]
                if not shared:
                    for e2, c2 in eng_count.items():
                        if c2 > 0:
                            shared[('e', e2)] = c2 - 1
                    for dk2, c2 in dma_count.items():
                        shared[dk2] = c2
                deps.update(shared)
                op.reads = ()
            for r in op.reads:
                st = state.get(r)
                if st:
                    for (k, i) in st[0]:
                        if deps.get(k, -1) < i:
                            deps[k] = i
            for w in op.writes:
                st = state.get(w)
                if st:
                    for (k, i) in st[0]:
                        if deps.get(k, -1) < i:
                            deps[k] = i
                    for (k, i) in st[1]:
                        if deps.get(k, -1) < i:
                            deps[k] = i
            vc = vcs[e]
            myk = ('e', e)
            for k, i in deps.items():
                if k == myk and e not in self.same_sync:
                    continue
                if vc.get(k, -1) >= i:
                    continue
                op.waits.append((k, i))
            for (k, i) in op.waits:
                if k[0] == 'e':
                    src = evop[(k, i)]
                    src.signal = True
                    svc = src.vc
                else:
                    svc = dmavc[(k, i)]
                for kk, ii in svc.items():
                    if vc.get(kk, -1) < ii:
                        vc[kk] = ii
                if vc.get(k, -1) < i:
                    vc[k] = i
            if op.fn is None:
                op.ev = None
                continue
            if op.dma is None:
                idx = eng_count[e]
                eng_count[e] = idx + 1
                op.ev = (myk, idx)
                evop[op.ev] = op
                snap = dict(vc)
                snap[myk] = idx
                op.vc = snap
                if e not in self.same_sync:
                    vc[myk] = idx
            else:
                dk = ('d', op.dma)
                c = dma_count.get(dk, 0) + 1
                dma_count[dk] = c
                op.ev = (dk, c)
                dmavc[op.ev] = dict(vc)
            for r in op.reads:
                st = state.setdefault(r, [[], []])
                rl = [x for x in st[1] if x[0] != op.ev[0]]
                rl.append(op.ev)
                st[1] = rl
            for w in op.writes:
                state[w] = [[op.ev], []]
        sems = {}
        for e in ENG_ATTR:
            sems[('e', e)] = stack.enter_context(nc.semaphore('sem_' + e))
        for dk in dma_count:
            sems[dk] = stack.enter_context(nc.semaphore('dsem_%d' % len(sems)))
        cnt = {e: 0 for e in ENG_ATTR}
        for op in self.ops:
            if op.dma is None and op.signal:
                cnt[op.eng] += 1
                op.sigcount = cnt[op.eng]
        for op in self.ops:
            engobj = getattr(nc, ENG_ATTR[op.eng])
            for (k, i) in op.waits:
                val = evop[(k, i)].sigcount if k[0] == 'e' else 16 * i
                engobj.wait_ge(sems[k], val)
            if op.fn is None:
                continue
            ins = op.fn()
            if op.dma is not None:
                ins.then_inc(sems[op.ev[0]], 16)
            elif op.signal:
                ins.then_inc(sems[('e', op.eng)], 1)
        self.stats = dict(nops=len(self.ops), sig=cnt, nsems=len(sems))


def layer_plan(l):
    if l == 0:
        blocks = [((0, 1), 'kv'), ((2, 3), 'kv')] + [((t, t + 1), 'full') for t in range(4, NT, 2)]
    elif l == 1:
        blocks = [((4,), 'cu'), ((5,), 'full')] + [((t, t + 1), 'full') for t in range(6, NT, 2)]
    elif l == 2:
        blocks = [((5, 6), 'kv'), ((7, 8), 'kv'), ((9,), 'full')] + \
                 [((t, t + 1), 'full') for t in range(10, NT, 2)]
    else:
        blocks = [((9,), 'cu')] + [((t, t + 1), 'full') for t in range(10, NT, 2)]
    return blocks


def build_program(layers, same_engine_sync=('act', 'dve', 'pool'), max_blocks=None):
    nc = bass.Bass("TRN2", target_bir_lowering=False, dynamic_dma_scratch_size=8192)
    last = layers[-1]
    final = (last == DEPTH - 1)
    dr = {}

    def din(name, shape):
        dr[name] = nc.dram_tensor(name, list(shape), F32, kind="ExternalInput").ap()

    din("xin", (NT * 128, D))
    din("valid", (128, NT))
    din("vrow", (128, NT, 2))
    din("mem", (256, D))
    din("ident", (128, 128))
    din("w_in", (len(layers), D, NIN))
    din("w_mem_kv", (len(layers), D, D))
    din("w_out", (len(layers), 1536, D))
    din("bt", (2, 128, 16, 256))
    din("chb", (2, 128, 16))
    din("convw", (2, 128, 8, 3))
    din("ln_g", (len(layers), D))
    din("ln_b", (len(layers), D))
    if final:
        xout = nc.dram_tensor("xout", [NMAIN * 128, D], F32, kind="ExternalOutput").ap()
    else:
        xout = nc.dram_tensor("xout", [NT * 128, D], F32, kind="ExternalOutput").ap()
    scratch = {}
    for l in layers[:-1]:
        scratch[l + 1] = nc.dram_tensor("xs%d" % (l + 1), [NT * 128, D], F32).ap()

    S = Sched(nc, same_engine_sync)
    with ExitStack() as st:
        def sb(name, shape, dt):
            return st.enter_context(nc.sbuf_tensor("sb_" + name, list(shape), dt))

        w_in_sb = sb("w_in_sb", (128, 8, NIN), BF16)
        w_out_sb = sb("w_out_sb", (128, 12, D), BF16)
        kmemT = sb("kmemT", (128, 4, 256), BF16)
        vmem = sb("vmem", (128, 2, 4, 129), BF16)
        valid_sb = sb("valid_sb", (128, NT), F32)
        vrow_sb = sb("vrow_sb", (128, NT, 2), F32)
        ident = sb("ident", (128, 128), F32)
        ident_bf = sb("ident_bf", (128, 128), BF16)
        lng = sb("lng", (128, D), F32)
        lnb = sb("lnb", (128, D), F32)
        xs = sb("xs", (128, NXS, D), F32)
        xT = sb("xT", (128, 8, 256), BF16)
        tb = sb("tb", (128, 1, D), F32)
        siluz = sb("siluz", (128, 2, 2, 1536), BF16)
        yT = sb("yT", (128, 12, 128), BF16)
        QmT = sb("QmT", (128, 2, 4, 256), BF16)
        PTm = sb("PTm", (128, 2, 256), BF16)
        small = sb("small", (128, 64), F32)
        stats = sb("stats", (128, 2, 2, 6), F32)
        mv = sb("mv", (128, 2, 4), F32)
        gtmp = sb("gtmp", (128, 2, 256), F32)
        chb_sb = sb("chb_sb", (128, 16), F32)
        mask0 = sb("mask0", (128, 128), BF16)
        import os as _os
        dbg_out = nc.dram_tensor("dbg", [128, 1536], BF16, kind="ExternalOutput").ap() if _os.environ.get("KDBG") else None
        dbg2 = nc.dram_tensor("dbg2", [128, 2048], BF16, kind="ExternalOutput").ap() if _os.environ.get("KDBG2") else None
        cw_sb = sb("cw_sb", (128, 8, 3), F32)
        MIXB = 51968
        MIX = sb("MIX", (128, MIXB // 2), BF16)

        def mview(off, shape, dt):
            esz = 4 if dt == F32 else 2
            n = 1
            for d_ in shape:
                n *= d_
            v = MIX[:, off // 2: off // 2 + (n * esz) // 2]
            if dt == F32:
                v = v.bitcast(F32)
            if len(shape) == 1:
                return v
            names = "abcd"[:len(shape)]
            pat = "p (%s) -> p %s" % (" ".join(names), " ".join(names))
            kw = {names[i]: shape[i] for i in range(len(shape) - 1)}
            return v.rearrange(pat, **kw)

        QT = mview(0, (2, 8, 256), BF16)
        KTr = mview(8192, (8, 1024), BF16)
        Vr = mview(24576, (8, 16, 65), BF16)
        PT = mview(41216, (2, 640), BF16)
        Ep = mview(43776, (16, 256), BF16)
        cu = mview(0, (2, 8, 258), F32)
        p0s = mview(16512, (2, 8, 256), F32)
        p1s = mview(32896, (2, 256), F32)
        cacc = mview(34944, (2, 256), F32)
        szT = mview(36992, (2, 8, 256), BF16)
        memx = mview(0, (2, D), F32)
        memT = mview(8192, (8, 256), BF16)
        wkv = mview(12288, (8, D), BF16)
        btst = tb[:, :, :].rearrange("p a (h c) -> p (a h) c", c=256)

        PS = st.enter_context(nc.psum_tensor("PS", [128, 8 * 512], F32))
        print("SBUF bytes remaining per partition:", nc.sbuf_bytes_remaining)

        def bank(b, n=512, off=0):
            return PS[:, b * 512 + off: b * 512 + off + n]

        T, A, V, G, SP = nc.tensor, nc.scalar, nc.vector, nc.gpsimd, nc.sync
        BK = lambda b: ('bank', b)

        rot = {'i': 0}

        def gen_bank():
            b = 4 + (rot['i'] % 3)
            rot['i'] += 1
            return b

        S.op('sp', lambda: SP.dma_start(out=ident[:, :], in_=dr["ident"]), writes=['ident'], dma='ident')
        S.op('sp', lambda: SP.dma_start(out=valid_sb[:, :], in_=dr["valid"]), writes=['valid'], dma='valid')
        S.op('sp', lambda: SP.dma_start(out=vrow_sb[:, :, :], in_=dr["vrow"]), writes=['vrow'], dma='vrow')
        S.op('dve', lambda: V.tensor_copy(out=ident_bf[:, :], in_=ident[:, :]), reads=['ident'], writes=['ident_bf'])
        S.op('pool', lambda: G.memset(small[:, 16:17], EPS), writes=['epsc'])

        WCH = ((0, 2048), (2048, 4096), (4096, 5120))

        def wkey(col):
            return ('win', 0 if col < 2048 else (1 if col < 4096 else 2))

        def load_weights(l):
            l = layers.index(l)
            for c, (c0, c1) in enumerate(WCH):
                S.op('pool', lambda l=l, c0=c0, c1=c1: G.dma_start(
                    out=w_in_sb[:, :, c0:c1],
                    in_=dr["w_in"][l, :, c0:c1].rearrange("(k p) n -> p k n", p=128)),
                    writes=[('win', c)], dma=('win', c))

        def load_w_out(l):
            l = layers.index(l)
            S.op('pool', lambda l=l: G.dma_start(
                out=w_out_sb[:, :, :], in_=dr["w_out"][l, :, :].rearrange("(k p) n -> p k n", p=128)),
                writes=[('wout', 0), ('wout', 1)], dma=('wout', 0))

        import os as _os2
        _PROBE2 = _os2.environ.get('KPROBE2', '')

        def load_wkv(l):
            l = layers.index(l)
            S.op('pool', lambda l=l: G.dma_start(
                out=wkv[:, :, :], in_=dr["w_mem_kv"][l, :, :].rearrange("(k p) n -> p k n", p=128)),
                writes=[('wkv', 0), ('wkv', 1)], dma=('wkv', 0))

        def kv_prologue(l):
            l = layers.index(l)
            S.op('sp', lambda: SP.dma_start(out=memx, in_=dr["mem"].rearrange("(j p) d -> p j d", p=128)),
                 writes=['memx'], dma='memx')
            if _PROBE2 == 'c1':
                return
            for j in range(2):
                for half in range(2):
                    b = gen_bank()
                    for q in range(4):
                        kc = half * 4 + q
                        S.op('pe', lambda b=b, q=q, kc=kc, j=j: T.transpose(
                            out=bank(b, 128, q * 128), in_=memx[:, j, kc * 128:(kc + 1) * 128], identity=ident[:, :]),
                            reads=['memx', 'ident'], writes=[BK(b)])
                    S.op('act', lambda b=b, half=half, j=j: A.copy(
                        out=memT[:, half * 4:half * 4 + 4, j * 128:(j + 1) * 128],
                        in_=bank(b).rearrange("p (c t) -> p c t", c=4)),
                        writes=[BK(b), ('memT', j, half)])
            if _PROBE2 == 'c2':
                return
            memT_keys = [('memT', j, half) for j in range(2) for half in range(2)]
            for h in range(4):
                b = gen_bank()
                for kc in range(8):
                    S.op('pe', lambda b=b, h=h, kc=kc: T.matmul(
                        bank(b, 256), lhsT=wkv[:, kc, h * 128:(h + 1) * 128], rhs=memT[:, kc, :],
                        start=(kc == 0), stop=(kc == 7)),
                        reads=[('wkv', 0)] + memT_keys, writes=[BK(b)])
                S.op('act', lambda b=b, h=h: A.copy(out=kmemT[:, h, :], in_=bank(b, 256)),
                     writes=[BK(b), 'kmemT'])
            if _PROBE2 == 'c3':
                return
            for mb in range(2):
                b = gen_bank()
                for kc in range(8):
                    S.op('pe', lambda b=b, mb=mb, kc=kc: T.matmul(
                        bank(b), lhsT=memT[:, kc, mb * 128:(mb + 1) * 128], rhs=wkv[:, kc, 512:1024],
                        start=(kc == 0), stop=(kc == 7)),
                        reads=[('wkv', 1)] + memT_keys, writes=[BK(b)])
                S.op('dve', lambda b=b, mb=mb: V.tensor_copy(
                    out=vmem[:, mb, :, 0:128], in_=bank(b).rearrange("p (h d) -> p h d", h=4)),
                    writes=[BK(b), 'vmem'])
                S.op('pool', lambda mb=mb: G.memset(vmem[:, mb, :, 128:129], 1.0), writes=['vmem'])

        import os as _os
        PROBE = _os.environ.get("KPROBE", "")
        if PROBE != "a":
            load_wkv(layers[0])
            load_weights(layers[0])
            load_w_out(layers[0])

        for li, l in enumerate(layers if PROBE not in ("a", "b") else []):
            attn = (l % 2 == 0)
            la = l // 2
            src = dr["xin"] if li == 0 else scratch[l]
            dst = xout if l == last else scratch[l + 1]

            def dst_rows(t, l=l):
                if l == DEPTH - 1:
                    return (t - HALO) * 128
                return t * 128

            kv_prologue(l)
            S.fence()
            if PROBE == "c":
                break

            S.op('sp', lambda l=li: SP.dma_start(out=lng[:, :], in_=dr["ln_g"][l:l + 1, :].to_broadcast([128, D])),
                 writes=['lng'], dma='lng')
            S.op('sp', lambda l=li: SP.dma_start(out=lnb[:, :], in_=dr["ln_b"][l:l + 1, :].to_broadcast([128, D])),
                 writes=['lnb'], dma='lnb')
            if attn:
                S.op('sp', lambda la=la: SP.dma_start(out=chb_sb[:, :], in_=dr["chb"][la]), writes=['chb'], dma='chb')
                S.op('dve', lambda: V.tensor_scalar(out=small[:, 32:48], in0=chb_sb[:, :], scalar1=-1.0,
                                                    scalar2=None, op0=ALU.mult),
                     reads=['chb'], writes=['negchb'])
                for g in range(4):
                    S.op('sp', lambda la=la, g=g: SP.dma_start(out=btst, in_=dr["bt"][la, :, 4 * g:4 * g + 4, :]),
                         writes=[('tb', 0)], dma='btst')
                    for hh in range(4):
                        h = 4 * g + hh
                        S.op('act', lambda h=h, hh=hh: A.activation(
                            out=Ep[:, h, :], in_=btst[:, hh, :], func=AF.Identity, bias=small[:, 32 + h:33 + h], scale=1.0),
                            reads=[('tb', 0), 'negchb'], writes=['Ep'])
                S.op('pool', lambda: G.memset(Ep[64:128, :, 128:192], -30000.0), writes=['Ep'])
                S.op('pool', lambda: G.memset(mask0[:, :], 0.0), writes=['mask0'])
                S.op('pool', lambda: G.memset(mask0[0:64, 64:128], -30000.0), writes=['mask0'])
            else:
                S.op('sp', lambda la=la: SP.dma_start(out=cw_sb[:, :, :], in_=dr["convw"][la]), writes=['cw'], dma='cw')

            blocks = layer_plan(l)
            if max_blocks is not None:
                blocks = blocks[:max_blocks]
            nb = len(blocks)
            st_l = {'cu_prev': None, 'xloaded': set()}
            plan_tiles = [t_ for tl_, _m in blocks for t_ in tl_]

            def phase1(bi, l=l, attn=attn, blocks=blocks, src=src, st_l=st_l):
                tiles, mode = blocks[bi]
                n = len(tiles)
                N = 128 * n
                pb = bi % 2
                items = []

                def xunit(ti, t):
                    slot = t % NXS
                    if t not in st_l['xloaded']:
                        st_l['xloaded'].add(t)
                        S.op('sp', lambda slot=slot, t=t: SP.dma_start(out=xs[:, slot, :], in_=src[t * 128:(t + 1) * 128, :]),
                             reads=[('xd', l, t)], writes=[('xs', slot)], dma=('xs', slot))
                    for half in range(2):
                        b = gen_bank()
                        for q in range(4):
                            kc = half * 4 + q
                            S.op('pe', lambda b=b, q=q, kc=kc, slot=slot: T.transpose(
                                out=bank(b, 128, q * 128), in_=xs[:, slot, kc * 128:(kc + 1) * 128], identity=ident[:, :]),
                                reads=[('xs', slot), 'ident'], writes=[BK(b)])
                        if half == 0:
                            S.op('act', lambda b=b, half=half, ti=ti: A.copy(
                                out=xT[:, half * 4:half * 4 + 4, ti * 128:(ti + 1) * 128],
                                in_=bank(b).rearrange("p (c t) -> p c t", c=4)),
                                writes=[BK(b), ('xT', ti, half)])
                        else:
                            S.op('dve', lambda b=b, half=half, ti=ti: V.tensor_copy(
                                out=xT[:, half * 4:half * 4 + 4, ti * 128:(ti + 1) * 128],
                                in_=bank(b).rearrange("p (c t) -> p c t", c=4)),
                                writes=[BK(b), ('xT', ti, half)])
                for ti_, t_ in enumerate(tiles):
                    items.append((xunit, (ti_, t_)))
                xTk = [('xT', ti, half) for ti in range(n) for half in range(2)]

                def tokmajor(ti, col0, evac):
                    b = gen_bank()
                    c = col0 // 512
                    for kc in range(8):
                        S.op('pe', lambda b=b, kc=kc, ti=ti, col0=col0: T.matmul(
                            bank(b), lhsT=xT[:, kc, ti * 128:(ti + 1) * 128], rhs=w_in_sb[:, kc, col0:col0 + 512],
                            start=(kc == 0), stop=(kc == 7)),
                            reads=[('xT', ti, 0), ('xT', ti, 1), wkey(col0)], writes=[BK(b)])
                    evac(b)

                def featmajor(col0, evac):
                    b = gen_bank()
                    c = col0 // 512
                    for kc in range(8):
                        S.op('pe', lambda b=b, kc=kc, col0=col0: T.matmul(
                            bank(b, N), lhsT=w_in_sb[:, kc, col0:col0 + 128], rhs=xT[:, kc, 0:N],
                            start=(kc == 0), stop=(kc == 7)),
                            reads=xTk + [wkey(col0)], writes=[BK(b)])
                    evac(b)

                if attn:
                    for ti, t in enumerate(tiles):
                        vs = t % 8
                        for hb in range(2):
                            def ev(b, t=t, vs=vs, hb=hb):
                                S.op('dve', lambda: V.tensor_scalar(
                                    out=Vr[:, vs, hb * 8:(hb + 1) * 8, 0:64],
                                    in0=bank(b).rearrange("p (h d) -> p h d", h=8),
                                    scalar1=valid_sb[:, t:t + 1], scalar2=None, op0=ALU.mult),
                                    reads=['valid'], writes=[BK(b), ('V', vs)])
                            items.append((tokmajor, (ti, 2048 + hb * 512, ev)))
                        S.op('pool', lambda vs=vs, t=t: G.tensor_copy(
                            out=Vr[:, vs, :, 64:65], in_=valid_sb[:, t:t + 1].unsqueeze(1).to_broadcast([128, 16, 1])),
                            reads=['valid'], writes=[('V', vs)])
                    for j in range(8):
                        def ev(b, j=j):
                            for ti, t in enumerate(tiles):
                                ks = t % 8
                                if (j + ti) % 2 == 0:
                                    S.op('act', lambda ti=ti, ks=ks: A.copy(
                                        out=KTr[:, j, ks * 128:(ks + 1) * 128], in_=bank(b, 128, ti * 128)),
                                        writes=[BK(b), ('K', ks)])
                                else:
                                    S.op('dve', lambda ti=ti, ks=ks: V.tensor_copy(
                                        out=KTr[:, j, ks * 128:(ks + 1) * 128], in_=bank(b, 128, ti * 128)),
                                        writes=[BK(b), ('K', ks)])
                        items.append((featmajor, (1024 + j * 128, ev)))
                    if mode == 'full':
                        for j in range(8):
                            def ev(b, j=j):
                                S.op('act', lambda: A.mul(out=QT[:, pb, j, 0:N], in_=bank(b, N), mul=0.125),
                                     writes=[BK(b), ('QT', pb)])
                            items.append((featmajor, (j * 128, ev)))
                else:
                    cb = bi % 2
                    prev = st_l['cu_prev']
                    if prev is not None:
                        pcb, pN, pt = prev
                        S.op('pool', lambda cb=cb, pcb=pcb, pN=pN, pt=pt: G.tensor_tensor(
                            out=cu[:, cb, :, 0:2], in0=cu[:, pcb, :, pN:pN + 2],
                            in1=vrow_sb[:, pt:pt + 1, :].to_broadcast([128, 8, 2]), op=ALU.mult),
                            reads=[('cu', pcb), 'vrow'], writes=[('cu', cb)])
                    else:
                        S.op('pool', lambda cb=cb: G.memset(cu[:, cb, :, 0:2], 0.0), writes=[('cu', cb)])
                    for j in range(8):
                        pj = j % 2

                        def ev1(b, pj=pj):
                            S.op('act', lambda: A.copy(out=p1s[:, pj, 0:N], in_=bank(b, N)),
                                 writes=[BK(b), ('p1s', pj)])
                        items.append((featmajor, (1024 + j * 128, ev1)))

                        def ev2(b, j=j, pj=pj, cb=cb):
                            S.op('dve', lambda: V.tensor_tensor(out=cu[:, cb, j, 2:2 + N], in0=bank(b, N),
                                                                in1=p1s[:, pj, 0:N], op=ALU.mult),
                                 reads=[('p1s', pj)], writes=[BK(b), ('cu', cb)])
                        items.append((featmajor, (2048 + j * 128, ev2)))
                    st_l['cu_prev'] = (cb, N, tiles[-1])
                    if mode == 'full':
                        for j in range(8):
                            def ev(b, j=j):
                                S.op('act', lambda: A.copy(out=p0s[:, pb, j, 0:N], in_=bank(b, N)),
                                     writes=[BK(b), ('p0s', pb, j)])
                            items.append((featmajor, (j * 128, ev)))

                            def evz(b, j=j):
                                S.op('act', lambda: A.activation(out=szT[:, pb, j, 0:N], in_=bank(b, N), func=AF.Silu),
                                     writes=[BK(b), ('szT', pb, j)])
                            items.append((featmajor, (3584 + j * 128, evz)))
                if mode == 'full':
                    for j in range(4):
                        def ev(b, j=j):
                            S.op('act', lambda: A.mul(out=QmT[:, pb, j, 0:N], in_=bank(b, N), mul=1.0 / math.sqrt(128.0)),
                                 writes=[BK(b), ('QmT', pb)])
                        items.append((featmajor, (3072 + j * 128, ev)))
                    zblocks = (0, 1, 2) if attn else (2,)
                    for ti, t in enumerate(tiles):
                        for zb in zblocks:
                            def ev(b, ti=ti, zb=zb):
                                S.op('act', lambda: A.activation(out=siluz[:, pb, ti, zb * 512:(zb + 1) * 512],
                                                                 in_=bank(b), func=AF.Silu),
                                     writes=[BK(b), ('siluz', pb, ti)])
                            items.append((tokmajor, (ti, 3584 + zb * 512, ev)))
                return items

            def phase2(bi, nxt, l=l, attn=attn, blocks=blocks, dst=dst, src=src, st_l=st_l, plan_tiles=plan_tiles):
                tiles, mode = blocks[bi]
                if mode != 'full':
                    for f_, a_ in nxt:
                        f_(*a_)
                    return
                pts = {'left': (22 if attn else 6) * len(tiles)}

                def fill():
                    left = max(pts['left'], 1)
                    k = -(-len(nxt) // left)
                    for _ in range(min(k, len(nxt))):
                        f_, a_ = nxt.pop(0)
                        f_(*a_)
                    pts['left'] -= 1
                _d2 = _os.environ.get("KDBG2")
                if _d2 and int(_os.environ["KDBG"]) == tiles[0]:
                    if _d2 == 'xT':
                        S.op('sp', lambda: SP.dma_start(out=dbg2, in_=xT[:, :, :].rearrange("p a b -> p (a b)")),
                             reads=[('xT', 0, 0), ('xT', 0, 1), ('xT', 1, 0), ('xT', 1, 1)], writes=[('xd', l + 1, 1)], dma='gdbg2')
                    elif _d2 == 'w':
                        S.op('sp', lambda: SP.dma_start(out=dbg2, in_=w_in_sb[:, 0, 3072:5120]),
                             reads=[('win', 1), ('win', 2)], writes=[('xd', l + 1, 1)], dma='gdbg2')
                    elif _d2 == 'QT':
                        S.op('sp', lambda: SP.dma_start(out=dbg2.rearrange("p (a b) -> p a b", a=8), in_=QT[:, bi % 2, :, :]),
                             reads=[('QT', bi % 2)], writes=[('xd', l + 1, 1)], dma='gdbg2')
                    elif _d2 == 'K':
                        S.op('sp', lambda: SP.dma_start(out=dbg2[:, 0:1024].rearrange("p (a b) -> p a b", a=8),
                                                        in_=KTr[:, :, (tiles[0] % 8) * 128:(tiles[0] % 8) * 128 + 128]),
                             reads=[('K', tiles[0] % 8)], writes=[('xd', l + 1, 1)], dma='gdbg2')
                    elif _d2 == 'QmT':
                        S.op('sp', lambda: SP.dma_start(out=dbg2[:, 0:1024].rearrange("p (a b) -> p a b", a=4), in_=QmT[:, bi % 2, :, :]),
                             reads=[('QmT', bi % 2)], writes=[('xd', l + 1, 1)], dma='gdbg2')
                    elif _d2 == 'V':
                        S.op('sp', lambda: SP.dma_start(out=dbg2[:, 0:1040].rearrange("p (a b) -> p a b", a=16), in_=Vr[:, tiles[0] % 8, :, :]),
                             reads=[('V', tiles[0] % 8)], writes=[('xd', l + 1, 1)], dma='gdbg2')
                    elif _d2 == 'sz':
                        S.op('sp', lambda: SP.dma_start(out=dbg2[:, 0:1536], in_=siluz[:, bi % 2, 0, :]),
                             reads=[('siluz', bi % 2, 0)], writes=[('xd', l + 1, 1)], dma='gdbg2')
                n = len(tiles)
                N = 128 * n
                pb = bi % 2
                if not attn:
                    cb = bi % 2
                    for j in range(8):
                        aj = j % 2
                        S.op('dve', lambda j=j, aj=aj: V.tensor_scalar(
                            out=cacc[:, aj, 0:N], in0=cu[:, cb, j, 0:N], scalar1=cw_sb[:, j, 0:1], scalar2=None,
                            op0=ALU.mult), reads=[('cu', cb), 'cw'], writes=[('cacc', aj)])
                        for k in (1, 2):
                            S.op('dve', lambda j=j, aj=aj, k=k: V.scalar_tensor_tensor(
                                out=cacc[:, aj, 0:N], in0=cu[:, cb, j, k:k + N], scalar=cw_sb[:, j, k:k + 1],
                                in1=cacc[:, aj, 0:N], op0=ALU.mult, op1=ALU.add),
                                reads=[('cu', cb), 'cw'], writes=[('cacc', aj)])
                        S.op('pool', lambda j=j, aj=aj: G.tensor_tensor(
                            out=cacc[:, aj, 0:N], in0=cacc[:, aj, 0:N], in1=p0s[:, pb, j, 0:N], op=ALU.mult),
                            reads=[('p0s', pb, j)], writes=[('cacc', aj)])
                        S.op('pool', lambda j=j, aj=aj: G.tensor_tensor(
                            out=szT[:, pb, j, 0:N], in0=cacc[:, aj, 0:N], in1=szT[:, pb, j, 0:N], op=ALU.mult),
                            reads=[('cacc', aj)], writes=[('szT', pb, j)])
                def tile_body(ti, t):
                    xslot = t % NXS
                    if attn:
                        def scores(h):
                            sbuf = h % 2
                            hp = h % 2
                            ch = h // 2
                            for j in range(5):
                                ks = (t - 4 + j) % 8
                                hasb = j in (0, 3, 4)
                                S.op('pe', lambda sbuf=sbuf, j=j, ks=ks, hp=hp, ch=ch, hasb=hasb: T.matmul(
                                    PS[:, sbuf * 1024 + j * 128: sbuf * 1024 + (j + 1) * 128],
                                    lhsT=KTr[hp * 64:(hp + 1) * 64, ch, ks * 128:(ks + 1) * 128],
                                    rhs=QT[hp * 64:(hp + 1) * 64, pb, ch, ti * 128:(ti + 1) * 128],
                                    start=True, stop=(not hasb)),
                                    reads=[('K', ks), ('QT', pb)], writes=[BK(2 * sbuf), BK(2 * sbuf + 1)])
                                if hasb:
                                    if j == 0:
                                        brhs = mask0[:, :]
                                        bk = 'mask0'
                                    else:
                                        brhs = Ep[:, h, (j - 3) * 128:(j - 2) * 128]
                                        bk = 'Ep'
                                    S.op('pe', lambda sbuf=sbuf, j=j, brhs=brhs: T.matmul(
                                        PS[:, sbuf * 1024 + j * 128: sbuf * 1024 + (j + 1) * 128],
                                        lhsT=ident_bf[:, :], rhs=brhs, start=False, stop=True),
                                        reads=[bk, 'ident_bf'], writes=[BK(2 * sbuf), BK(2 * sbuf + 1)])
                            pbuf = h % 2
                            S.op('act', lambda sbuf=sbuf, pbuf=pbuf: A.activation(
                                out=PT[:, pbuf, :], in_=PS[:, sbuf * 1024: sbuf * 1024 + 640], func=AF.Exp),
                                writes=[BK(2 * sbuf), BK(2 * sbuf + 1), ('PT', pbuf)])

                        pvb = {'b': None}

                        def pv(h):
                            hg = h % 4
                            if hg == 0:
                                pvb['b'] = 7
                            b = pvb['b']
                            pbuf = h % 2
                            for j in range(5):
                                vs = (t - 4 + j) % 8
                                S.op('pe', lambda b=b, hg=hg, j=j, vs=vs, pbuf=pbuf, h=h: T.matmul(
                                    bank(b, 65, hg * 65), lhsT=PT[:, pbuf, j * 128:(j + 1) * 128],
                                    rhs=Vr[:, vs, h, :], start=(j == 0), stop=(j == 4)),
                                    reads=[('PT', pbuf), ('V', vs)], writes=[BK(b)])
                            if hg == 3:
                                h0 = h - 3
                                gb = (h // 4) % 2
                                pv4 = bank(b, 260).rearrange("p (h d) -> p h d", h=4)
                                S.op('dve', lambda pv4=pv4: V.tensor_scalar(
                                    out=small[:, 0:4].unsqueeze(2), in0=pv4[:, :, 64:65], scalar1=1e-30, scalar2=None,
                                    op0=ALU.add), writes=[BK(b), 'rs'])
                                S.op('dve', lambda: V.reciprocal(out=small[:, 0:4], in_=small[:, 0:4]), writes=['rs'])
                                S.op('dve', lambda pv4=pv4, gb=gb: V.tensor_tensor(
                                    out=gtmp[:, gb, :].rearrange("p (h d) -> p h d", h=4), in0=pv4[:, :, 0:64],
                                    in1=small[:, 0:4].unsqueeze(2).to_broadcast([128, 4, 64]), op=ALU.mult),
                                    reads=['rs'], writes=[BK(b), ('gtmp', gb)])
                                S.op('pool', lambda gb=gb, h0=h0: G.tensor_tensor(
                                    out=siluz[:, pb, ti, h0 * 64:(h0 + 4) * 64], in0=gtmp[:, gb, :],
                                    in1=siluz[:, pb, ti, h0 * 64:(h0 + 4) * 64], op=ALU.mult),
                                    reads=[('gtmp', gb)], writes=[('siluz', pb, ti)])

                        scores(0)
                        for h in range(16):
                            if h + 1 < 16:
                                scores(h + 1)
                            fill()
                            pv(h)
                    mbank = {}

                    def mscore(hm):
                        b = gen_bank()
                        mbank[hm] = b
                        for mb in range(2):
                            S.op('pe', lambda b=b, mb=mb, hm=hm: T.matmul(
                                bank(b, 128, mb * 128), lhsT=kmemT[:, hm, mb * 128:(mb + 1) * 128],
                                rhs=QmT[:, pb, hm, ti * 128:(ti + 1) * 128], start=True, stop=True),
                                reads=['kmemT', ('QmT', pb)], writes=[BK(b)])
                        mbuf = hm % 2
                        S.op('act', lambda b=b, mbuf=mbuf: A.activation(out=PTm[:, mbuf, :], in_=bank(b, 256), func=AF.Exp),
                             writes=[BK(b), ('PTm', mbuf)])

                    def mpv(hm):
                        b = mbank[hm]
                        mbuf = hm % 2
                        for mb in range(2):
                            S.op('pe', lambda b=b, mb=mb, hm=hm, mbuf=mbuf: T.matmul(
                                bank(b, 129, 256), lhsT=PTm[:, mbuf, mb * 128:(mb + 1) * 128],
                                rhs=vmem[:, mb, hm, :], start=(mb == 0), stop=(mb == 1)),
                                reads=[('PTm', mbuf), 'vmem'], writes=[BK(b)])
                        S.op('dve', lambda b=b: V.tensor_scalar(
                            out=small[:, 8:9], in0=bank(b, 1, 256 + 128), scalar1=1e-30, scalar2=None, op0=ALU.add),
                            writes=[BK(b), 'rsm'])
                        S.op('dve', lambda: V.reciprocal(out=small[:, 8:9], in_=small[:, 8:9]), writes=['rsm'])
                        S.op('dve', lambda b=b, hm=hm: V.scalar_tensor_tensor(
                            out=siluz[:, pb, ti, 1024 + hm * 128:1024 + (hm + 1) * 128], in0=bank(b, 128, 256),
                            scalar=small[:, 8:9], in1=siluz[:, pb, ti, 1024 + hm * 128:1024 + (hm + 1) * 128],
                            op0=ALU.mult, op1=ALU.mult),
                            reads=['rsm'], writes=[BK(b), ('siluz', pb, ti)])

                    mscore(0)
                    for hm in range(4):
                        if hm + 1 < 4:
                            mscore(hm + 1)
                        mpv(hm)
                        fill()
                    if _os.environ.get("KDBG") and int(_os.environ["KDBG"]) == t:
                        S.op('sp', lambda: SP.dma_start(out=dbg_out, in_=siluz[:, pb, ti, :]),
                             reads=[('siluz', pb, ti)], writes=[('xd', l + 1, 0)], dma='gdbg')
                    if _os.environ.get("KDBG2") == 'PT' and int(_os.environ["KDBG"]) == t:
                        S.op('sp', lambda: SP.dma_start(out=dbg2[:, 0:1280].rearrange("p (a b) -> p a b", a=2), in_=PT[:, :, :]),
                             reads=[('PT', 0), ('PT', 1)], writes=[('xd', l + 1, 1)], dma='gdbg2')
                        S.op('sp', lambda: SP.dma_start(out=dbg2[:, 1280:1792].rearrange("p (a b) -> p a b", a=2), in_=PTm[:, :, :]),
                             reads=[('PTm', 0), ('PTm', 1)], writes=[('xd', l + 1, 2)], dma='gdbg3')
                    chunks = list(range(12)) if attn else list(range(8, 12))
                    for g0 in range(0, len(chunks), 4):
                        grp = chunks[g0:g0 + 4]
                        b = gen_bank()
                        psb = bank(b).bitcast(BF16)
                        for q, j in enumerate(grp):
                            S.op('pe', lambda q=q, j=j, psb=psb: T.transpose(
                                out=psb[:, q * 128:(q + 1) * 128], in_=siluz[:, pb, ti, j * 128:(j + 1) * 128],
                                identity=ident_bf[:, :]),
                                reads=[('siluz', pb, ti), 'ident_bf'], writes=[BK(b)])
                        j0 = grp[0]
                        S.op('dve', lambda psb=psb, j0=j0: V.tensor_copy(
                            out=yT[:, j0:j0 + 4, :], in_=psb[:, 0:512].rearrange("p (c t) -> p c t", c=4)),
                            writes=[BK(b), ('yT', j0 // 4)])
                    fill()
                    tslot = 0
                    for half in range(2):
                        b = gen_bank()
                        for ec in range(12):
                            if attn or ec >= 8:
                                lh = yT[:, ec, :]
                                rk = [('yT', ec // 4)]
                            else:
                                lh = szT[:, pb, ec, ti * 128:(ti + 1) * 128]
                                rk = [('szT', pb, ec)]
                            S.op('pe', lambda b=b, ec=ec, half=half, lh=lh: T.matmul(
                                bank(b), lhsT=lh, rhs=w_out_sb[:, ec, half * 512:(half + 1) * 512],
                                start=(ec == 0), stop=(ec == 11)),
                                reads=rk + [('wout', half)], writes=[BK(b)])
                        S.op('dve', lambda b=b, half=half, xslot=xslot, tslot=tslot: V.scalar_tensor_tensor(
                            out=tb[:, tslot, half * 512:(half + 1) * 512], in0=xs[:, xslot, half * 512:(half + 1) * 512],
                            scalar=ALPHA, in1=bank(b), op0=ALU.mult, op1=ALU.add),
                            reads=[('xs', xslot)], writes=[BK(b), ('tb', tslot)])
                        S.op('dve', lambda half=half, tslot=tslot: V.bn_stats(
                            out=stats[:, tslot, half, :], in_=tb[:, tslot, half * 512:(half + 1) * 512]),
                            reads=[('tb', tslot)], writes=[('stats', tslot)])
                    tn = t + NXS
                    if tn in plan_tiles and tn not in st_l['xloaded']:
                        st_l['xloaded'].add(tn)
                        S.op('sp', lambda xslot=xslot, tn=tn: SP.dma_start(out=xs[:, xslot, :], in_=src[tn * 128:(tn + 1) * 128, :]),
                             reads=[('xd', l, tn)], writes=[('xs', xslot)], dma=('xs', xslot))
                    S.op('dve', lambda tslot=tslot: V.bn_aggr(
                        out=mv[:, tslot, 0:2], in_=stats[:, tslot, :, :].rearrange("p a b -> p (a b)")),
                        reads=[('stats', tslot)], writes=[('mv', tslot)])
                    S.op('act', lambda tslot=tslot: A.activation(
                        out=small[:, 20:21], in_=mv[:, tslot, 1:2], func=AF.Ln, bias=small[:, 16:17], scale=1.0),
                        reads=[('mv', tslot), 'epsc'], writes=[('mvb', tslot)])
                    S.op('act', lambda tslot=tslot: A.activation(
                        out=mv[:, tslot, 2:3], in_=small[:, 20:21], func=AF.Exp, scale=-0.5),
                        reads=[('mvb', tslot)], writes=[('mvc', tslot)])
                    S.op('dve', lambda tslot=tslot: V.scalar_tensor_tensor(
                        out=mv[:, tslot, 3:4], in0=mv[:, tslot, 0:1], scalar=-1.0, in1=mv[:, tslot, 2:3],
                        op0=ALU.mult, op1=ALU.mult), reads=[('mv', tslot), ('mvc', tslot)], writes=[('mvd', tslot)])
                    S.op('act', lambda tslot=tslot: A.activation(
                        out=tb[:, tslot, :], in_=tb[:, tslot, :], func=AF.Identity,
                        bias=mv[:, tslot, 3:4], scale=mv[:, tslot, 2:3]),
                        reads=[('mvd', tslot), ('mvc', tslot)], writes=[('tb', tslot)])
                    S.op('pool', lambda tslot=tslot: G.tensor_tensor(
                        out=tb[:, tslot, :], in0=tb[:, tslot, :], in1=lng[:, :], op=ALU.mult),
                        reads=['lng'], writes=[('tb', tslot)])
                    S.op('pool', lambda tslot=tslot: G.tensor_tensor(
                        out=tb[:, tslot, :], in0=tb[:, tslot, :], in1=lnb[:, :], op=ALU.add),
                        reads=['lnb'], writes=[('tb', tslot)])
                    if l == DEPTH - 1 and t < HALO:
                        return
                    r0 = dst_rows(t)
                    S.op('sp', lambda tslot=tslot, r0=r0: SP.dma_start(out=dst[r0:r0 + 128, :], in_=tb[:, tslot, :]),
                         reads=[('tb', tslot)], writes=[('xd', l + 1, t)], dma=('tbo', tslot))

                for ti_, t_ in enumerate(tiles):
                    tile_body(ti_, t_)
                    fill()
                for f_, a_ in nxt:
                    f_(*a_)

            if nb > 0:
                for f_, a_ in phase1(0):
                    f_(*a_)
            for bi in range(nb):
                nxt = phase1(bi + 1) if bi + 1 < nb else []
                if bi + 1 == nb and li + 1 < len(layers):
                    load_weights(layers[li + 1])
                phase2(bi, nxt)
            S.fence()
            if li + 1 < len(layers):
                load_wkv(layers[li + 1])
                load_w_out(layers[li + 1])

        outkeys = [('xd', last + 1, t) for t in range(NT)]
        S.op('sp', None, reads=outkeys)
        S.finalize(st)
    return nc, S


def _host_layout(inputs):
    x = np.asarray(inputs["x"], dtype=np.float32)
    mem = np.asarray(inputs["mem"], dtype=np.float32)
    rel_bias = np.asarray(inputs["rel_bias"], dtype=np.float32)
    conv_w = np.asarray(inputs["conv_w"], dtype=np.float32)
    k = np.arange(128)[:, None]
    q = np.arange(128)[None, :]
    idx3 = np.minimum(q - k + 128, 128) + 128
    idx4 = (q - k) + 128
    idx = np.concatenate([idx3, idx4], axis=1)
    bt = np.ascontiguousarray(np.transpose(rel_bias[:, :, idx], (0, 2, 1, 3)))
    chb = np.ascontiguousarray(np.broadcast_to(rel_bias[:, None, :, 256], (2, 128, 16)))
    convw = np.ascontiguousarray(np.transpose(conv_w.reshape(2, 3, 8, 128), (0, 3, 2, 1)))
    common = dict(
        ident=np.eye(128, dtype=np.float32),
        w_in=np.asarray(inputs["w_in"], dtype=np.float32),
        w_mem_kv=np.asarray(inputs["w_mem_kv"], dtype=np.float32),
        w_out=np.asarray(inputs["w_out"], dtype=np.float32),
        bt=bt, chb=chb, convw=convw,
        ln_g=np.asarray(inputs["ln_g"], dtype=np.float32),
        ln_b=np.asarray(inputs["ln_b"], dtype=np.float32),
    )
    per_core = []
    for c in range(8):
        b, half = c // 2, c % 2
        s0 = half * 4096
        w0 = s0 - HALO * 128
        xw = np.zeros((NT * 128, D), np.float32)
        lo = max(w0, 0)
        xw[lo - w0:] = x[b, lo:s0 + 4096]
        vt = np.zeros((NT * 128,), np.float32)
        vt[lo - w0:] = 1.0
        valid = np.ascontiguousarray(vt.reshape(NT, 128).T)
        vrow = np.ascontiguousarray(np.broadcast_to(vt.reshape(NT, 128)[None, :, 126:128], (128, NT, 2)))
        m = dict(common)
        m.update(xin=xw, valid=valid, vrow=vrow, mem=np.ascontiguousarray(mem[b]))
        per_core.append(m)
    return per_core


_CACHE = {}


def _get_program(layers):
    key = tuple(layers)
    if key not in _CACHE:
        _CACHE[key] = build_program(list(layers))[0]
    return _CACHE[key]


FUSED = True


def kernel(**inputs):
    per_core = _host_layout(inputs)
    groups = [[0, 1, 2, 3]] if FUSED else [[0], [1], [2], [3]]
    full = {k: per_core[0][k] for k in ("w_in", "w_mem_kv", "w_out", "ln_g", "ln_b")}
    for grp in groups:
        nc = _get_program(grp)
        sl = {k: np.ascontiguousarray(v[grp[0]:grp[-1] + 1]) for k, v in full.items()}
        maps = []
        for c in range(8):
            m = dict(per_core[c])
            m.update(sl)
            maps.append(m)
        res = run_bass_kernel_spmd(nc, maps, core_ids=list(range(8)))
        outs = [r["xout"] for r in res.results]
        if grp[-1] != DEPTH - 1:
            for c in range(8):
                per_core[c]["xin"] = outs[c]
    out = np.zeros((4, 8192, D), np.float32)
    for c in range(8):
        b, half = c // 2, c % 2
        out[b, half * 4096:(half + 1) * 4096] = outs[c]
    return out
```

```python
import math
from contextlib import ExitStack

import numpy as np
import concourse.bass as bass
import concourse.mybir as mybir
from concourse.bass_utils import run_bass_kernel_spmd

F32 = mybir.dt.float32
BF16 = mybir.dt.bfloat16
AF = mybir.ActivationFunctionType
ALU = mybir.AluOpType

D = 1024
NIN = 5120
NT = 42
HALO = 10
NMAIN = 32
DEPTH = 4
ALPHA = (2.0 * DEPTH) ** 0.25
EPS = 1e-5
NXS = 4
BIAS_ON_PE = False

ENG_ATTR = {'pe': 'tensor', 'act': 'scalar', 'dve': 'vector', 'pool': 'gpsimd', 'sp': 'sync'}


class Op:
    __slots__ = ('eng', 'fn', 'reads', 'writes', 'dma', 'waits', 'signal', 'ev', 'vc', 'sigcount')

    def __init__(self, eng, fn, reads, writes, dma):
        self.eng = eng
        self.fn = fn
        self.reads = reads
        self.writes = writes
        self.dma = dma
        self.waits = []
        self.signal = False
        self.ev = None
        self.vc = None
        self.sigcount = 0


class Sched:
    def __init__(self, nc, same_engine_sync=('act', 'dve', 'pool')):
        self.nc = nc
        self.ops = []
        self.same_sync = set(same_engine_sync)

    def op(self, eng, fn, reads=(), writes=(), dma=None):
        self.ops.append(Op(eng, fn, tuple(reads), tuple(writes), dma))

    def fence(self):
        shared = {}
        for e in ENG_ATTR:
            self.ops.append(Op(e, None, ('__fence__', shared), (), None))

    def finalize(self, stack):
        nc = self.nc
        state = {}
        vcs = {e: {} for e in ENG_ATTR}
        eng_count = {e: 0 for e in ENG_ATTR}
        dma_count = {}
        evop = {}
        dmavc = {}
        for op in self.ops:
            e = op.eng
            deps = {}
            if len(op.reads) == 2 and op.reads[0] == '__fence__':
                shared = op.reads[1]
                if not shared:
                    for e2, c2 in eng_count.items():
                        if c2 > 0:
                            shared[('e', e2)] = c2 - 1
                    for dk2, c2 in dma_count.items():
                        shared[dk2] = c2
                deps.update(shared)
                op.reads = ()
            for r in op.reads:
                st = state.get(r)
                if st:
                    for (k, i) in st[0]:
                        if deps.get(k, -1) < i:
                            deps[k] = i
            for w in op.writes:
                st = state.get(w)
                if st:
                    for (k, i) in st[0]:
                        if deps.get(k, -1) < i:
                            deps[k] = i
                    for (k, i) in st[1]:
                        if deps.get(k, -1) < i:
                            deps[k] = i
            vc = vcs[e]
            myk = ('e', e)
            for k, i in deps.items():
                if k == myk and e not in self.same_sync:
                    continue
                if vc.get(k, -1) >= i:
                    continue
                op.waits.append((k, i))
            for (k, i) in op.waits:
                if k[0] == 'e':
                    src = evop[(k, i)]
                    src.signal = True
                    svc = src.vc
                else:
                    svc = dmavc[(k, i)]
                for kk, ii in svc.items():
                    if vc.get(kk, -1) < ii:
                        vc[kk] = ii
                if vc.get(k, -1) < i:
                    vc[k] = i
            if op.fn is None:
                op.ev = None
                continue
            if op.dma is None:
                idx = eng_count[e]
                eng_count[e] = idx + 1
                op.ev = (myk, idx)
                evop[op.ev] = op
                snap = dict(vc)
                snap[myk] = idx
                op.vc = snap
                if e not in self.same_sync:
                    vc[myk] = idx
            else:
                dk = ('d', op.dma)
                c = dma_count.get(dk, 0) + 1
                dma_count[dk] = c
                op.ev = (dk, c)
                dmavc[op.ev] = dict(vc)
            for r in op.reads:
                st = state.setdefault(r, [[], []])
                rl = [x for x in st[1] if x[0] != op.ev[0]]
                rl.append(op.ev)
                st[1] = rl
            for w in op.writes:
                state[w] = [[op.ev], []]
        sems = {}
        for e in ENG_ATTR:
            sems[('e', e)] = stack.enter_context(nc.semaphore('sem_' + e))
        for dk in dma_count:
            sems[dk] = stack.enter_context(nc.semaphore('dsem_%d' % len(sems)))
        cnt = {e: 0 for e in ENG_ATTR}
        for op in self.ops:
            if op.dma is None and op.signal:
                cnt[op.eng] += 1
                op.sigcount = cnt[op.eng]
        for op in self.ops:
            engobj = getattr(nc, ENG_ATTR[op.eng])
            for (k, i) in op.waits:
                val = evop[(k, i)].sigcount if k[0] == 'e' else 16 * i
                engobj.wait_ge(sems[k], val)
            if op.fn is None:
                continue
            ins = op.fn()
            if op.dma is not None:
                ins.then_inc(sems[op.ev[0]], 16)
            elif op.signal:
                ins.then_inc(sems[('e', op.eng)], 1)
        self.stats = dict(nops=len(self.ops), sig=cnt, nsems=len(sems))


def layer_plan(l):
    if l == 0:
        blocks = [((0, 1), 'kv'), ((2, 3), 'kv')] + [((t, t + 1), 'full') for t in range(4, NT, 2)]
    elif l == 1:
        blocks = [((4,), 'cu'), ((5,), 'full')] + [((t, t + 1), 'full') for t in range(6, NT, 2)]
    elif l == 2:
        blocks = [((5, 6), 'kv'), ((7, 8), 'kv'), ((9,), 'full')] + \
                 [((t, t + 1), 'full') for t in range(10, NT, 2)]
    else:
        blocks = [((9,), 'cu')] + [((t, t + 1), 'full') for t in range(10, NT, 2)]
    return blocks


def build_program(layers, same_engine_sync=('act', 'dve', 'pool'), max_blocks=None):
    nc = bass.Bass("TRN2", target_bir_lowering=False, dynamic_dma_scratch_size=8192)
    last = layers[-1]
    final = (last == DEPTH - 1)
    dr = {}

    def din(name, shape):
        dr[name] = nc.dram_tensor(name, list(shape), F32, kind="ExternalInput").ap()

    din("xin", (NT * 128, D))
    din("valid", (128, NT))
    din("vrow", (128, NT, 2))
    din("mem", (256, D))
    din("ident", (128, 128))
    din("w_in", (len(layers), D, NIN))
    din("w_mem_kv", (len(layers), D, D))
    din("w_out", (len(layers), 1536, D))
    din("bt", (2, 128, 16, 256))
    din("chb", (2, 128, 16))
    din("convw", (2, 128, 8, 3))
    din("ln_g", (len(layers), D))
    din("ln_b", (len(layers), D))
    if final:
        xout = nc.dram_tensor("xout", [NMAIN * 128, D], F32, kind="ExternalOutput").ap()
    else:
        xout = nc.dram_tensor("xout", [NT * 128, D], F32, kind="ExternalOutput").ap()
    scratch = {}
    for l in layers[:-1]:
        scratch[l + 1] = nc.dram_tensor("xs%d" % (l + 1), [NT * 128, D], F32).ap()

    S = Sched(nc, same_engine_sync)
    with ExitStack() as st:
        def sb(name, shape, dt):
            return st.enter_context(nc.sbuf_tensor("sb_" + name, list(shape), dt))

        w_in_sb = sb("w_in_sb", (128, 8, NIN), BF16)
        w_out_sb = sb("w_out_sb", (128, 12, D), BF16)
        kmemT = sb("kmemT", (128, 4, 256), BF16)
        vmem = sb("vmem", (128, 2, 4, 129), BF16)
        valid_sb = sb("valid_sb", (128, NT), F32)
        vrow_sb = sb("vrow_sb", (128, NT, 2), F32)
        ident = sb("ident", (128, 128), F32)
        ident_bf = sb("ident_bf", (128, 128), BF16)
        lng = sb("lng", (128, D), F32)
        lnb = sb("lnb", (128, D), F32)
        xs = sb("xs", (128, NXS, D), F32)
        xT = sb("xT", (128, 8, 256), BF16)
        tb = sb("tb", (128, 1, D), F32)
        siluz = sb("siluz", (128, 2, 2, 1536), BF16)
        yT = sb("yT", (128, 12, 128), BF16)
        QmT = sb("QmT", (128, 2, 4, 256), BF16)
        PTm = sb("PTm", (128, 2, 256), BF16)
        small = sb("small", (128, 64), F32)
        stats = sb("stats", (128, 2, 2, 6), F32)
        mv = sb("mv", (128, 2, 4), F32)
        gtmp = sb("gtmp", (128, 2, 256), F32)
        chb_sb = sb("chb_sb", (128, 16), F32)
        mask0 = sb("mask0", (128, 128), BF16) if BIAS_ON_PE else None
        import os as _os
        dbg_out = nc.dram_tensor("dbg", [128, 1536], BF16, kind="ExternalOutput").ap() if _os.environ.get("KDBG") else None
        dbg2 = nc.dram_tensor("dbg2", [128, 2048], BF16, kind="ExternalOutput").ap() if _os.environ.get("KDBG2") else None
        cw_sb = sb("cw_sb", (128, 8, 3), F32)
        MIXB = 53248
        MIX = sb("MIX", (128, MIXB // 2), BF16)

        def mview(off, shape, dt):
            esz = 4 if dt == F32 else 2
            n = 1
            for d_ in shape:
                n *= d_
            v = MIX[:, off // 2: off // 2 + (n * esz) // 2]
            if dt == F32:
                v = v.bitcast(F32)
            if len(shape) == 1:
                return v
            names = "abcd"[:len(shape)]
            pat = "p (%s) -> p %s" % (" ".join(names), " ".join(names))
            kw = {names[i]: shape[i] for i in range(len(shape) - 1)}
            return v.rearrange(pat, **kw)

        QT = mview(0, (2, 8, 256), BF16)
        KTr = mview(8192, (8, 1024), BF16)
        Vr = mview(24576, (8, 16, 65), BF16)
        PT = mview(41216, (3, 640), BF16)
        Ep = mview(45056, (16, 256), BF16)
        cu = mview(0, (2, 8, 258), F32)
        p0s = mview(16512, (2, 8, 256), F32)
        p1s = mview(32896, (2, 256), F32)
        cacc = mview(34944, (2, 256), F32)
        szT = mview(36992, (2, 8, 256), BF16)
        memx = mview(0, (2, D), F32)
        memT = mview(8192, (8, 256), BF16)
        wkv = mview(12288, (8, D), BF16)
        btst = tb[:, :, :].rearrange("p a (h c) -> p (a h) c", c=256)

        PS = st.enter_context(nc.psum_tensor("PS", [128, 8 * 512], F32))
        print("SBUF bytes remaining per partition:", nc.sbuf_bytes_remaining)

        def bank(b, n=512, off=0):
            return PS[:, b * 512 + off: b * 512 + off + n]

        T, A, V, G, SP = nc.tensor, nc.scalar, nc.vector, nc.gpsimd, nc.sync
        BK = lambda b: ('bank', b)

        rot = {'i': 0}

        def gen_bank():
            b = 4 + (rot['i'] % 3)
            rot['i'] += 1
            return b

        S.op('sp', lambda: SP.dma_start(out=ident[:, :], in_=dr["ident"]), writes=['ident'], dma='ident')
        S.op('sp', lambda: SP.dma_start(out=valid_sb[:, :], in_=dr["valid"]), writes=['valid'], dma='valid')
        S.op('sp', lambda: SP.dma_start(out=vrow_sb[:, :, :], in_=dr["vrow"]), writes=['vrow'], dma='vrow')
        S.op('dve', lambda: V.tensor_copy(out=ident_bf[:, :], in_=ident[:, :]), reads=['ident'], writes=['ident_bf'])
        S.op('pool', lambda: G.memset(small[:, 16:17], EPS), writes=['epsc'])

        WCH = ((0, 2048), (2048, 4096), (4096, 5120))

        def wkey(col):
            return ('win', 0 if col < 2048 else (1 if col < 4096 else 2))

        def load_weights(l):
            l = layers.index(l)
            for c, (c0, c1) in enumerate(WCH):
                S.op('pool', lambda l=l, c0=c0, c1=c1: G.dma_start(
                    out=w_in_sb[:, :, c0:c1],
                    in_=dr["w_in"][l, :, c0:c1].rearrange("(k p) n -> p k n", p=128)),
                    writes=[('win', c)], dma=('win', c))

        def load_w_out(l):
            l = layers.index(l)
            S.op('pool', lambda l=l: G.dma_start(
                out=w_out_sb[:, :, :], in_=dr["w_out"][l, :, :].rearrange("(k p) n -> p k n", p=128)),
                writes=[('wout', 0), ('wout', 1)], dma=('wout', 0))

        import os as _os2
        _PROBE2 = _os2.environ.get('KPROBE2', '')

        def load_wkv(l):
            l = layers.index(l)
            S.op('pool', lambda l=l: G.dma_start(
                out=wkv[:, :, :], in_=dr["w_mem_kv"][l, :, :].rearrange("(k p) n -> p k n", p=128)),
                writes=[('wkv', 0), ('wkv', 1)], dma=('wkv', 0))

        def kv_prologue(l):
            l = layers.index(l)
            S.op('sp', lambda: SP.dma_start(out=memx, in_=dr["mem"].rearrange("(j p) d -> p j d", p=128)),
                 writes=['memx'], dma='memx')
            if _PROBE2 == 'c1':
                return
            for j in range(2):
                for half in range(2):
                    b = gen_bank()
                    for q in range(4):
                        kc = half * 4 + q
                        S.op('pe', lambda b=b, q=q, kc=kc, j=j: T.transpose(
                            out=bank(b, 128, q * 128), in_=memx[:, j, kc * 128:(kc + 1) * 128], identity=ident[:, :]),
                            reads=['memx', 'ident'], writes=[BK(b)])
                    S.op('act', lambda b=b, half=half, j=j: A.copy(
                        out=memT[:, half * 4:half * 4 + 4, j * 128:(j + 1) * 128],
                        in_=bank(b).rearrange("p (c t) -> p c t", c=4)),
                        writes=[BK(b), ('memT', j, half)])
            if _PROBE2 == 'c2':
                return
            memT_keys = [('memT', j, half) for j in range(2) for half in range(2)]
            for h in range(4):
                b = gen_bank()
                for kc in range(8):
                    S.op('pe', lambda b=b, h=h, kc=kc: T.matmul(
                        bank(b, 256), lhsT=wkv[:, kc, h * 128:(h + 1) * 128], rhs=memT[:, kc, :],
                        start=(kc == 0), stop=(kc == 7)),
                        reads=[('wkv', 0)] + memT_keys, writes=[BK(b)])
                S.op('act', lambda b=b, h=h: A.copy(out=kmemT[:, h, :], in_=bank(b, 256)),
                     writes=[BK(b), 'kmemT'])
            if _PROBE2 == 'c3':
                return
            for mb in range(2):
                b = gen_bank()
                for kc in range(8):
                    S.op('pe', lambda b=b, mb=mb, kc=kc: T.matmul(
                        bank(b), lhsT=memT[:, kc, mb * 128:(mb + 1) * 128], rhs=wkv[:, kc, 512:1024],
                        start=(kc == 0), stop=(kc == 7)),
                        reads=[('wkv', 1)] + memT_keys, writes=[BK(b)])
                S.op('dve', lambda b=b, mb=mb: V.tensor_copy(
                    out=vmem[:, mb, :, 0:128], in_=bank(b).rearrange("p (h d) -> p h d", h=4)),
                    writes=[BK(b), 'vmem'])
                S.op('pool', lambda mb=mb: G.memset(vmem[:, mb, :, 128:129], 1.0), writes=['vmem'])

        import os as _os
        PROBE = _os.environ.get("KPROBE", "")
        if PROBE != "a":
            load_wkv(layers[0])
            load_weights(layers[0])
            load_w_out(layers[0])

        for li, l in enumerate(layers if PROBE not in ("a", "b") else []):
            attn = (l % 2 == 0)
            la = l // 2
            src = dr["xin"] if li == 0 else scratch[l]
            dst = xout if l == last else scratch[l + 1]

            def dst_rows(t, l=l):
                if l == DEPTH - 1:
                    return (t - HALO) * 128
                return t * 128

            kv_prologue(l)
            S.fence()
            if PROBE == "c":
                break

            S.op('sp', lambda l=li: SP.dma_start(out=lng[:, :], in_=dr["ln_g"][l:l + 1, :].to_broadcast([128, D])),
                 writes=['lng'], dma='lng')
            S.op('sp', lambda l=li: SP.dma_start(out=lnb[:, :], in_=dr["ln_b"][l:l + 1, :].to_broadcast([128, D])),
                 writes=['lnb'], dma='lnb')
            if attn:
                S.op('sp', lambda la=la: SP.dma_start(out=chb_sb[:, :], in_=dr["chb"][la]), writes=['chb'], dma='chb')
                S.op('dve', lambda: V.tensor_scalar(out=small[:, 32:48], in0=chb_sb[:, :], scalar1=-1.0,
                                                    scalar2=None, op0=ALU.mult),
                     reads=['chb'], writes=['negchb'])
                for g in range(4):
                    S.op('sp', lambda la=la, g=g: SP.dma_start(out=btst, in_=dr["bt"][la, :, 4 * g:4 * g + 4, :]),
                         writes=[('tb', 0)], dma='btst')
                    for hh in range(4):
                        h = 4 * g + hh
                        S.op('act', lambda h=h, hh=hh: A.activation(
                            out=Ep[:, h, :], in_=btst[:, hh, :], func=(AF.Identity if BIAS_ON_PE else AF.Exp), bias=small[:, 32 + h:33 + h], scale=1.0),
                            reads=[('tb', 0), 'negchb'], writes=['Ep'])
                S.op('pool', lambda: G.memset(Ep[64:128, :, 128:192], (-30000.0 if BIAS_ON_PE else 0.0)), writes=['Ep'])
                if BIAS_ON_PE:
                    S.op('pool', lambda: G.memset(mask0[:, :], 0.0), writes=['mask0'])
                    S.op('pool', lambda: G.memset(mask0[0:64, 64:128], -30000.0), writes=['mask0'])
            else:
                S.op('sp', lambda la=la: SP.dma_start(out=cw_sb[:, :, :], in_=dr["convw"][la]), writes=['cw'], dma='cw')

            blocks = layer_plan(l)
            if max_blocks is not None:
                blocks = blocks[:max_blocks]
            nb = len(blocks)
            st_l = {'cu_prev': None, 'xloaded': set()}
            plan_tiles = [t_ for tl_, _m in blocks for t_ in tl_]

            def phase1(bi, l=l, attn=attn, blocks=blocks, src=src, st_l=st_l):
                tiles, mode = blocks[bi]
                n = len(tiles)
                N = 128 * n
                pb = bi % 2
                items = []

                def xunit(ti, t):
                    slot = t % NXS
                    if t not in st_l['xloaded']:
                        st_l['xloaded'].add(t)
                        S.op('sp', lambda slot=slot, t=t: SP.dma_start(out=xs[:, slot, :], in_=src[t * 128:(t + 1) * 128, :]),
                             reads=[('xd', l, t)], writes=[('xs', slot)], dma=('xs', slot))
                    for half in range(2):
                        b = gen_bank()
                        for q in range(4):
                            kc = half * 4 + q
                            S.op('pe', lambda b=b, q=q, kc=kc, slot=slot: T.transpose(
                                out=bank(b, 128, q * 128), in_=xs[:, slot, kc * 128:(kc + 1) * 128], identity=ident[:, :]),
                                reads=[('xs', slot), 'ident'], writes=[BK(b)])
                        if half == 0:
                            S.op('act', lambda b=b, half=half, ti=ti: A.copy(
                                out=xT[:, half * 4:half * 4 + 4, ti * 128:(ti + 1) * 128],
                                in_=bank(b).rearrange("p (c t) -> p c t", c=4)),
                                writes=[BK(b), ('xT', ti, half)])
                        else:
                            S.op('dve', lambda b=b, half=half, ti=ti: V.tensor_copy(
                                out=xT[:, half * 4:half * 4 + 4, ti * 128:(ti + 1) * 128],
                                in_=bank(b).rearrange("p (c t) -> p c t", c=4)),
                                writes=[BK(b), ('xT', ti, half)])
                for ti_, t_ in enumerate(tiles):
                    items.append((xunit, (ti_, t_)))
                xTk = [('xT', ti, half) for ti in range(n) for half in range(2)]

                def tokmajor(ti, col0, evac):
                    b = gen_bank()
                    c = col0 // 512
                    for kc in range(8):
                        S.op('pe', lambda b=b, kc=kc, ti=ti, col0=col0: T.matmul(
                            bank(b), lhsT=xT[:, kc, ti * 128:(ti + 1) * 128], rhs=w_in_sb[:, kc, col0:col0 + 512],
                            start=(kc == 0), stop=(kc == 7)),
                            reads=[('xT', ti, 0), ('xT', ti, 1), wkey(col0)], writes=[BK(b)])
                    evac(b)

                def featmajor(col0, evac):
                    b = gen_bank()
                    c = col0 // 512
                    for kc in range(8):
                        S.op('pe', lambda b=b, kc=kc, col0=col0: T.matmul(
                            bank(b, N), lhsT=w_in_sb[:, kc, col0:col0 + 128], rhs=xT[:, kc, 0:N],
                            start=(kc == 0), stop=(kc == 7)),
                            reads=xTk + [wkey(col0)], writes=[BK(b)])
                    evac(b)

                if attn:
                    for ti, t in enumerate(tiles):
                        vs = t % 8
                        for hb in range(2):
                            def ev(b, t=t, vs=vs, hb=hb):
                                S.op('dve', lambda: V.tensor_scalar(
                                    out=Vr[:, vs, hb * 8:(hb + 1) * 8, 0:64],
                                    in0=bank(b).rearrange("p (h d) -> p h d", h=8),
                                    scalar1=valid_sb[:, t:t + 1], scalar2=None, op0=ALU.mult),
                                    reads=['valid'], writes=[BK(b), ('V', vs)])
                            items.append((tokmajor, (ti, 2048 + hb * 512, ev)))
                        S.op('pool', lambda vs=vs, t=t: G.tensor_copy(
                            out=Vr[:, vs, :, 64:65], in_=valid_sb[:, t:t + 1].unsqueeze(1).to_broadcast([128, 16, 1])),
                            reads=['valid'], writes=[('V', vs)])
                    for j in range(8):
                        def ev(b, j=j):
                            for ti, t in enumerate(tiles):
                                ks = t % 8
                                if (j + ti) % 2 == 0:
                                    S.op('act', lambda ti=ti, ks=ks: A.copy(
                                        out=KTr[:, j, ks * 128:(ks + 1) * 128], in_=bank(b, 128, ti * 128)),
                                        writes=[BK(b), ('K', ks)])
                                else:
                                    S.op('dve', lambda ti=ti, ks=ks: V.tensor_copy(
                                        out=KTr[:, j, ks * 128:(ks + 1) * 128], in_=bank(b, 128, ti * 128)),
                                        writes=[BK(b), ('K', ks)])
                        items.append((featmajor, (1024 + j * 128, ev)))
                    if mode == 'full':
                        for j in range(8):
                            def ev(b, j=j):
                                S.op('act', lambda: A.mul(out=QT[:, pb, j, 0:N], in_=bank(b, N), mul=0.125),
                                     writes=[BK(b), ('QT', pb)])
                            items.append((featmajor, (j * 128, ev)))
                else:
                    cb = bi % 2
                    prev = st_l['cu_prev']
                    if prev is not None:
                        pcb, pN, pt = prev
                        S.op('pool', lambda cb=cb, pcb=pcb, pN=pN, pt=pt: G.tensor_tensor(
                            out=cu[:, cb, :, 0:2], in0=cu[:, pcb, :, pN:pN + 2],
                            in1=vrow_sb[:, pt:pt + 1, :].to_broadcast([128, 8, 2]), op=ALU.mult),
                            reads=[('cu', pcb), 'vrow'], writes=[('cu', cb)])
                    else:
                        S.op('pool', lambda cb=cb: G.memset(cu[:, cb, :, 0:2], 0.0), writes=[('cu', cb)])
                    for j in range(8):
                        pj = j % 2

                        def ev1(b, pj=pj):
                            S.op('act', lambda: A.copy(out=p1s[:, pj, 0:N], in_=bank(b, N)),
                                 writes=[BK(b), ('p1s', pj)])
                        items.append((featmajor, (1024 + j * 128, ev1)))

                        def ev2(b, j=j, pj=pj, cb=cb):
                            S.op('dve', lambda: V.tensor_tensor(out=cu[:, cb, j, 2:2 + N], in0=bank(b, N),
                                                                in1=p1s[:, pj, 0:N], op=ALU.mult),
                                 reads=[('p1s', pj)], writes=[BK(b), ('cu', cb)])
                        items.append((featmajor, (2048 + j * 128, ev2)))
                    st_l['cu_prev'] = (cb, N, tiles[-1])
                    if mode == 'full':
                        for j in range(8):
                            def ev(b, j=j):
                                S.op('act', lambda: A.copy(out=p0s[:, pb, j, 0:N], in_=bank(b, N)),
                                     writes=[BK(b), ('p0s', pb, j)])
                            items.append((featmajor, (j * 128, ev)))

                            def evz(b, j=j):
                                S.op('act', lambda: A.activation(out=szT[:, pb, j, 0:N], in_=bank(b, N), func=AF.Silu),
                                     writes=[BK(b), ('szT', pb, j)])
                            items.append((featmajor, (3584 + j * 128, evz)))
                if mode == 'full':
                    for j in range(4):
                        def ev(b, j=j):
                            S.op('act', lambda: A.mul(out=QmT[:, pb, j, 0:N], in_=bank(b, N), mul=1.0 / math.sqrt(128.0)),
                                 writes=[BK(b), ('QmT', pb)])
                        items.append((featmajor, (3072 + j * 128, ev)))
                    zblocks = (0, 1, 2) if attn else (2,)
                    for ti, t in enumerate(tiles):
                        for zb in zblocks:
                            def ev(b, ti=ti, zb=zb):
                                S.op('act', lambda: A.activation(out=siluz[:, pb, ti, zb * 512:(zb + 1) * 512],
                                                                 in_=bank(b), func=AF.Silu),
                                     writes=[BK(b), ('siluz', pb, ti)])
                            items.append((tokmajor, (ti, 3584 + zb * 512, ev)))
                return items

            def phase2(bi, nxt, l=l, attn=attn, blocks=blocks, dst=dst, src=src, st_l=st_l, plan_tiles=plan_tiles):
                tiles, mode = blocks[bi]
                if mode != 'full':
                    for f_, a_ in nxt:
                        f_(*a_)
                    return
                pts = {'left': (22 if attn else 6) * len(tiles)}

                def fill():
                    left = max(pts['left'], 1)
                    k = -(-len(nxt) // left)
                    for _ in range(min(k, len(nxt))):
                        f_, a_ = nxt.pop(0)
                        f_(*a_)
                    pts['left'] -= 1
                _d2 = _os.environ.get("KDBG2")
                if _d2 and int(_os.environ["KDBG"]) == tiles[0]:
                    if _d2 == 'xT':
                        S.op('sp', lambda: SP.dma_start(out=dbg2, in_=xT[:, :, :].rearrange("p a b -> p (a b)")),
                             reads=[('xT', 0, 0), ('xT', 0, 1), ('xT', 1, 0), ('xT', 1, 1)], writes=[('xd', l + 1, 1)], dma='gdbg2')
                    elif _d2 == 'w':
                        S.op('sp', lambda: SP.dma_start(out=dbg2, in_=w_in_sb[:, 0, 3072:5120]),
                             reads=[('win', 1), ('win', 2)], writes=[('xd', l + 1, 1)], dma='gdbg2')
                    elif _d2 == 'QT':
                        S.op('sp', lambda: SP.dma_start(out=dbg2.rearrange("p (a b) -> p a b", a=8), in_=QT[:, bi % 2, :, :]),
                             reads=[('QT', bi % 2)], writes=[('xd', l + 1, 1)], dma='gdbg2')
                    elif _d2 == 'K':
                        S.op('sp', lambda: SP.dma_start(out=dbg2[:, 0:1024].rearrange("p (a b) -> p a b", a=8),
                                                        in_=KTr[:, :, (tiles[0] % 8) * 128:(tiles[0] % 8) * 128 + 128]),
                             reads=[('K', tiles[0] % 8)], writes=[('xd', l + 1, 1)], dma='gdbg2')
                    elif _d2 == 'QmT':
                        S.op('sp', lambda: SP.dma_start(out=dbg2[:, 0:1024].rearrange("p (a b) -> p a b", a=4), in_=QmT[:, bi % 2, :, :]),
                             reads=[('QmT', bi % 2)], writes=[('xd', l + 1, 1)], dma='gdbg2')
                    elif _d2 == 'V':
                        S.op('sp', lambda: SP.dma_start(out=dbg2[:, 0:1040].rearrange("p (a b) -> p a b", a=16), in_=Vr[:, tiles[0] % 8, :, :]),
                             reads=[('V', tiles[0] % 8)], writes=[('xd', l + 1, 1)], dma='gdbg2')
                    elif _d2 == 'sz':
                        S.op('sp', lambda: SP.dma_start(out=dbg2[:, 0:1536], in_=siluz[:, bi % 2, 0, :]),
                             reads=[('siluz', bi % 2, 0)], writes=[('xd', l + 1, 1)], dma='gdbg2')
                n = len(tiles)
                N = 128 * n
                pb = bi % 2
                if not attn:
                    cb = bi % 2
                    for j in range(8):
                        aj = j % 2
                        S.op('dve', lambda j=j, aj=aj: V.tensor_scalar(
                            out=cacc[:, aj, 0:N], in0=cu[:, cb, j, 0:N], scalar1=cw_sb[:, j, 0:1], scalar2=None,
                            op0=ALU.mult), reads=[('cu', cb), 'cw'], writes=[('cacc', aj)])
                        for k in (1, 2):
                            S.op('dve', lambda j=j, aj=aj, k=k: V.scalar_tensor_tensor(
                                out=cacc[:, aj, 0:N], in0=cu[:, cb, j, k:k + N], scalar=cw_sb[:, j, k:k + 1],
                                in1=cacc[:, aj, 0:N], op0=ALU.mult, op1=ALU.add),
                                reads=[('cu', cb), 'cw'], writes=[('cacc', aj)])
                        S.op('pool', lambda j=j, aj=aj: G.tensor_tensor(
                            out=cacc[:, aj, 0:N], in0=cacc[:, aj, 0:N], in1=p0s[:, pb, j, 0:N], op=ALU.mult),
                            reads=[('p0s', pb, j)], writes=[('cacc', aj)])
                        S.op('pool', lambda j=j, aj=aj: G.tensor_tensor(
                            out=szT[:, pb, j, 0:N], in0=cacc[:, aj, 0:N], in1=szT[:, pb, j, 0:N], op=ALU.mult),
                            reads=[('cacc', aj)], writes=[('szT', pb, j)])
                def tile_body(ti, t):
                    xslot = t % NXS
                    if attn:
                        def scores(h):
                            sbuf = h % 2
                            hp = h % 2
                            ch = h // 2
                            for j in range(5):
                                ks = (t - 4 + j) % 8
                                hasb = BIAS_ON_PE and j in (0, 3, 4)
                                S.op('pe', lambda sbuf=sbuf, j=j, ks=ks, hp=hp, ch=ch, hasb=hasb: T.matmul(
                                    PS[:, sbuf * 1024 + j * 128: sbuf * 1024 + (j + 1) * 128],
                                    lhsT=KTr[hp * 64:(hp + 1) * 64, ch, ks * 128:(ks + 1) * 128],
                                    rhs=QT[hp * 64:(hp + 1) * 64, pb, ch, ti * 128:(ti + 1) * 128],
                                    start=True, stop=(not hasb)),
                                    reads=[('K', ks), ('QT', pb)], writes=[BK(2 * sbuf), BK(2 * sbuf + 1)])
                                if hasb:
                                    if j == 0:
                                        brhs = mask0[:, :]
                                        bk = 'mask0'
                                    else:
                                        brhs = Ep[:, h, (j - 3) * 128:(j - 2) * 128]
                                        bk = 'Ep'
                                    S.op('pe', lambda sbuf=sbuf, j=j, brhs=brhs: T.matmul(
                                        PS[:, sbuf * 1024 + j * 128: sbuf * 1024 + (j + 1) * 128],
                                        lhsT=ident_bf[:, :], rhs=brhs, start=False, stop=True),
                                        reads=[bk, 'ident_bf'], writes=[BK(2 * sbuf), BK(2 * sbuf + 1)])
                            pbuf = h % 3
                            S.op('act', lambda sbuf=sbuf, pbuf=pbuf: A.activation(
                                out=PT[:, pbuf, :], in_=PS[:, sbuf * 1024: sbuf * 1024 + 640], func=AF.Exp),
                                writes=[BK(2 * sbuf), BK(2 * sbuf + 1), ('PT', pbuf)])
                            if not BIAS_ON_PE:
                                S.op('dve', lambda pbuf=pbuf, h=h: V.tensor_tensor(
                                    out=PT[:, pbuf, 384:640], in0=PT[:, pbuf, 384:640], in1=Ep[:, h, :], op=ALU.mult),
                                    reads=['Ep'], writes=[('PT', pbuf)])
                                S.op('dve', lambda pbuf=pbuf: V.memset(PT[0:64, pbuf, 64:128], 0.0),
                                     writes=[('PT', pbuf)])

                        pvb = {'b': None}

                        def pv(h):
                            hg = h % 4
                            if hg == 0:
                                pvb['b'] = 7
                            b = pvb['b']
                            pbuf = h % 3
                            for j in range(5):
                                vs = (t - 4 + j) % 8
                                S.op('pe', lambda b=b, hg=hg, j=j, vs=vs, pbuf=pbuf, h=h: T.matmul(
                                    bank(b, 65, hg * 65), lhsT=PT[:, pbuf, j * 128:(j + 1) * 128],
                                    rhs=Vr[:, vs, h, :], start=(j == 0), stop=(j == 4)),
                                    reads=[('PT', pbuf), ('V', vs)], writes=[BK(b)])
                            if hg == 3:
                                h0 = h - 3
                                gb = (h // 4) % 2
                                pv4 = bank(b, 260).rearrange("p (h d) -> p h d", h=4)
                                S.op('dve', lambda pv4=pv4: V.tensor_scalar(
                                    out=small[:, 0:4].unsqueeze(2), in0=pv4[:, :, 64:65], scalar1=1e-30, scalar2=None,
                                    op0=ALU.add), writes=[BK(b), 'rs'])
                                S.op('dve', lambda: V.reciprocal(out=small[:, 0:4], in_=small[:, 0:4]), writes=['rs'])
                                S.op('dve', lambda pv4=pv4, gb=gb: V.tensor_tensor(
                                    out=gtmp[:, gb, :].rearrange("p (h d) -> p h d", h=4), in0=pv4[:, :, 0:64],
                                    in1=small[:, 0:4].unsqueeze(2).to_broadcast([128, 4, 64]), op=ALU.mult),
                                    reads=['rs'], writes=[BK(b), ('gtmp', gb)])
                                S.op('pool', lambda gb=gb, h0=h0: G.tensor_tensor(
                                    out=siluz[:, pb, ti, h0 * 64:(h0 + 4) * 64], in0=gtmp[:, gb, :],
                                    in1=siluz[:, pb, ti, h0 * 64:(h0 + 4) * 64], op=ALU.mult),
                                    reads=[('gtmp', gb)], writes=[('siluz', pb, ti)])

                        scores(0)
                        scores(1)
                        for h in range(16):
                            if h + 2 < 16:
                                scores(h + 2)
                            fill()
                            pv(h)
                    mbank = {}

                    def mscore(hm):
                        b = gen_bank()
                        mbank[hm] = b
                        for mb in range(2):
                            S.op('pe', lambda b=b, mb=mb, hm=hm: T.matmul(
                                bank(b, 128, mb * 128), lhsT=kmemT[:, hm, mb * 128:(mb + 1) * 128],
                                rhs=QmT[:, pb, hm, ti * 128:(ti + 1) * 128], start=True, stop=True),
                                reads=['kmemT', ('QmT', pb)], writes=[BK(b)])
                        mbuf = hm % 2
                        S.op('act', lambda b=b, mbuf=mbuf: A.activation(out=PTm[:, mbuf, :], in_=bank(b, 256), func=AF.Exp),
                             writes=[BK(b), ('PTm', mbuf)])

                    def mpv(hm):
                        b = mbank[hm]
                        mbuf = hm % 2
                        for mb in range(2):
                            S.op('pe', lambda b=b, mb=mb, hm=hm, mbuf=mbuf: T.matmul(
                                bank(b, 129, 256), lhsT=PTm[:, mbuf, mb * 128:(mb + 1) * 128],
                                rhs=vmem[:, mb, hm, :], start=(mb == 0), stop=(mb == 1)),
                                reads=[('PTm', mbuf), 'vmem'], writes=[BK(b)])
                        S.op('dve', lambda b=b: V.tensor_scalar(
                            out=small[:, 8:9], in0=bank(b, 1, 256 + 128), scalar1=1e-30, scalar2=None, op0=ALU.add),
                            writes=[BK(b), 'rsm'])
                        S.op('dve', lambda: V.reciprocal(out=small[:, 8:9], in_=small[:, 8:9]), writes=['rsm'])
                        S.op('dve', lambda b=b, hm=hm: V.scalar_tensor_tensor(
                            out=siluz[:, pb, ti, 1024 + hm * 128:1024 + (hm + 1) * 128], in0=bank(b, 128, 256),
                            scalar=small[:, 8:9], in1=siluz[:, pb, ti, 1024 + hm * 128:1024 + (hm + 1) * 128],
                            op0=ALU.mult, op1=ALU.mult),
                            reads=['rsm'], writes=[BK(b), ('siluz', pb, ti)])

                    mscore(0)
                    for hm in range(4):
                        if hm + 1 < 4:
                            mscore(hm + 1)
                        mpv(hm)
                        fill()
                    if _os.environ.get("KDBG") and int(_os.environ["KDBG"]) == t:
                        S.op('sp', lambda: SP.dma_start(out=dbg_out, in_=siluz[:, pb, ti, :]),
                             reads=[('siluz', pb, ti)], writes=[('xd', l + 1, 0)], dma='gdbg')
                    if _os.environ.get("KDBG2") == 'PT' and int(_os.environ["KDBG"]) == t:
                        S.op('sp', lambda: SP.dma_start(out=dbg2[:, 0:1280].rearrange("p (a b) -> p a b", a=2), in_=PT[:, :, :]),
                             reads=[('PT', 0), ('PT', 1)], writes=[('xd', l + 1, 1)], dma='gdbg2')
                        S.op('sp', lambda: SP.dma_start(out=dbg2[:, 1280:1792].rearrange("p (a b) -> p a b", a=2), in_=PTm[:, :, :]),
                             reads=[('PTm', 0), ('PTm', 1)], writes=[('xd', l + 1, 2)], dma='gdbg3')
                    chunks = list(range(12)) if attn else list(range(8, 12))
                    for g0 in range(0, len(chunks), 4):
                        grp = chunks[g0:g0 + 4]
                        b = gen_bank()
                        psb = bank(b).bitcast(BF16)
                        for q, j in enumerate(grp):
                            S.op('pe', lambda q=q, j=j, psb=psb: T.transpose(
                                out=psb[:, q * 128:(q + 1) * 128], in_=siluz[:, pb, ti, j * 128:(j + 1) * 128],
                                identity=ident_bf[:, :]),
                                reads=[('siluz', pb, ti), 'ident_bf'], writes=[BK(b)])
                        j0 = grp[0]
                        S.op('dve', lambda psb=psb, j0=j0: V.tensor_copy(
                            out=yT[:, j0:j0 + 4, :], in_=psb[:, 0:512].rearrange("p (c t) -> p c t", c=4)),
                            writes=[BK(b), ('yT', j0 // 4)])
                    fill()
                    tslot = 0
                    for half in range(2):
                        b = gen_bank()
                        for ec in range(12):
                            if attn or ec >= 8:
                                lh = yT[:, ec, :]
                                rk = [('yT', ec // 4)]
                            else:
                                lh = szT[:, pb, ec, ti * 128:(ti + 1) * 128]
                                rk = [('szT', pb, ec)]
                            S.op('pe', lambda b=b, ec=ec, half=half, lh=lh: T.matmul(
                                bank(b), lhsT=lh, rhs=w_out_sb[:, ec, half * 512:(half + 1) * 512],
                                start=(ec == 0), stop=(ec == 11)),
                                reads=rk + [('wout', half)], writes=[BK(b)])
                        S.op('dve', lambda b=b, half=half, xslot=xslot, tslot=tslot: V.scalar_tensor_tensor(
                            out=tb[:, tslot, half * 512:(half + 1) * 512], in0=xs[:, xslot, half * 512:(half + 1) * 512],
                            scalar=ALPHA, in1=bank(b), op0=ALU.mult, op1=ALU.add),
                            reads=[('xs', xslot)], writes=[BK(b), ('tb', tslot)])
                        S.op('dve', lambda half=half, tslot=tslot: V.bn_stats(
                            out=stats[:, tslot, half, :], in_=tb[:, tslot, half * 512:(half + 1) * 512]),
                            reads=[('tb', tslot)], writes=[('stats', tslot)])
                    tn = t + NXS
                    if tn in plan_tiles and tn not in st_l['xloaded']:
                        st_l['xloaded'].add(tn)
                        S.op('sp', lambda xslot=xslot, tn=tn: SP.dma_start(out=xs[:, xslot, :], in_=src[tn * 128:(tn + 1) * 128, :]),
                             reads=[('xd', l, tn)], writes=[('xs', xslot)], dma=('xs', xslot))
                    S.op('dve', lambda tslot=tslot: V.bn_aggr(
                        out=mv[:, tslot, 0:2], in_=stats[:, tslot, :, :].rearrange("p a b -> p (a b)")),
                        reads=[('stats', tslot)], writes=[('mv', tslot)])
                    S.op('act', lambda tslot=tslot: A.activation(
                        out=small[:, 20:21], in_=mv[:, tslot, 1:2], func=AF.Ln, bias=small[:, 16:17], scale=1.0),
                        reads=[('mv', tslot), 'epsc'], writes=[('mvb', tslot)])
                    S.op('act', lambda tslot=tslot: A.activation(
                        out=mv[:, tslot, 2:3], in_=small[:, 20:21], func=AF.Exp, scale=-0.5),
                        reads=[('mvb', tslot)], writes=[('mvc', tslot)])
                    S.op('dve', lambda tslot=tslot: V.scalar_tensor_tensor(
                        out=mv[:, tslot, 3:4], in0=mv[:, tslot, 0:1], scalar=-1.0, in1=mv[:, tslot, 2:3],
                        op0=ALU.mult, op1=ALU.mult), reads=[('mv', tslot), ('mvc', tslot)], writes=[('mvd', tslot)])
                    S.op('act', lambda tslot=tslot: A.activation(
                        out=tb[:, tslot, :], in_=tb[:, tslot, :], func=AF.Identity,
                        bias=mv[:, tslot, 3:4], scale=mv[:, tslot, 2:3]),
                        reads=[('mvd', tslot), ('mvc', tslot)], writes=[('tb', tslot)])
                    S.op('pool', lambda tslot=tslot: G.tensor_tensor(
                        out=tb[:, tslot, :], in0=tb[:, tslot, :], in1=lng[:, :], op=ALU.mult),
                        reads=['lng'], writes=[('tb', tslot)])
                    S.op('pool', lambda tslot=tslot: G.tensor_tensor(
                        out=tb[:, tslot, :], in0=tb[:, tslot, :], in1=lnb[:, :], op=ALU.add),
                        reads=['lnb'], writes=[('tb', tslot)])
                    if l == DEPTH - 1 and t < HALO:
                        return
                    r0 = dst_rows(t)
                    S.op('sp', lambda tslot=tslot, r0=r0: SP.dma_start(out=dst[r0:r0 + 128, :], in_=tb[:, tslot, :]),
                         reads=[('tb', tslot)], writes=[('xd', l + 1, t)], dma=('tbo', tslot))

                for ti_, t_ in enumerate(tiles):
                    tile_body(ti_, t_)
                    fill()
                for f_, a_ in nxt:
                    f_(*a_)

            if nb > 0:
                for f_, a_ in phase1(0):
                    f_(*a_)
            for bi in range(nb):
                nxt = phase1(bi + 1) if bi + 1 < nb else []
                if bi + 1 == nb and li + 1 < len(layers):
                    load_weights(layers[li + 1])
                phase2(bi, nxt)
            S.fence()
            if li + 1 < len(layers):
                load_wkv(layers[li + 1])
                load_w_out(layers[li + 1])

        outkeys = [('xd', last + 1, t) for t in range(NT)]
        S.op('sp', None, reads=outkeys)
        S.finalize(st)
    return nc, S


def _host_layout(inputs):
    x = np.asarray(inputs["x"], dtype=np.float32)
    mem = np.asarray(inputs["mem"], dtype=np.float32)
    rel_bias = np.asarray(inputs["rel_bias"], dtype=np.float32)
    conv_w = np.asarray(inputs["conv_w"], dtype=np.float32)
    k = np.arange(128)[:, None]
    q = np.arange(128)[None, :]
    idx3 = np.minimum(q - k + 128, 128) + 128
    idx4 = (q - k) + 128
    idx = np.concatenate([idx3, idx4], axis=1)
    bt = np.ascontiguousarray(np.transpose(rel_bias[:, :, idx], (0, 2, 1, 3)))
    chb = np.ascontiguousarray(np.broadcast_to(rel_bias[:, None, :, 256], (2, 128, 16)))
    convw = np.ascontiguousarray(np.transpose(conv_w.reshape(2, 3, 8, 128), (0, 3, 2, 1)))
    common = dict(
        ident=np.eye(128, dtype=np.float32),
        w_in=np.asarray(inputs["w_in"], dtype=np.float32),
        w_mem_kv=np.asarray(inputs["w_mem_kv"], dtype=np.float32),
        w_out=np.asarray(inputs["w_out"], dtype=np.float32),
        bt=bt, chb=chb, convw=convw,
        ln_g=np.asarray(inputs["ln_g"], dtype=np.float32),
        ln_b=np.asarray(inputs["ln_b"], dtype=np.float32),
    )
    per_core = []
    for c in range(8):
        b, half = c // 2, c % 2
        s0 = half * 4096
        w0 = s0 - HALO * 128
        xw = np.zeros((NT * 128, D), np.float32)
        lo = max(w0, 0)
        xw[lo - w0:] = x[b, lo:s0 + 4096]
        vt = np.zeros((NT * 128,), np.float32)
        vt[lo - w0:] = 1.0
        valid = np.ascontiguousarray(vt.reshape(NT, 128).T)
        vrow = np.ascontiguousarray(np.broadcast_to(vt.reshape(NT, 128)[None, :, 126:128], (128, NT, 2)))
        m = dict(common)
        m.update(xin=xw, valid=valid, vrow=vrow, mem=np.ascontiguousarray(mem[b]))
        per_core.append(m)
    return per_core


_CACHE = {}


def _get_program(layers):
    key = tuple(layers)
    if key not in _CACHE:
        _CACHE[key] = build_program(list(layers))[0]
    return _CACHE[key]


FUSED = True


def kernel(**inputs):
    per_core = _host_layout(inputs)
    groups = [[0, 1, 2, 3]] if FUSED else [[0], [1], [2], [3]]
    full = {k: per_core[0][k] for k in ("w_in", "w_mem_kv", "w_out", "ln_g", "ln_b")}
    for grp in groups:
        nc = _get_program(grp)
        sl = {k: np.ascontiguousarray(v[grp[0]:grp[-1] + 1]) for k, v in full.items()}
        maps = []
        for c in range(8):
            m = dict(per_core[c])
            m.update(sl)
            maps.append(m)
        res = run_bass_kernel_spmd(nc, maps, core_ids=list(range(8)))
        outs = [r["xout"] for r in res.results]
        if grp[-1] != DEPTH - 1:
            for c in range(8):
                per_core[c]["xin"] = outs[c]
    out = np.zeros((4, 8192, D), np.float32)
    for c in range(8):
        b, half = c // 2, c % 2
        out[b, half * 4096:(half + 1) * 4096] = outs[c]
    return out
```

```python
import math
from contextlib import ExitStack

import numpy as np
import concourse.bass as bass
import concourse.mybir as mybir
from concourse.bass_utils import run_bass_kernel_spmd

F32 = mybir.dt.float32
BF16 = mybir.dt.bfloat16
AF = mybir.ActivationFunctionType
ALU = mybir.AluOpType

D = 1024
NIN = 5120
NT = 42
HALO = 10
NMAIN = 32
DEPTH = 4
ALPHA = (2.0 * DEPTH) ** 0.25
EPS = 1e-5
NXS = 4
BIAS_ON_PE = False

ENG_ATTR = {'pe': 'tensor', 'act': 'scalar', 'dve': 'vector', 'pool': 'gpsimd', 'sp': 'sync'}


class Op:
    __slots__ = ('eng', 'fn', 'reads', 'writes', 'dma', 'waits', 'signal', 'ev', 'vc', 'sigcount')

    def __init__(self, eng, fn, reads, writes, dma):
        self.eng = eng
        self.fn = fn
        self.reads = reads
        self.writes = writes
        self.dma = dma
        self.waits = []
        self.signal = False
        self.ev = None
        self.vc = None
        self.sigcount = 0


class Sched:
    def __init__(self, nc, same_engine_sync=('act', 'dve', 'pool')):
        self.nc = nc
        self.ops = []
        self.same_sync = set(same_engine_sync)

    def op(self, eng, fn, reads=(), writes=(), dma=None):
        self.ops.append(Op(eng, fn, tuple(reads), tuple(writes), dma))

    def fence(self):
        shared = {}
        for e in ENG_ATTR:
            self.ops.append(Op(e, None, ('__fence__', shared), (), None))

    def finalize(self, stack):
        nc = self.nc
        state = {}
        vcs = {e: {} for e in ENG_ATTR}
        eng_count = {e: 0 for e in ENG_ATTR}
        dma_count = {}
        evop = {}
        dmavc = {}
        for op in self.ops:
            e = op.eng
            deps = {}
            if len(op.reads) == 2 and op.reads[0] == '__fence__':
                shared = op.reads[1]
                if not shared:
                    for e2, c2 in eng_count.items():
                        if c2 > 0:
                            shared[('e', e2)] = c2 - 1
                    for dk2, c2 in dma_count.items():
                        shared[dk2] = c2
                deps.update(shared)
                op.reads = ()
            for r in op.reads:
                st = state.get(r)
                if st:
                    for (k, i) in st[0]:
                        if deps.get(k, -1) < i:
                            deps[k] = i
            for w in op.writes:
                st = state.get(w)
                if st:
                    for (k, i) in st[0]:
                        if deps.get(k, -1) < i:
                            deps[k] = i
                    for (k, i) in st[1]:
                        if deps.get(k, -1) < i:
                            deps[k] = i
            vc = vcs[e]
            myk = ('e', e)
            for k, i in deps.items():
                if k == myk and e not in self.same_sync:
                    continue
                if vc.get(k, -1) >= i:
                    continue
                op.waits.append((k, i))
            for (k, i) in op.waits:
                if k[0] == 'e':
                    src = evop[(k, i)]
                    src.signal = True
                    svc = src.vc
                else:
                    svc = dmavc[(k, i)]
                for kk, ii in svc.items():
                    if vc.get(kk, -1) < ii:
                        vc[kk] = ii
                if vc.get(k, -1) < i:
                    vc[k] = i
            if op.fn is None:
                op.ev = None
                continue
            if op.dma is None:
                idx = eng_count[e]
                eng_count[e] = idx + 1
                op.ev = (myk, idx)
                evop[op.ev] = op
                snap = dict(vc)
                snap[myk] = idx
                op.vc = snap
                if e not in self.same_sync:
                    vc[myk] = idx
            else:
                dk = ('d', op.dma)
                c = dma_count.get(dk, 0) + 1
                dma_count[dk] = c
                op.ev = (dk, c)
                dmavc[op.ev] = dict(vc)
            for r in op.reads:
                st = state.setdefault(r, [[], []])
                rl = [x for x in st[1] if x[0] != op.ev[0]]
                rl.append(op.ev)
                st[1] = rl
            for w in op.writes:
                state[w] = [[op.ev], []]
        sems = {}
        for e in ENG_ATTR:
            sems[('e', e)] = stack.enter_context(nc.semaphore('sem_' + e))
        for dk in dma_count:
            sems[dk] = stack.enter_context(nc.semaphore('dsem_%d' % len(sems)))
        cnt = {e: 0 for e in ENG_ATTR}
        for op in self.ops:
            if op.dma is None and op.signal:
                cnt[op.eng] += 1
                op.sigcount = cnt[op.eng]
        for op in self.ops:
            engobj = getattr(nc, ENG_ATTR[op.eng])
            for (k, i) in op.waits:
                val = evop[(k, i)].sigcount if k[0] == 'e' else 16 * i
                engobj.wait_ge(sems[k], val)
            if op.fn is None:
                continue
            ins = op.fn()
            if op.dma is not None:
                ins.then_inc(sems[op.ev[0]], 16)
            elif op.signal:
                ins.then_inc(sems[('e', op.eng)], 1)
        self.stats = dict(nops=len(self.ops), sig=cnt, nsems=len(sems))


def layer_plan(l):
    if l == 0:
        blocks = [((0, 1), 'kv'), ((2, 3), 'kv')] + [((t, t + 1), 'full') for t in range(4, NT, 2)]
    elif l == 1:
        blocks = [((4,), 'cu'), ((5,), 'full')] + [((t, t + 1), 'full') for t in range(6, NT, 2)]
    elif l == 2:
        blocks = [((5, 6), 'kv'), ((7, 8), 'kv'), ((9,), 'full')] + \
                 [((t, t + 1), 'full') for t in range(10, NT, 2)]
    else:
        blocks = [((9,), 'cu')] + [((t, t + 1), 'full') for t in range(10, NT, 2)]
    return blocks


def build_program(layers, same_engine_sync=('act', 'dve', 'pool'), max_blocks=None):
    nc = bass.Bass("TRN2", target_bir_lowering=False, dynamic_dma_scratch_size=8192)
    last = layers[-1]
    final = (last == DEPTH - 1)
    dr = {}

    def din(name, shape):
        dr[name] = nc.dram_tensor(name, list(shape), F32, kind="ExternalInput").ap()

    din("xin", (NT * 128, D))
    din("valid", (128, NT))
    din("vrow", (128, NT, 2))
    din("mem", (256, D))
    din("ident", (128, 128))
    din("w_in", (len(layers), D, NIN))
    din("w_mem_kv", (len(layers), D, D))
    din("w_out", (len(layers), 1536, D))
    din("bt", (2, 128, 16, 256))
    din("chb", (2, 128, 16))
    din("convw", (2, 128, 8, 3))
    din("ln_g", (len(layers), D))
    din("ln_b", (len(layers), D))
    if final:
        xout = nc.dram_tensor("xout", [NMAIN * 128, D], F32, kind="ExternalOutput").ap()
    else:
        xout = nc.dram_tensor("xout", [NT * 128, D], F32, kind="ExternalOutput").ap()
    scratch = {}
    for l in layers[:-1]:
        scratch[l + 1] = nc.dram_tensor("xs%d" % (l + 1), [NT * 128, D], F32).ap()

    S = Sched(nc, same_engine_sync)
    with ExitStack() as st:
        def sb(name, shape, dt):
            return st.enter_context(nc.sbuf_tensor("sb_" + name, list(shape), dt))

        w_in_sb = sb("w_in_sb", (128, 8, NIN), BF16)
        w_out_sb = sb("w_out_sb", (128, 12, D), BF16)
        kmemT = sb("kmemT", (128, 4, 256), BF16)
        vmem = sb("vmem", (128, 2, 4, 129), BF16)
        valid_sb = sb("valid_sb", (128, NT), F32)
        vrow_sb = sb("vrow_sb", (128, NT, 2), F32)
        ident = sb("ident", (128, 128), F32)
        ident_bf = sb("ident_bf", (128, 128), BF16)
        lng = sb("lng", (128, D), F32)
        lnb = sb("lnb", (128, D), F32)
        xs = sb("xs", (128, NXS, D), F32)
        xT = sb("xT", (128, 8, 256), BF16)
        tb = sb("tb", (128, 1, D), F32)
        siluz = sb("siluz", (128, 2, 2, 1536), BF16)
        yT = sb("yT", (128, 12, 128), BF16)
        QmT = sb("QmT", (128, 2, 4, 256), BF16)
        PTm = sb("PTm", (128, 2, 256), BF16)
        small = sb("small", (128, 64), F32)
        stats = sb("stats", (128, 2, 2, 6), F32)
        mv = sb("mv", (128, 2, 4), F32)
        gtmp = sb("gtmp", (128, 2, 256), F32)
        chb_sb = sb("chb_sb", (128, 16), F32)
        mask0 = sb("mask0", (128, 128), BF16) if BIAS_ON_PE else None
        import os as _os
        dbg_out = nc.dram_tensor("dbg", [128, 1536], BF16, kind="ExternalOutput").ap() if _os.environ.get("KDBG") else None
        dbg2 = nc.dram_tensor("dbg2", [128, 2048], BF16, kind="ExternalOutput").ap() if _os.environ.get("KDBG2") else None
        cw_sb = sb("cw_sb", (128, 8, 3), F32)
        MIXB = 53248
        MIX = sb("MIX", (128, MIXB // 2), BF16)

        def mview(off, shape, dt):
            esz = 4 if dt == F32 else 2
            n = 1
            for d_ in shape:
                n *= d_
            v = MIX[:, off // 2: off // 2 + (n * esz) // 2]
            if dt == F32:
                v = v.bitcast(F32)
            if len(shape) == 1:
                return v
            names = "abcd"[:len(shape)]
            pat = "p (%s) -> p %s" % (" ".join(names), " ".join(names))
            kw = {names[i]: shape[i] for i in range(len(shape) - 1)}
            return v.rearrange(pat, **kw)

        QT = mview(0, (2, 8, 256), BF16)
        KTr = mview(8192, (8, 1024), BF16)
        Vr = mview(24576, (8, 16, 65), BF16)
        PT = mview(41216, (3, 640), BF16)
        Ep = mview(45056, (16, 256), BF16)
        cu = mview(0, (2, 8, 258), F32)
        p0s = mview(16512, (2, 8, 256), F32)
        p1s = mview(32896, (2, 256), F32)
        cacc = mview(34944, (2, 256), F32)
        szT = mview(36992, (2, 8, 256), BF16)
        memx = mview(0, (2, D), F32)
        memT = mview(8192, (8, 256), BF16)
        wkv = mview(12288, (8, D), BF16)
        btst = tb[:, :, :].rearrange("p a (h c) -> p (a h) c", c=256)

        PS = st.enter_context(nc.psum_tensor("PS", [128, 8 * 512], F32))
        print("SBUF bytes remaining per partition:", nc.sbuf_bytes_remaining)

        def bank(b, n=512, off=0):
            return PS[:, b * 512 + off: b * 512 + off + n]

        T, A, V, G, SP = nc.tensor, nc.scalar, nc.vector, nc.gpsimd, nc.sync
        BK = lambda b: ('bank', b)

        rot = {'i': 0}

        def gen_bank():
            b = 4 + (rot['i'] % 3)
            rot['i'] += 1
            return b

        S.op('sp', lambda: SP.dma_start(out=ident[:, :], in_=dr["ident"]), writes=['ident'], dma='ident')
        S.op('sp', lambda: SP.dma_start(out=valid_sb[:, :], in_=dr["valid"]), writes=['valid'], dma='valid')
        S.op('sp', lambda: SP.dma_start(out=vrow_sb[:, :, :], in_=dr["vrow"]), writes=['vrow'], dma='vrow')
        S.op('dve', lambda: V.tensor_copy(out=ident_bf[:, :], in_=ident[:, :]), reads=['ident'], writes=['ident_bf'])
        S.op('pool', lambda: G.memset(small[:, 16:17], EPS), writes=['epsc'])

        WCH = ((0, 2048), (2048, 4096), (4096, 5120))

        def wkey(col):
            return ('win', 0 if col < 2048 else (1 if col < 4096 else 2))

        def load_weights(l):
            l = layers.index(l)
            for c, (c0, c1) in enumerate(WCH):
                S.op('pool', lambda l=l, c0=c0, c1=c1: G.dma_start(
                    out=w_in_sb[:, :, c0:c1],
                    in_=dr["w_in"][l, :, c0:c1].rearrange("(k p) n -> p k n", p=128)),
                    writes=[('win', c)], dma=('win', c))

        def load_w_out(l):
            l = layers.index(l)
            S.op('pool', lambda l=l: G.dma_start(
                out=w_out_sb[:, :, :], in_=dr["w_out"][l, :, :].rearrange("(k p) n -> p k n", p=128)),
                writes=[('wout', 0), ('wout', 1)], dma=('wout', 0))

        import os as _os2
        _PROBE2 = _os2.environ.get('KPROBE2', '')

        def load_wkv(l):
            l = layers.index(l)
            S.op('pool', lambda l=l: G.dma_start(
                out=wkv[:, :, :], in_=dr["w_mem_kv"][l, :, :].rearrange("(k p) n -> p k n", p=128)),
                writes=[('wkv', 0), ('wkv', 1)], dma=('wkv', 0))

        def kv_prologue(l):
            l = layers.index(l)
            S.op('sp', lambda: SP.dma_start(out=memx, in_=dr["mem"].rearrange("(j p) d -> p j d", p=128)),
                 writes=['memx'], dma='memx')
            if _PROBE2 == 'c1':
                return
            for j in range(2):
                for half in range(2):
                    b = gen_bank()
                    for q in range(4):
                        kc = half * 4 + q
                        S.op('pe', lambda b=b, q=q, kc=kc, j=j: T.transpose(
                            out=bank(b, 128, q * 128), in_=memx[:, j, kc * 128:(kc + 1) * 128], identity=ident[:, :]),
                            reads=['memx', 'ident'], writes=[BK(b)])
                    S.op('act', lambda b=b, half=half, j=j: A.copy(
                        out=memT[:, half * 4:half * 4 + 4, j * 128:(j + 1) * 128],
                        in_=bank(b).rearrange("p (c t) -> p c t", c=4)),
                        writes=[BK(b), ('memT', j, half)])
            if _PROBE2 == 'c2':
                return
            memT_keys = [('memT', j, half) for j in range(2) for half in range(2)]
            for h in range(4):
                b = gen_bank()
                for kc in range(8):
                    S.op('pe', lambda b=b, h=h, kc=kc: T.matmul(
                        bank(b, 256), lhsT=wkv[:, kc, h * 128:(h + 1) * 128], rhs=memT[:, kc, :],
                        start=(kc == 0), stop=(kc == 7)),
                        reads=[('wkv', 0)] + memT_keys, writes=[BK(b)])
                S.op('act', lambda b=b, h=h: A.copy(out=kmemT[:, h, :], in_=bank(b, 256)),
                     writes=[BK(b), 'kmemT'])
            if _PROBE2 == 'c3':
                return
            for mb in range(2):
                b = gen_bank()
                for kc in range(8):
                    S.op('pe', lambda b=b, mb=mb, kc=kc: T.matmul(
                        bank(b), lhsT=memT[:, kc, mb * 128:(mb + 1) * 128], rhs=wkv[:, kc, 512:1024],
                        start=(kc == 0), stop=(kc == 7)),
                        reads=[('wkv', 1)] + memT_keys, writes=[BK(b)])
                S.op('dve', lambda b=b, mb=mb: V.tensor_copy(
                    out=vmem[:, mb, :, 0:128], in_=bank(b).rearrange("p (h d) -> p h d", h=4)),
                    writes=[BK(b), 'vmem'])
                S.op('pool', lambda mb=mb: G.memset(vmem[:, mb, :, 128:129], 1.0), writes=['vmem'])

        import os as _os
        PROBE = _os.environ.get("KPROBE", "")
        if PROBE != "a":
            load_wkv(layers[0])
            load_weights(layers[0])
            load_w_out(layers[0])

        for li, l in enumerate(layers if PROBE not in ("a", "b") else []):
            attn = (l % 2 == 0)
            la = l // 2
            src = dr["xin"] if li == 0 else scratch[l]
            dst = xout if l == last else scratch[l + 1]

            def dst_rows(t, l=l):
                if l == DEPTH - 1:
                    return (t - HALO) * 128
                return t * 128

            kv_prologue(l)
            S.fence()
            if PROBE == "c":
                break

            S.op('sp', lambda l=li: SP.dma_start(out=lng[:, :], in_=dr["ln_g"][l:l + 1, :].to_broadcast([128, D])),
                 writes=['lng'], dma='lng')
            S.op('sp', lambda l=li: SP.dma_start(out=lnb[:, :], in_=dr["ln_b"][l:l + 1, :].to_broadcast([128, D])),
                 writes=['lnb'], dma='lnb')
            if attn:
                S.op('sp', lambda la=la: SP.dma_start(out=chb_sb[:, :], in_=dr["chb"][la]), writes=['chb'], dma='chb')
                S.op('dve', lambda: V.tensor_scalar(out=small[:, 32:48], in0=chb_sb[:, :], scalar1=-1.0,
                                                    scalar2=None, op0=ALU.mult),
                     reads=['chb'], writes=['negchb'])
                for g in range(4):
                    S.op('sp', lambda la=la, g=g: SP.dma_start(out=btst, in_=dr["bt"][la, :, 4 * g:4 * g + 4, :]),
                         writes=[('tb', 0)], dma='btst')
                    for hh in range(4):
                        h = 4 * g + hh
                        S.op('act', lambda h=h, hh=hh: A.activation(
                            out=Ep[:, h, :], in_=btst[:, hh, :], func=(AF.Identity if BIAS_ON_PE else AF.Exp), bias=small[:, 32 + h:33 + h], scale=1.0),
                            reads=[('tb', 0), 'negchb'], writes=['Ep'])
                S.op('pool', lambda: G.memset(Ep[64:128, :, 128:192], (-30000.0 if BIAS_ON_PE else 0.0)), writes=['Ep'])
                if BIAS_ON_PE:
                    S.op('pool', lambda: G.memset(mask0[:, :], 0.0), writes=['mask0'])
                    S.op('pool', lambda: G.memset(mask0[0:64, 64:128], -30000.0), writes=['mask0'])
            else:
                S.op('sp', lambda la=la: SP.dma_start(out=cw_sb[:, :, :], in_=dr["convw"][la]), writes=['cw'], dma='cw')

            blocks = layer_plan(l)
            if max_blocks is not None:
                blocks = blocks[:max_blocks]
            nb = len(blocks)
            st_l = {'cu_prev': None, 'xslot': {}, 'nload': 0}
            plan_tiles = [t_ for tl_, _m in blocks for t_ in tl_]

            def phase1(bi, l=l, attn=attn, blocks=blocks, src=src, st_l=st_l):
                tiles, mode = blocks[bi]
                n = len(tiles)
                N = 128 * n
                pb = bi % 2
                items = []

                def xunit(ti, t):
                    slot = st_l['xslot'][t]
                    for half in range(2):
                        b = gen_bank()
                        for q in range(4):
                            kc = half * 4 + q
                            S.op('pe', lambda b=b, q=q, kc=kc, slot=slot: T.transpose(
                                out=bank(b, 128, q * 128), in_=xs[:, slot, kc * 128:(kc + 1) * 128], identity=ident[:, :]),
                                reads=[('xs', slot), 'ident'], writes=[BK(b)])
                        if half == 0:
                            S.op('act', lambda b=b, half=half, ti=ti: A.copy(
                                out=xT[:, half * 4:half * 4 + 4, ti * 128:(ti + 1) * 128],
                                in_=bank(b).rearrange("p (c t) -> p c t", c=4)),
                                writes=[BK(b), ('xT', ti, half)])
                        else:
                            S.op('dve', lambda b=b, half=half, ti=ti: V.tensor_copy(
                                out=xT[:, half * 4:half * 4 + 4, ti * 128:(ti + 1) * 128],
                                in_=bank(b).rearrange("p (c t) -> p c t", c=4)),
                                writes=[BK(b), ('xT', ti, half)])
                for ti_, t_ in enumerate(tiles):
                    items.append((xunit, (ti_, t_)))
                xTk = [('xT', ti, half) for ti in range(n) for half in range(2)]

                def tokmajor(ti, col0, evac):
                    b = gen_bank()
                    c = col0 // 512
                    for kc in range(8):
                        S.op('pe', lambda b=b, kc=kc, ti=ti, col0=col0: T.matmul(
                            bank(b), lhsT=xT[:, kc, ti * 128:(ti + 1) * 128], rhs=w_in_sb[:, kc, col0:col0 + 512],
                            start=(kc == 0), stop=(kc == 7)),
                            reads=[('xT', ti, 0), ('xT', ti, 1), wkey(col0)], writes=[BK(b)])
                    evac(b)

                def featmajor(col0, evac):
                    b = gen_bank()
                    c = col0 // 512
                    for kc in range(8):
                        S.op('pe', lambda b=b, kc=kc, col0=col0: T.matmul(
                            bank(b, N), lhsT=w_in_sb[:, kc, col0:col0 + 128], rhs=xT[:, kc, 0:N],
                            start=(kc == 0), stop=(kc == 7)),
                            reads=xTk + [wkey(col0)], writes=[BK(b)])
                    evac(b)

                if attn:
                    for ti, t in enumerate(tiles):
                        vs = t % 8
                        for hb in range(2):
                            def ev(b, t=t, vs=vs, hb=hb):
                                S.op('dve', lambda: V.tensor_scalar(
                                    out=Vr[:, vs, hb * 8:(hb + 1) * 8, 0:64],
                                    in0=bank(b).rearrange("p (h d) -> p h d", h=8),
                                    scalar1=valid_sb[:, t:t + 1], scalar2=None, op0=ALU.mult),
                                    reads=['valid'], writes=[BK(b), ('V', vs)])
                            items.append((tokmajor, (ti, 2048 + hb * 512, ev)))
                        S.op('pool', lambda vs=vs, t=t: G.tensor_copy(
                            out=Vr[:, vs, :, 64:65], in_=valid_sb[:, t:t + 1].unsqueeze(1).to_broadcast([128, 16, 1])),
                            reads=['valid'], writes=[('V', vs)])
                    for j in range(8):
                        def ev(b, j=j):
                            for ti, t in enumerate(tiles):
                                ks = t % 8
                                if (j + ti) % 2 == 0:
                                    S.op('act', lambda ti=ti, ks=ks: A.copy(
                                        out=KTr[:, j, ks * 128:(ks + 1) * 128], in_=bank(b, 128, ti * 128)),
                                        writes=[BK(b), ('K', ks)])
                                else:
                                    S.op('dve', lambda ti=ti, ks=ks: V.tensor_copy(
                                        out=KTr[:, j, ks * 128:(ks + 1) * 128], in_=bank(b, 128, ti * 128)),
                                        writes=[BK(b), ('K', ks)])
                        items.append((featmajor, (1024 + j * 128, ev)))
                    if mode == 'full':
                        for j in range(8):
                            def ev(b, j=j):
                                S.op('act', lambda: A.mul(out=QT[:, pb, j, 0:N], in_=bank(b, N), mul=0.125),
                                     writes=[BK(b), ('QT', pb)])
                            items.append((featmajor, (j * 128, ev)))
                else:
                    cb = bi % 2
                    prev = st_l['cu_prev']
                    if prev is not None:
                        pcb, pN, pt = prev
                        S.op('pool', lambda cb=cb, pcb=pcb, pN=pN, pt=pt: G.tensor_tensor(
                            out=cu[:, cb, :, 0:2], in0=cu[:, pcb, :, pN:pN + 2],
                            in1=vrow_sb[:, pt:pt + 1, :].to_broadcast([128, 8, 2]), op=ALU.mult),
                            reads=[('cu', pcb), 'vrow'], writes=[('cu', cb)])
                    else:
                        S.op('pool', lambda cb=cb: G.memset(cu[:, cb, :, 0:2], 0.0), writes=[('cu', cb)])
                    for j in range(8):
                        pj = j % 2

                        def ev1(b, pj=pj):
                            S.op('act', lambda: A.copy(out=p1s[:, pj, 0:N], in_=bank(b, N)),
                                 writes=[BK(b), ('p1s', pj)])
                        items.append((featmajor, (1024 + j * 128, ev1)))

                        def ev2(b, j=j, pj=pj, cb=cb):
                            S.op('dve', lambda: V.tensor_tensor(out=cu[:, cb, j, 2:2 + N], in0=bank(b, N),
                                                                in1=p1s[:, pj, 0:N], op=ALU.mult),
                                 reads=[('p1s', pj)], writes=[BK(b), ('cu', cb)])
                        items.append((featmajor, (2048 + j * 128, ev2)))
                    st_l['cu_prev'] = (cb, N, tiles[-1])
                    if mode == 'full':
                        for j in range(8):
                            def ev(b, j=j):
                                S.op('act', lambda: A.copy(out=p0s[:, pb, j, 0:N], in_=bank(b, N)),
                                     writes=[BK(b), ('p0s', pb, j)])
                            items.append((featmajor, (j * 128, ev)))

                            def evz(b, j=j):
                                S.op('act', lambda: A.activation(out=szT[:, pb, j, 0:N], in_=bank(b, N), func=AF.Silu),
                                     writes=[BK(b), ('szT', pb, j)])
                            items.append((featmajor, (3584 + j * 128, evz)))
                if mode == 'full':
                    for j in range(4):
                        def ev(b, j=j):
                            S.op('act', lambda: A.mul(out=QmT[:, pb, j, 0:N], in_=bank(b, N), mul=1.0 / math.sqrt(128.0)),
                                 writes=[BK(b), ('QmT', pb)])
                        items.append((featmajor, (3072 + j * 128, ev)))
                    zblocks = (0, 1, 2) if attn else (2,)
                    for ti, t in enumerate(tiles):
                        for zb in zblocks:
                            def ev(b, ti=ti, zb=zb):
                                S.op('act', lambda: A.activation(out=siluz[:, pb, ti, zb * 512:(zb + 1) * 512],
                                                                 in_=bank(b), func=AF.Silu),
                                     writes=[BK(b), ('siluz', pb, ti)])
                            items.append((tokmajor, (ti, 3584 + zb * 512, ev)))
                return items

            def phase2(bi, nxt, l=l, attn=attn, blocks=blocks, dst=dst, src=src, st_l=st_l, plan_tiles=plan_tiles):
                tiles, mode = blocks[bi]
                if mode != 'full':
                    for f_, a_ in nxt:
                        f_(*a_)
                    return
                pts = {'left': (22 if attn else 6) * len(tiles)}

                def fill():
                    left = max(pts['left'], 1)
                    k = -(-len(nxt) // left)
                    for _ in range(min(k, len(nxt))):
                        f_, a_ = nxt.pop(0)
                        f_(*a_)
                    pts['left'] -= 1
                _d2 = _os.environ.get("KDBG2")
                if _d2 and int(_os.environ["KDBG"]) == tiles[0]:
                    if _d2 == 'xT':
                        S.op('sp', lambda: SP.dma_start(out=dbg2, in_=xT[:, :, :].rearrange("p a b -> p (a b)")),
                             reads=[('xT', 0, 0), ('xT', 0, 1), ('xT', 1, 0), ('xT', 1, 1)], writes=[('xd', l + 1, 1)], dma='gdbg2')
                    elif _d2 == 'w':
                        S.op('sp', lambda: SP.dma_start(out=dbg2, in_=w_in_sb[:, 0, 3072:5120]),
                             reads=[('win', 1), ('win', 2)], writes=[('xd', l + 1, 1)], dma='gdbg2')
                    elif _d2 == 'QT':
                        S.op('sp', lambda: SP.dma_start(out=dbg2.rearrange("p (a b) -> p a b", a=8), in_=QT[:, bi % 2, :, :]),
                             reads=[('QT', bi % 2)], writes=[('xd', l + 1, 1)], dma='gdbg2')
                    elif _d2 == 'K':
                        S.op('sp', lambda: SP.dma_start(out=dbg2[:, 0:1024].rearrange("p (a b) -> p a b", a=8),
                                                        in_=KTr[:, :, (tiles[0] % 8) * 128:(tiles[0] % 8) * 128 + 128]),
                             reads=[('K', tiles[0] % 8)], writes=[('xd', l + 1, 1)], dma='gdbg2')
                    elif _d2 == 'QmT':
                        S.op('sp', lambda: SP.dma_start(out=dbg2[:, 0:1024].rearrange("p (a b) -> p a b", a=4), in_=QmT[:, bi % 2, :, :]),
                             reads=[('QmT', bi % 2)], writes=[('xd', l + 1, 1)], dma='gdbg2')
                    elif _d2 == 'V':
                        S.op('sp', lambda: SP.dma_start(out=dbg2[:, 0:1040].rearrange("p (a b) -> p a b", a=16), in_=Vr[:, tiles[0] % 8, :, :]),
                             reads=[('V', tiles[0] % 8)], writes=[('xd', l + 1, 1)], dma='gdbg2')
                    elif _d2 == 'sz':
                        S.op('sp', lambda: SP.dma_start(out=dbg2[:, 0:1536], in_=siluz[:, bi % 2, 0, :]),
                             reads=[('siluz', bi % 2, 0)], writes=[('xd', l + 1, 1)], dma='gdbg2')
                n = len(tiles)
                N = 128 * n
                pb = bi % 2
                if not attn:
                    cb = bi % 2
                    for j in range(8):
                        aj = j % 2
                        S.op('dve', lambda j=j, aj=aj: V.tensor_scalar(
                            out=cacc[:, aj, 0:N], in0=cu[:, cb, j, 0:N], scalar1=cw_sb[:, j, 0:1], scalar2=None,
                            op0=ALU.mult), reads=[('cu', cb), 'cw'], writes=[('cacc', aj)])
                        for k in (1, 2):
                            S.op('dve', lambda j=j, aj=aj, k=k: V.scalar_tensor_tensor(
                                out=cacc[:, aj, 0:N], in0=cu[:, cb, j, k:k + N], scalar=cw_sb[:, j, k:k + 1],
                                in1=cacc[:, aj, 0:N], op0=ALU.mult, op1=ALU.add),
                                reads=[('cu', cb), 'cw'], writes=[('cacc', aj)])
                        S.op('pool', lambda j=j, aj=aj: G.tensor_tensor(
                            out=cacc[:, aj, 0:N], in0=cacc[:, aj, 0:N], in1=p0s[:, pb, j, 0:N], op=ALU.mult),
                            reads=[('p0s', pb, j)], writes=[('cacc', aj)])
                        S.op('pool', lambda j=j, aj=aj: G.tensor_tensor(
                            out=szT[:, pb, j, 0:N], in0=cacc[:, aj, 0:N], in1=szT[:, pb, j, 0:N], op=ALU.mult),
                            reads=[('cacc', aj)], writes=[('szT', pb, j)])
                def tile_body(ti, t):
                    def ln_handoff():
                        if st_l.get('ln_tail') is not None:
                            st_l['ln_tail']()
                            st_l['ln_tail'] = None
                        S.op('sp', lambda t=t: SP.dma_start(out=tb[:, 0, :], in_=src[t * 128:(t + 1) * 128, :]),
                             reads=[('xd', l, t)], writes=[('tb', 0)], dma=('tbi', 0))
                    if attn:
                        def scores(h):
                            sbuf = h % 2
                            hp = h % 2
                            ch = h // 2
                            for j in range(5):
                                ks = (t - 4 + j) % 8
                                hasb = BIAS_ON_PE and j in (0, 3, 4)
                                S.op('pe', lambda sbuf=sbuf, j=j, ks=ks, hp=hp, ch=ch, hasb=hasb: T.matmul(
                                    PS[:, sbuf * 1024 + j * 128: sbuf * 1024 + (j + 1) * 128],
                                    lhsT=KTr[hp * 64:(hp + 1) * 64, ch, ks * 128:(ks + 1) * 128],
                                    rhs=QT[hp * 64:(hp + 1) * 64, pb, ch, ti * 128:(ti + 1) * 128],
                                    start=True, stop=(not hasb)),
                                    reads=[('K', ks), ('QT', pb)], writes=[BK(2 * sbuf), BK(2 * sbuf + 1)])
                                if hasb:
                                    if j == 0:
                                        brhs = mask0[:, :]
                                        bk = 'mask0'
                                    else:
                                        brhs = Ep[:, h, (j - 3) * 128:(j - 2) * 128]
                                        bk = 'Ep'
                                    S.op('pe', lambda sbuf=sbuf, j=j, brhs=brhs: T.matmul(
                                        PS[:, sbuf * 1024 + j * 128: sbuf * 1024 + (j + 1) * 128],
                                        lhsT=ident_bf[:, :], rhs=brhs, start=False, stop=True),
                                        reads=[bk, 'ident_bf'], writes=[BK(2 * sbuf), BK(2 * sbuf + 1)])
                            pbuf = h % 3
                            S.op('act', lambda sbuf=sbuf, pbuf=pbuf: A.activation(
                                out=PT[:, pbuf, :], in_=PS[:, sbuf * 1024: sbuf * 1024 + 640], func=AF.Exp),
                                writes=[BK(2 * sbuf), BK(2 * sbuf + 1), ('PT', pbuf)])
                            if not BIAS_ON_PE:
                                S.op('dve', lambda pbuf=pbuf, h=h: V.tensor_tensor(
                                    out=PT[:, pbuf, 384:640], in0=PT[:, pbuf, 384:640], in1=Ep[:, h, :], op=ALU.mult),
                                    reads=['Ep'], writes=[('PT', pbuf)])
                                S.op('dve', lambda pbuf=pbuf: V.memset(PT[0:64, pbuf, 64:128], 0.0),
                                     writes=[('PT', pbuf)])

                        pvb = {'b': None}

                        def pv(h):
                            hg = h % 4
                            if hg == 0:
                                pvb['b'] = 7
                            b = pvb['b']
                            pbuf = h % 3
                            for j in range(5):
                                vs = (t - 4 + j) % 8
                                S.op('pe', lambda b=b, hg=hg, j=j, vs=vs, pbuf=pbuf, h=h: T.matmul(
                                    bank(b, 65, hg * 65), lhsT=PT[:, pbuf, j * 128:(j + 1) * 128],
                                    rhs=Vr[:, vs, h, :], start=(j == 0), stop=(j == 4)),
                                    reads=[('PT', pbuf), ('V', vs)], writes=[BK(b)])
                            if hg == 3:
                                h0 = h - 3
                                gb = (h // 4) % 2
                                pv4 = bank(b, 260).rearrange("p (h d) -> p h d", h=4)
                                S.op('dve', lambda pv4=pv4: V.tensor_scalar(
                                    out=small[:, 0:4].unsqueeze(2), in0=pv4[:, :, 64:65], scalar1=1e-30, scalar2=None,
                                    op0=ALU.add), writes=[BK(b), 'rs'])
                                S.op('dve', lambda: V.reciprocal(out=small[:, 0:4], in_=small[:, 0:4]), writes=['rs'])
                                S.op('dve', lambda pv4=pv4, gb=gb: V.tensor_tensor(
                                    out=gtmp[:, gb, :].rearrange("p (h d) -> p h d", h=4), in0=pv4[:, :, 0:64],
                                    in1=small[:, 0:4].unsqueeze(2).to_broadcast([128, 4, 64]), op=ALU.mult),
                                    reads=['rs'], writes=[BK(b), ('gtmp', gb)])
                                S.op('pool', lambda gb=gb, h0=h0: G.tensor_tensor(
                                    out=siluz[:, pb, ti, h0 * 64:(h0 + 4) * 64], in0=gtmp[:, gb, :],
                                    in1=siluz[:, pb, ti, h0 * 64:(h0 + 4) * 64], op=ALU.mult),
                                    reads=[('gtmp', gb)], writes=[('siluz', pb, ti)])

                        scores(0)
                        scores(1)
                        for h in range(16):
                            if h + 2 < 16:
                                scores(h + 2)
                            fill()
                            pv(h)
                            if h == 3:
                                ln_handoff()
                    mbank = {}

                    def mscore(hm):
                        b = gen_bank()
                        mbank[hm] = b
                        for mb in range(2):
                            S.op('pe', lambda b=b, mb=mb, hm=hm: T.matmul(
                                bank(b, 128, mb * 128), lhsT=kmemT[:, hm, mb * 128:(mb + 1) * 128],
                                rhs=QmT[:, pb, hm, ti * 128:(ti + 1) * 128], start=True, stop=True),
                                reads=['kmemT', ('QmT', pb)], writes=[BK(b)])
                        mbuf = hm % 2
                        S.op('act', lambda b=b, mbuf=mbuf: A.activation(out=PTm[:, mbuf, :], in_=bank(b, 256), func=AF.Exp),
                             writes=[BK(b), ('PTm', mbuf)])

                    def mpv(hm):
                        b = mbank[hm]
                        mbuf = hm % 2
                        for mb in range(2):
                            S.op('pe', lambda b=b, mb=mb, hm=hm, mbuf=mbuf: T.matmul(
                                bank(b, 129, 256), lhsT=PTm[:, mbuf, mb * 128:(mb + 1) * 128],
                                rhs=vmem[:, mb, hm, :], start=(mb == 0), stop=(mb == 1)),
                                reads=[('PTm', mbuf), 'vmem'], writes=[BK(b)])
                        S.op('dve', lambda b=b: V.tensor_scalar(
                            out=small[:, 8:9], in0=bank(b, 1, 256 + 128), scalar1=1e-30, scalar2=None, op0=ALU.add),
                            writes=[BK(b), 'rsm'])
                        S.op('dve', lambda: V.reciprocal(out=small[:, 8:9], in_=small[:, 8:9]), writes=['rsm'])
                        S.op('dve', lambda b=b, hm=hm: V.scalar_tensor_tensor(
                            out=siluz[:, pb, ti, 1024 + hm * 128:1024 + (hm + 1) * 128], in0=bank(b, 128, 256),
                            scalar=small[:, 8:9], in1=siluz[:, pb, ti, 1024 + hm * 128:1024 + (hm + 1) * 128],
                            op0=ALU.mult, op1=ALU.mult),
                            reads=['rsm'], writes=[BK(b), ('siluz', pb, ti)])

                    if not attn:
                        ln_handoff()
                    mscore(0)
                    for hm in range(4):
                        if hm + 1 < 4:
                            mscore(hm + 1)
                        mpv(hm)
                        fill()
                    if _os.environ.get("KDBG") and int(_os.environ["KDBG"]) == t:
                        S.op('sp', lambda: SP.dma_start(out=dbg_out, in_=siluz[:, pb, ti, :]),
                             reads=[('siluz', pb, ti)], writes=[('xd', l + 1, 0)], dma='gdbg')
                    if _os.environ.get("KDBG2") == 'PT' and int(_os.environ["KDBG"]) == t:
                        S.op('sp', lambda: SP.dma_start(out=dbg2[:, 0:1280].rearrange("p (a b) -> p a b", a=2), in_=PT[:, :, :]),
                             reads=[('PT', 0), ('PT', 1)], writes=[('xd', l + 1, 1)], dma='gdbg2')
                        S.op('sp', lambda: SP.dma_start(out=dbg2[:, 1280:1792].rearrange("p (a b) -> p a b", a=2), in_=PTm[:, :, :]),
                             reads=[('PTm', 0), ('PTm', 1)], writes=[('xd', l + 1, 2)], dma='gdbg3')
                    chunks = list(range(12)) if attn else list(range(8, 12))
                    for g0 in range(0, len(chunks), 4):
                        grp = chunks[g0:g0 + 4]
                        b = gen_bank()
                        psb = bank(b).bitcast(BF16)
                        for q, j in enumerate(grp):
                            S.op('pe', lambda q=q, j=j, psb=psb: T.transpose(
                                out=psb[:, q * 128:(q + 1) * 128], in_=siluz[:, pb, ti, j * 128:(j + 1) * 128],
                                identity=ident_bf[:, :]),
                                reads=[('siluz', pb, ti), 'ident_bf'], writes=[BK(b)])
                        j0 = grp[0]
                        S.op('dve', lambda psb=psb, j0=j0: V.tensor_copy(
                            out=yT[:, j0:j0 + 4, :], in_=psb[:, 0:512].rearrange("p (c t) -> p c t", c=4)),
                            writes=[BK(b), ('yT', j0 // 4)])
                    fill()
                    tslot = 0
                    for half in range(2):
                        b = gen_bank()
                        for ec in range(12):
                            if attn or ec >= 8:
                                lh = yT[:, ec, :]
                                rk = [('yT', ec // 4)]
                            else:
                                lh = szT[:, pb, ec, ti * 128:(ti + 1) * 128]
                                rk = [('szT', pb, ec)]
                            S.op('pe', lambda b=b, ec=ec, half=half, lh=lh: T.matmul(
                                bank(b), lhsT=lh, rhs=w_out_sb[:, ec, half * 512:(half + 1) * 512],
                                start=(ec == 0), stop=(ec == 11)),
                                reads=rk + [('wout', half)], writes=[BK(b)])
                        S.op('dve', lambda b=b, half=half, tslot=tslot: V.scalar_tensor_tensor(
                            out=tb[:, tslot, half * 512:(half + 1) * 512], in0=tb[:, tslot, half * 512:(half + 1) * 512],
                            scalar=ALPHA, in1=bank(b), op0=ALU.mult, op1=ALU.add),
                            writes=[BK(b), ('tb', tslot)])
                        S.op('dve', lambda half=half, tslot=tslot: V.bn_stats(
                            out=stats[:, tslot, half, :], in_=tb[:, tslot, half * 512:(half + 1) * 512]),
                            reads=[('tb', tslot)], writes=[('stats', tslot)])
                    def ln_tail(t=t, tslot=tslot):
                        S.op('dve', lambda tslot=tslot: V.bn_aggr(
                            out=mv[:, tslot, 0:2], in_=stats[:, tslot, :, :].rearrange("p a b -> p (a b)")),
                            reads=[('stats', tslot)], writes=[('mv', tslot)])
                        S.op('act', lambda tslot=tslot: A.activation(
                            out=small[:, 20:21], in_=mv[:, tslot, 1:2], func=AF.Ln, bias=small[:, 16:17], scale=1.0),
                            reads=[('mv', tslot), 'epsc'], writes=[('mvb', tslot)])
                        S.op('act', lambda tslot=tslot: A.activation(
                            out=mv[:, tslot, 2:3], in_=small[:, 20:21], func=AF.Exp, scale=-0.5),
                            reads=[('mvb', tslot)], writes=[('mvc', tslot)])
                        S.op('dve', lambda tslot=tslot: V.scalar_tensor_tensor(
                            out=mv[:, tslot, 3:4], in0=mv[:, tslot, 0:1], scalar=-1.0, in1=mv[:, tslot, 2:3],
                            op0=ALU.mult, op1=ALU.mult), reads=[('mv', tslot), ('mvc', tslot)], writes=[('mvd', tslot)])
                        S.op('act', lambda tslot=tslot: A.activation(
                            out=tb[:, tslot, :], in_=tb[:, tslot, :], func=AF.Identity,
                            bias=mv[:, tslot, 3:4], scale=mv[:, tslot, 2:3]),
                            reads=[('mvd', tslot), ('mvc', tslot)], writes=[('tb', tslot)])
                        S.op('pool', lambda tslot=tslot: G.tensor_tensor(
                            out=tb[:, tslot, :], in0=tb[:, tslot, :], in1=lng[:, :], op=ALU.mult),
                            reads=['lng'], writes=[('tb', tslot)])
                        S.op('pool', lambda tslot=tslot: G.tensor_tensor(
                            out=tb[:, tslot, :], in0=tb[:, tslot, :], in1=lnb[:, :], op=ALU.add),
                            reads=['lnb'], writes=[('tb', tslot)])
                        if l == DEPTH - 1 and t < HALO:
                            return
                        r0 = dst_rows(t)
                        S.op('sp', lambda tslot=tslot, r0=r0: SP.dma_start(out=dst[r0:r0 + 128, :], in_=tb[:, tslot, :]),
                             reads=[('tb', tslot)], writes=[('xd', l + 1, t)], dma=('tbo', tslot))
                    st_l['ln_tail'] = ln_tail

                for ti_, t_ in enumerate(tiles):
                    tile_body(ti_, t_)
                    fill()
                for f_, a_ in nxt:
                    f_(*a_)

            def xload(bi, l=l, src=src, st_l=st_l, blocks=blocks):
                if bi >= len(blocks):
                    return
                for t in blocks[bi][0]:
                    slot = st_l['nload'] % NXS
                    st_l['nload'] += 1
                    st_l['xslot'][t] = slot
                    S.op('sp', lambda slot=slot, t=t: SP.dma_start(out=xs[:, slot, :], in_=src[t * 128:(t + 1) * 128, :]),
                         reads=[('xd', l, t)], writes=[('xs', slot)], dma=('xs', slot))

            if nb > 0:
                xload(0)
                xload(1)
                for f_, a_ in phase1(0):
                    f_(*a_)
            for bi in range(nb):
                nxt = phase1(bi + 1) if bi + 1 < nb else []
                xload(bi + 2)
                if bi + 1 == nb and li + 1 < len(layers):
                    load_weights(layers[li + 1])
                phase2(bi, nxt)
            if st_l.get('ln_tail') is not None:
                st_l['ln_tail']()
                st_l['ln_tail'] = None
            S.fence()
            if li + 1 < len(layers):
                load_wkv(layers[li + 1])
                load_w_out(layers[li + 1])

        outkeys = [('xd', last + 1, t) for t in range(NT)]
        S.op('sp', None, reads=outkeys)
        S.finalize(st)
    return nc, S


def _host_layout(inputs):
    x = np.asarray(inputs["x"], dtype=np.float32)
    mem = np.asarray(inputs["mem"], dtype=np.float32)
    rel_bias = np.asarray(inputs["rel_bias"], dtype=np.float32)
    conv_w = np.asarray(inputs["conv_w"], dtype=np.float32)
    k = np.arange(128)[:, None]
    q = np.arange(128)[None, :]
    idx3 = np.minimum(q - k + 128, 128) + 128
    idx4 = (q - k) + 128
    idx = np.concatenate([idx3, idx4], axis=1)
    bt = np.ascontiguousarray(np.transpose(rel_bias[:, :, idx], (0, 2, 1, 3)))
    chb = np.ascontiguousarray(np.broadcast_to(rel_bias[:, None, :, 256], (2, 128, 16)))
    convw = np.ascontiguousarray(np.transpose(conv_w.reshape(2, 3, 8, 128), (0, 3, 2, 1)))
    common = dict(
        ident=np.eye(128, dtype=np.float32),
        w_in=np.asarray(inputs["w_in"], dtype=np.float32),
        w_mem_kv=np.asarray(inputs["w_mem_kv"], dtype=np.float32),
        w_out=np.asarray(inputs["w_out"], dtype=np.float32),
        bt=bt, chb=chb, convw=convw,
        ln_g=np.asarray(inputs["ln_g"], dtype=np.float32),
        ln_b=np.asarray(inputs["ln_b"], dtype=np.float32),
    )
    per_core = []
    for c in range(8):
        b, half = c // 2, c % 2
        s0 = half * 4096
        w0 = s0 - HALO * 128
        xw = np.zeros((NT * 128, D), np.float32)
        lo = max(w0, 0)
        xw[lo - w0:] = x[b, lo:s0 + 4096]
        vt = np.zeros((NT * 128,), np.float32)
        vt[lo - w0:] = 1.0
        valid = np.ascontiguousarray(vt.reshape(NT, 128).T)
        vrow = np.ascontiguousarray(np.broadcast_to(vt.reshape(NT, 128)[None, :, 126:128], (128, NT, 2)))
        m = dict(common)
        m.update(xin=xw, valid=valid, vrow=vrow, mem=np.ascontiguousarray(mem[b]))
        per_core.append(m)
    return per_core


_CACHE = {}


def _get_program(layers):
    key = tuple(layers)
    if key not in _CACHE:
        _CACHE[key] = build_program(list(layers))[0]
    return _CACHE[key]


FUSED = True


def kernel(**inputs):
    per_core = _host_layout(inputs)
    groups = [[0, 1, 2, 3]] if FUSED else [[0], [1], [2], [3]]
    full = {k: per_core[0][k] for k in ("w_in", "w_mem_kv", "w_out", "ln_g", "ln_b")}
    for grp in groups:
        nc = _get_program(grp)
        sl = {k: np.ascontiguousarray(v[grp[0]:grp[-1] + 1]) for k, v in full.items()}
        maps = []
        for c in range(8):
            m = dict(per_core[c])
            m.update(sl)
            maps.append(m)
        res = run_bass_kernel_spmd(nc, maps, core_ids=list(range(8)))
        outs = [r["xout"] for r in res.results]
        if grp[-1] != DEPTH - 1:
            for c in range(8):
                per_core[c]["xin"] = outs[c]
    out = np.zeros((4, 8192, D), np.float32)
    for c in range(8):
        b, half = c // 2, c % 2
        out[b, half * 4096:(half + 1) * 4096] = outs[c]
    return out
```

```python
import math
from contextlib import ExitStack

import numpy as np
import concourse.bass as bass
import concourse.mybir as mybir
from concourse.bass_utils import run_bass_kernel_spmd

F32 = mybir.dt.float32
BF16 = mybir.dt.bfloat16
AF = mybir.ActivationFunctionType
ALU = mybir.AluOpType

D = 1024
NIN = 5120
NT = 42
HALO = 10
NMAIN = 32
DEPTH = 4
ALPHA = (2.0 * DEPTH) ** 0.25
EPS = 1e-5
NXS = 4
BIAS_ON_PE = False

ENG_ATTR = {'pe': 'tensor', 'act': 'scalar', 'dve': 'vector', 'pool': 'gpsimd', 'sp': 'sync'}


class Op:
    __slots__ = ('eng', 'fn', 'reads', 'writes', 'dma', 'waits', 'signal', 'ev', 'vc', 'sigcount')

    def __init__(self, eng, fn, reads, writes, dma):
        self.eng = eng
        self.fn = fn
        self.reads = reads
        self.writes = writes
        self.dma = dma
        self.waits = []
        self.signal = False
        self.ev = None
        self.vc = None
        self.sigcount = 0


class Sched:
    def __init__(self, nc, same_engine_sync=('act', 'dve', 'pool')):
        self.nc = nc
        self.ops = []
        self.same_sync = set(same_engine_sync)

    def op(self, eng, fn, reads=(), writes=(), dma=None):
        self.ops.append(Op(eng, fn, tuple(reads), tuple(writes), dma))

    def fence(self):
        shared = {}
        for e in ENG_ATTR:
            self.ops.append(Op(e, None, ('__fence__', shared), (), None))

    def finalize(self, stack):
        nc = self.nc
        state = {}
        vcs = {e: {} for e in ENG_ATTR}
        eng_count = {e: 0 for e in ENG_ATTR}
        dma_count = {}
        evop = {}
        dmavc = {}
        for op in self.ops:
            e = op.eng
            deps = {}
            if len(op.reads) == 2 and op.reads[0] == '__fence__':
                shared = op.reads[1]
                if not shared:
                    for e2, c2 in eng_count.items():
                        if c2 > 0:
                            shared[('e', e2)] = c2 - 1
                    for dk2, c2 in dma_count.items():
                        shared[dk2] = c2
                deps.update(shared)
                op.reads = ()
            for r in op.reads:
                st = state.get(r)
                if st:
                    for (k, i) in st[0]:
                        if deps.get(k, -1) < i:
                            deps[k] = i
            for w in op.writes:
                st = state.get(w)
                if st:
                    for (k, i) in st[0]:
                        if deps.get(k, -1) < i:
                            deps[k] = i
                    for (k, i) in st[1]:
                        if deps.get(k, -1) < i:
                            deps[k] = i
            vc = vcs[e]
            myk = ('e', e)
            for k, i in deps.items():
                if k == myk and e not in self.same_sync:
                    continue
                if vc.get(k, -1) >= i:
                    continue
                op.waits.append((k, i))
            for (k, i) in op.waits:
                if k[0] == 'e':
                    src = evop[(k, i)]
                    src.signal = True
                    svc = src.vc
                else:
                    svc = dmavc[(k, i)]
                for kk, ii in svc.items():
                    if vc.get(kk, -1) < ii:
                        vc[kk] = ii
                if vc.get(k, -1) < i:
                    vc[k] = i
            if op.fn is None:
                op.ev = None
                continue
            if op.dma is None:
                idx = eng_count[e]
                eng_count[e] = idx + 1
                op.ev = (myk, idx)
                evop[op.ev] = op
                snap = dict(vc)
                snap[myk] = idx
                op.vc = snap
                if e not in self.same_sync:
                    vc[myk] = idx
            else:
                dk = ('d', op.dma)
                c = dma_count.get(dk, 0) + 1
                dma_count[dk] = c
                op.ev = (dk, c)
                dmavc[op.ev] = dict(vc)
            for r in op.reads:
                st = state.setdefault(r, [[], []])
                rl = [x for x in st[1] if x[0] != op.ev[0]]
                rl.append(op.ev)
                st[1] = rl
            for w in op.writes:
                state[w] = [[op.ev], []]
        sems = {}
        for e in ENG_ATTR:
            sems[('e', e)] = stack.enter_context(nc.semaphore('sem_' + e))
        for dk in dma_count:
            sems[dk] = stack.enter_context(nc.semaphore('dsem_%d' % len(sems)))
        cnt = {e: 0 for e in ENG_ATTR}
        for op in self.ops:
            if op.dma is None and op.signal:
                cnt[op.eng] += 1
                op.sigcount = cnt[op.eng]
        for op in self.ops:
            engobj = getattr(nc, ENG_ATTR[op.eng])
            for (k, i) in op.waits:
                val = evop[(k, i)].sigcount if k[0] == 'e' else 16 * i
                engobj.wait_ge(sems[k], val)
            if op.fn is None:
                continue
            ins = op.fn()
            if op.dma is not None:
                ins.then_inc(sems[op.ev[0]], 16)
            elif op.signal:
                ins.then_inc(sems[('e', op.eng)], 1)
        self.stats = dict(nops=len(self.ops), sig=cnt, nsems=len(sems))


def layer_plan(l):
    if l == 0:
        blocks = [((0, 1), 'kv'), ((2, 3), 'kv')] + [((t, t + 1), 'full') for t in range(4, NT, 2)]
    elif l == 1:
        blocks = [((4,), 'cu'), ((5,), 'full')] + [((t, t + 1), 'full') for t in range(6, NT, 2)]
    elif l == 2:
        blocks = [((5, 6), 'kv'), ((7, 8), 'kv'), ((9,), 'full')] + \
                 [((t, t + 1), 'full') for t in range(10, NT, 2)]
    else:
        blocks = [((9,), 'cu')] + [((t, t + 1), 'full') for t in range(10, NT, 2)]
    return blocks


def build_program(layers, same_engine_sync=('act', 'dve', 'pool'), max_blocks=None):
    nc = bass.Bass("TRN2", target_bir_lowering=False, dynamic_dma_scratch_size=8192)
    last = layers[-1]
    final = (last == DEPTH - 1)
    dr = {}

    def din(name, shape):
        dr[name] = nc.dram_tensor(name, list(shape), F32, kind="ExternalInput").ap()

    din("xin", (NT * 128, D))
    din("valid", (128, NT))
    din("vrow", (128, NT, 2))
    din("mem", (256, D))
    din("ident", (128, 128))
    din("w_in", (len(layers), D, NIN))
    din("w_mem_kv", (len(layers), D, D))
    din("w_out", (len(layers), 1536, D))
    din("bt", (2, 128, 16, 256))
    din("chb", (2, 128, 16))
    din("convw", (2, 128, 8, 3))
    din("ln_g", (len(layers), D))
    din("ln_b", (len(layers), D))
    if final:
        xout = nc.dram_tensor("xout", [NMAIN * 128, D], F32, kind="ExternalOutput").ap()
    else:
        xout = nc.dram_tensor("xout", [NT * 128, D], F32, kind="ExternalOutput").ap()
    scratch = {}
    for l in layers[:-1]:
        scratch[l + 1] = nc.dram_tensor("xs%d" % (l + 1), [NT * 128, D], F32).ap()

    S = Sched(nc, same_engine_sync)
    with ExitStack() as st:
        def sb(name, shape, dt):
            return st.enter_context(nc.sbuf_tensor("sb_" + name, list(shape), dt))

        w_in_sb = sb("w_in_sb", (128, 8, NIN), BF16)
        w_out_sb = sb("w_out_sb", (128, 12, D), BF16)
        kmemT = sb("kmemT", (128, 4, 256), BF16)
        vmem = sb("vmem", (128, 2, 4, 129), BF16)
        valid_sb = sb("valid_sb", (128, NT), F32)
        vrow_sb = sb("vrow_sb", (128, NT, 2), F32)
        ident = sb("ident", (128, 128), F32)
        ident_bf = sb("ident_bf", (128, 128), BF16)
        lng = sb("lng", (128, D), F32)
        lnb = sb("lnb", (128, D), F32)
        xs = sb("xs", (128, NXS, D), F32)
        xT = sb("xT", (128, 8, 256), BF16)
        tb = sb("tb", (128, 1, D), F32)
        siluz = sb("siluz", (128, 2, 2, 1536), BF16)
        yT = sb("yT", (128, 12, 128), BF16)
        QmT = sb("QmT", (128, 2, 4, 256), BF16)
        PTm = sb("PTm", (128, 2, 256), BF16)
        small = sb("small", (128, 64), F32)
        stats = sb("stats", (128, 2, 2, 6), F32)
        mv = sb("mv", (128, 2, 4), F32)
        gtmp = sb("gtmp", (128, 2, 256), F32)
        chb_sb = sb("chb_sb", (128, 16), F32)
        mask0 = sb("mask0", (128, 128), BF16) if BIAS_ON_PE else None
        import os as _os
        dbg_out = nc.dram_tensor("dbg", [128, 1536], BF16, kind="ExternalOutput").ap() if _os.environ.get("KDBG") else None
        dbg2 = nc.dram_tensor("dbg2", [128, 2048], BF16, kind="ExternalOutput").ap() if _os.environ.get("KDBG2") else None
        cw_sb = sb("cw_sb", (128, 8, 3), F32)
        MIXB = 53248
        MIX = sb("MIX", (128, MIXB // 2), BF16)

        def mview(off, shape, dt):
            esz = 4 if dt == F32 else 2
            n = 1
            for d_ in shape:
                n *= d_
            v = MIX[:, off // 2: off // 2 + (n * esz) // 2]
            if dt == F32:
                v = v.bitcast(F32)
            if len(shape) == 1:
                return v
            names = "abcd"[:len(shape)]
            pat = "p (%s) -> p %s" % (" ".join(names), " ".join(names))
            kw = {names[i]: shape[i] for i in range(len(shape) - 1)}
            return v.rearrange(pat, **kw)

        QT = mview(0, (2, 8, 256), BF16)
        KTr = mview(8192, (8, 1024), BF16)
        Vr = mview(24576, (8, 16, 65), BF16)
        PT = mview(41216, (3, 640), BF16)
        Ep = mview(45056, (16, 256), BF16)
        cu = mview(0, (2, 8, 258), F32)
        p0s = mview(16512, (2, 8, 256), F32)
        p1s = mview(32896, (2, 256), F32)
        cacc = mview(34944, (2, 256), F32)
        szT = mview(36992, (2, 8, 256), BF16)
        memx = mview(0, (2, D), F32)
        memT = mview(8192, (8, 256), BF16)
        wkv = mview(12288, (8, D), BF16)
        btst = tb[:, :, :].rearrange("p a (h c) -> p (a h) c", c=256)

        PS = st.enter_context(nc.psum_tensor("PS", [128, 8 * 512], F32))
        print("SBUF bytes remaining per partition:", nc.sbuf_bytes_remaining)

        def bank(b, n=512, off=0):
            return PS[:, b * 512 + off: b * 512 + off + n]

        T, A, V, G, SP = nc.tensor, nc.scalar, nc.vector, nc.gpsimd, nc.sync
        BK = lambda b: ('bank', b)

        rot = {'i': 0}

        def gen_bank():
            b = 4 + (rot['i'] % 3)
            rot['i'] += 1
            return b

        S.op('sp', lambda: SP.dma_start(out=ident[:, :], in_=dr["ident"]), writes=['ident'], dma='ident')
        S.op('sp', lambda: SP.dma_start(out=valid_sb[:, :], in_=dr["valid"]), writes=['valid'], dma='valid')
        S.op('sp', lambda: SP.dma_start(out=vrow_sb[:, :, :], in_=dr["vrow"]), writes=['vrow'], dma='vrow')
        S.op('dve', lambda: V.tensor_copy(out=ident_bf[:, :], in_=ident[:, :]), reads=['ident'], writes=['ident_bf'])
        S.op('pool', lambda: G.memset(small[:, 16:17], EPS), writes=['epsc'])
        S.op('pool', lambda: G.memset(small[:, 17:18], -0.5), writes=['epsc'])

        WCH = ((0, 2048), (2048, 4096), (4096, 5120))

        def wkey(col):
            return ('win', 0 if col < 2048 else (1 if col < 4096 else 2))

        def load_weights(l):
            l = layers.index(l)
            for c, (c0, c1) in enumerate(WCH):
                S.op('pool', lambda l=l, c0=c0, c1=c1: G.dma_start(
                    out=w_in_sb[:, :, c0:c1],
                    in_=dr["w_in"][l, :, c0:c1].rearrange("(k p) n -> p k n", p=128)),
                    writes=[('win', c)], dma=('win', c))

        def load_w_out(l):
            l = layers.index(l)
            S.op('pool', lambda l=l: G.dma_start(
                out=w_out_sb[:, :, :], in_=dr["w_out"][l, :, :].rearrange("(k p) n -> p k n", p=128)),
                writes=[('wout', 0), ('wout', 1)], dma=('wout', 0))

        import os as _os2
        _PROBE2 = _os2.environ.get('KPROBE2', '')

        def load_wkv(l):
            l = layers.index(l)
            S.op('pool', lambda l=l: G.dma_start(
                out=wkv[:, :, :], in_=dr["w_mem_kv"][l, :, :].rearrange("(k p) n -> p k n", p=128)),
                writes=[('wkv', 0), ('wkv', 1)], dma=('wkv', 0))

        def kv_prologue(l):
            l = layers.index(l)
            S.op('sp', lambda: SP.dma_start(out=memx, in_=dr["mem"].rearrange("(j p) d -> p j d", p=128)),
                 writes=['memx'], dma='memx')
            if _PROBE2 == 'c1':
                return
            for j in range(2):
                for half in range(2):
                    b = gen_bank()
                    for q in range(4):
                        kc = half * 4 + q
                        S.op('pe', lambda b=b, q=q, kc=kc, j=j: T.transpose(
                            out=bank(b, 128, q * 128), in_=memx[:, j, kc * 128:(kc + 1) * 128], identity=ident[:, :]),
                            reads=['memx', 'ident'], writes=[BK(b)])
                    S.op('act', lambda b=b, half=half, j=j: A.copy(
                        out=memT[:, half * 4:half * 4 + 4, j * 128:(j + 1) * 128],
                        in_=bank(b).rearrange("p (c t) -> p c t", c=4)),
                        writes=[BK(b), ('memT', j, half)])
            if _PROBE2 == 'c2':
                return
            memT_keys = [('memT', j, half) for j in range(2) for half in range(2)]
            for h in range(4):
                b = gen_bank()
                for kc in range(8):
                    S.op('pe', lambda b=b, h=h, kc=kc: T.matmul(
                        bank(b, 256), lhsT=wkv[:, kc, h * 128:(h + 1) * 128], rhs=memT[:, kc, :],
                        start=(kc == 0), stop=(kc == 7)),
                        reads=[('wkv', 0)] + memT_keys, writes=[BK(b)])
                S.op('act', lambda b=b, h=h: A.copy(out=kmemT[:, h, :], in_=bank(b, 256)),
                     writes=[BK(b), 'kmemT'])
            if _PROBE2 == 'c3':
                return
            for mb in range(2):
                b = gen_bank()
                for kc in range(8):
                    S.op('pe', lambda b=b, mb=mb, kc=kc: T.matmul(
                        bank(b), lhsT=memT[:, kc, mb * 128:(mb + 1) * 128], rhs=wkv[:, kc, 512:1024],
                        start=(kc == 0), stop=(kc == 7)),
                        reads=[('wkv', 1)] + memT_keys, writes=[BK(b)])
                S.op('dve', lambda b=b, mb=mb: V.tensor_copy(
                    out=vmem[:, mb, :, 0:128], in_=bank(b).rearrange("p (h d) -> p h d", h=4)),
                    writes=[BK(b), 'vmem'])
                S.op('pool', lambda mb=mb: G.memset(vmem[:, mb, :, 128:129], 1.0), writes=['vmem'])

        import os as _os
        PROBE = _os.environ.get("KPROBE", "")
        if PROBE != "a":
            load_wkv(layers[0])
            load_weights(layers[0])
            load_w_out(layers[0])

        for li, l in enumerate(layers if PROBE not in ("a", "b") else []):
            attn = (l % 2 == 0)
            la = l // 2
            src = dr["xin"] if li == 0 else scratch[l]
            dst = xout if l == last else scratch[l + 1]

            def dst_rows(t, l=l):
                if l == DEPTH - 1:
                    return (t - HALO) * 128
                return t * 128

            kv_prologue(l)
            S.fence()
            if PROBE == "c":
                break

            S.op('sp', lambda l=li: SP.dma_start(out=lng[:, :], in_=dr["ln_g"][l:l + 1, :].to_broadcast([128, D])),
                 writes=['lng'], dma='lng')
            S.op('sp', lambda l=li: SP.dma_start(out=lnb[:, :], in_=dr["ln_b"][l:l + 1, :].to_broadcast([128, D])),
                 writes=['lnb'], dma='lnb')
            if attn:
                S.op('sp', lambda la=la: SP.dma_start(out=chb_sb[:, :], in_=dr["chb"][la]), writes=['chb'], dma='chb')
                S.op('dve', lambda: V.tensor_scalar(out=small[:, 32:48], in0=chb_sb[:, :], scalar1=-1.0,
                                                    scalar2=None, op0=ALU.mult),
                     reads=['chb'], writes=['negchb'])
                for g in range(4):
                    S.op('sp', lambda la=la, g=g: SP.dma_start(out=btst, in_=dr["bt"][la, :, 4 * g:4 * g + 4, :]),
                         writes=[('tb', 0)], dma='btst')
                    for hh in range(4):
                        h = 4 * g + hh
                        S.op('act', lambda h=h, hh=hh: A.activation(
                            out=Ep[:, h, :], in_=btst[:, hh, :], func=(AF.Identity if BIAS_ON_PE else AF.Exp), bias=small[:, 32 + h:33 + h], scale=1.0),
                            reads=[('tb', 0), 'negchb'], writes=['Ep'])
                S.op('pool', lambda: G.memset(Ep[64:128, :, 128:192], (-30000.0 if BIAS_ON_PE else 0.0)), writes=['Ep'])
                if BIAS_ON_PE:
                    S.op('pool', lambda: G.memset(mask0[:, :], 0.0), writes=['mask0'])
                    S.op('pool', lambda: G.memset(mask0[0:64, 64:128], -30000.0), writes=['mask0'])
            else:
                S.op('sp', lambda la=la: SP.dma_start(out=cw_sb[:, :, :], in_=dr["convw"][la]), writes=['cw'], dma='cw')

            blocks = layer_plan(l)
            if max_blocks is not None:
                blocks = blocks[:max_blocks]
            nb = len(blocks)
            st_l = {'cu_prev': None, 'xslot': {}, 'nload': 0}
            plan_tiles = [t_ for tl_, _m in blocks for t_ in tl_]

            def phase1(bi, l=l, attn=attn, blocks=blocks, src=src, st_l=st_l):
                tiles, mode = blocks[bi]
                n = len(tiles)
                N = 128 * n
                pb = bi % 2
                items = []

                def xunit(ti, t):
                    slot = st_l['xslot'][t]
                    for half in range(2):
                        b = gen_bank()
                        for q in range(4):
                            kc = half * 4 + q
                            S.op('pe', lambda b=b, q=q, kc=kc, slot=slot: T.transpose(
                                out=bank(b, 128, q * 128), in_=xs[:, slot, kc * 128:(kc + 1) * 128], identity=ident[:, :]),
                                reads=[('xs', slot), 'ident'], writes=[BK(b)])
                        if half == 0:
                            S.op('act', lambda b=b, half=half, ti=ti: A.copy(
                                out=xT[:, half * 4:half * 4 + 4, ti * 128:(ti + 1) * 128],
                                in_=bank(b).rearrange("p (c t) -> p c t", c=4)),
                                writes=[BK(b), ('xT', ti, half)])
                        else:
                            S.op('dve', lambda b=b, half=half, ti=ti: V.tensor_copy(
                                out=xT[:, half * 4:half * 4 + 4, ti * 128:(ti + 1) * 128],
                                in_=bank(b).rearrange("p (c t) -> p c t", c=4)),
                                writes=[BK(b), ('xT', ti, half)])
                for ti_, t_ in enumerate(tiles):
                    items.append((xunit, (ti_, t_)))
                xTk = [('xT', ti, half) for ti in range(n) for half in range(2)]

                def tokmajor(ti, col0, evac):
                    b = gen_bank()
                    c = col0 // 512
                    for kc in range(8):
                        S.op('pe', lambda b=b, kc=kc, ti=ti, col0=col0: T.matmul(
                            bank(b), lhsT=xT[:, kc, ti * 128:(ti + 1) * 128], rhs=w_in_sb[:, kc, col0:col0 + 512],
                            start=(kc == 0), stop=(kc == 7)),
                            reads=[('xT', ti, 0), ('xT', ti, 1), wkey(col0)], writes=[BK(b)])
                    evac(b)

                def featmajor(col0, evac):
                    b = gen_bank()
                    c = col0 // 512
                    for kc in range(8):
                        S.op('pe', lambda b=b, kc=kc, col0=col0: T.matmul(
                            bank(b, N), lhsT=w_in_sb[:, kc, col0:col0 + 128], rhs=xT[:, kc, 0:N],
                            start=(kc == 0), stop=(kc == 7)),
                            reads=xTk + [wkey(col0)], writes=[BK(b)])
                    evac(b)

                if attn:
                    for ti, t in enumerate(tiles):
                        vs = t % 8
                        for hb in range(2):
                            def ev(b, t=t, vs=vs, hb=hb):
                                S.op('dve', lambda: V.tensor_scalar(
                                    out=Vr[:, vs, hb * 8:(hb + 1) * 8, 0:64],
                                    in0=bank(b).rearrange("p (h d) -> p h d", h=8),
                                    scalar1=valid_sb[:, t:t + 1], scalar2=None, op0=ALU.mult),
                                    reads=['valid'], writes=[BK(b), ('V', vs)])
                            items.append((tokmajor, (ti, 2048 + hb * 512, ev)))
                        S.op('pool', lambda vs=vs, t=t: G.tensor_copy(
                            out=Vr[:, vs, :, 64:65], in_=valid_sb[:, t:t + 1].unsqueeze(1).to_broadcast([128, 16, 1])),
                            reads=['valid'], writes=[('V', vs)])
                    for j in range(8):
                        def ev(b, j=j):
                            for ti, t in enumerate(tiles):
                                ks = t % 8
                                if (j + ti) % 2 == 0:
                                    S.op('act', lambda ti=ti, ks=ks: A.copy(
                                        out=KTr[:, j, ks * 128:(ks + 1) * 128], in_=bank(b, 128, ti * 128)),
                                        writes=[BK(b), ('K', ks)])
                                else:
                                    S.op('dve', lambda ti=ti, ks=ks: V.tensor_copy(
                                        out=KTr[:, j, ks * 128:(ks + 1) * 128], in_=bank(b, 128, ti * 128)),
                                        writes=[BK(b), ('K', ks)])
                        items.append((featmajor, (1024 + j * 128, ev)))
                    if mode == 'full':
                        for j in range(8):
                            def ev(b, j=j):
                                S.op('act', lambda: A.mul(out=QT[:, pb, j, 0:N], in_=bank(b, N), mul=0.125),
                                     writes=[BK(b), ('QT', pb)])
                            items.append((featmajor, (j * 128, ev)))
                else:
                    cb = bi % 2
                    prev = st_l['cu_prev']
                    if prev is not None:
                        pcb, pN, pt = prev
                        S.op('pool', lambda cb=cb, pcb=pcb, pN=pN, pt=pt: G.tensor_tensor(
                            out=cu[:, cb, :, 0:2], in0=cu[:, pcb, :, pN:pN + 2],
                            in1=vrow_sb[:, pt:pt + 1, :].to_broadcast([128, 8, 2]), op=ALU.mult),
                            reads=[('cu', pcb), 'vrow'], writes=[('cu', cb)])
                    else:
                        S.op('pool', lambda cb=cb: G.memset(cu[:, cb, :, 0:2], 0.0), writes=[('cu', cb)])
                    for j in range(8):
                        pj = j % 2

                        def ev1(b, pj=pj):
                            S.op('act', lambda: A.copy(out=p1s[:, pj, 0:N], in_=bank(b, N)),
                                 writes=[BK(b), ('p1s', pj)])
                        items.append((featmajor, (1024 + j * 128, ev1)))

                        def ev2(b, j=j, pj=pj, cb=cb):
                            S.op('dve', lambda: V.tensor_tensor(out=cu[:, cb, j, 2:2 + N], in0=bank(b, N),
                                                                in1=p1s[:, pj, 0:N], op=ALU.mult),
                                 reads=[('p1s', pj)], writes=[BK(b), ('cu', cb)])
                        items.append((featmajor, (2048 + j * 128, ev2)))
                    st_l['cu_prev'] = (cb, N, tiles[-1])
                    if mode == 'full':
                        for j in range(8):
                            def ev(b, j=j):
                                S.op('act', lambda: A.copy(out=p0s[:, pb, j, 0:N], in_=bank(b, N)),
                                     writes=[BK(b), ('p0s', pb, j)])
                            items.append((featmajor, (j * 128, ev)))

                            def evz(b, j=j):
                                S.op('act', lambda: A.activation(out=szT[:, pb, j, 0:N], in_=bank(b, N), func=AF.Silu),
                                     writes=[BK(b), ('szT', pb, j)])
                            items.append((featmajor, (3584 + j * 128, evz)))
                if mode == 'full':
                    for j in range(4):
                        def ev(b, j=j):
                            S.op('act', lambda: A.mul(out=QmT[:, pb, j, 0:N], in_=bank(b, N), mul=1.0 / math.sqrt(128.0)),
                                 writes=[BK(b), ('QmT', pb)])
                        items.append((featmajor, (3072 + j * 128, ev)))
                    zblocks = (0, 1, 2) if attn else (2,)
                    for ti, t in enumerate(tiles):
                        for zb in zblocks:
                            def ev(b, ti=ti, zb=zb):
                                S.op('act', lambda: A.activation(out=siluz[:, pb, ti, zb * 512:(zb + 1) * 512],
                                                                 in_=bank(b), func=AF.Silu),
                                     writes=[BK(b), ('siluz', pb, ti)])
                            items.append((tokmajor, (ti, 3584 + zb * 512, ev)))
                return items

            def phase2(bi, nxt, l=l, attn=attn, blocks=blocks, dst=dst, src=src, st_l=st_l, plan_tiles=plan_tiles):
                tiles, mode = blocks[bi]
                if mode != 'full':
                    for f_, a_ in nxt:
                        f_(*a_)
                    return
                pts = {'left': (22 if attn else 6) * len(tiles)}

                def fill():
                    for _ in range(2):
                        if conv_items:
                            conv_chunk(conv_items.pop(0))
                    left = max(pts['left'], 1)
                    k = -(-len(nxt) // left)
                    for _ in range(min(k, len(nxt))):
                        f_, a_ = nxt.pop(0)
                        f_(*a_)
                    pts['left'] -= 1
                _d2 = _os.environ.get("KDBG2")
                if _d2 and int(_os.environ["KDBG"]) == tiles[0]:
                    if _d2 == 'xT':
                        S.op('sp', lambda: SP.dma_start(out=dbg2, in_=xT[:, :, :].rearrange("p a b -> p (a b)")),
                             reads=[('xT', 0, 0), ('xT', 0, 1), ('xT', 1, 0), ('xT', 1, 1)], writes=[('xd', l + 1, 1)], dma='gdbg2')
                    elif _d2 == 'w':
                        S.op('sp', lambda: SP.dma_start(out=dbg2, in_=w_in_sb[:, 0, 3072:5120]),
                             reads=[('win', 1), ('win', 2)], writes=[('xd', l + 1, 1)], dma='gdbg2')
                    elif _d2 == 'QT':
                        S.op('sp', lambda: SP.dma_start(out=dbg2.rearrange("p (a b) -> p a b", a=8), in_=QT[:, bi % 2, :, :]),
                             reads=[('QT', bi % 2)], writes=[('xd', l + 1, 1)], dma='gdbg2')
                    elif _d2 == 'K':
                        S.op('sp', lambda: SP.dma_start(out=dbg2[:, 0:1024].rearrange("p (a b) -> p a b", a=8),
                                                        in_=KTr[:, :, (tiles[0] % 8) * 128:(tiles[0] % 8) * 128 + 128]),
                             reads=[('K', tiles[0] % 8)], writes=[('xd', l + 1, 1)], dma='gdbg2')
                    elif _d2 == 'QmT':
                        S.op('sp', lambda: SP.dma_start(out=dbg2[:, 0:1024].rearrange("p (a b) -> p a b", a=4), in_=QmT[:, bi % 2, :, :]),
                             reads=[('QmT', bi % 2)], writes=[('xd', l + 1, 1)], dma='gdbg2')
                    elif _d2 == 'V':
                        S.op('sp', lambda: SP.dma_start(out=dbg2[:, 0:1040].rearrange("p (a b) -> p a b", a=16), in_=Vr[:, tiles[0] % 8, :, :]),
                             reads=[('V', tiles[0] % 8)], writes=[('xd', l + 1, 1)], dma='gdbg2')
                    elif _d2 == 'sz':
                        S.op('sp', lambda: SP.dma_start(out=dbg2[:, 0:1536], in_=siluz[:, bi % 2, 0, :]),
                             reads=[('siluz', bi % 2, 0)], writes=[('xd', l + 1, 1)], dma='gdbg2')
                n = len(tiles)
                N = 128 * n
                pb = bi % 2
                conv_items = []
                if not attn:
                    cb = bi % 2

                    def conv_chunk(j):
                        aj = j % 2
                        S.op('dve', lambda j=j, aj=aj: V.tensor_scalar(
                            out=cacc[:, aj, 0:N], in0=cu[:, cb, j, 0:N], scalar1=cw_sb[:, j, 0:1], scalar2=None,
                            op0=ALU.mult), reads=[('cu', cb), 'cw'], writes=[('cacc', aj)])
                        for k in (1, 2):
                            S.op('dve', lambda j=j, aj=aj, k=k: V.scalar_tensor_tensor(
                                out=cacc[:, aj, 0:N], in0=cu[:, cb, j, k:k + N], scalar=cw_sb[:, j, k:k + 1],
                                in1=cacc[:, aj, 0:N], op0=ALU.mult, op1=ALU.add),
                                reads=[('cu', cb), 'cw'], writes=[('cacc', aj)])
                        S.op('pool', lambda j=j, aj=aj: G.tensor_tensor(
                            out=cacc[:, aj, 0:N], in0=cacc[:, aj, 0:N], in1=p0s[:, pb, j, 0:N], op=ALU.mult),
                            reads=[('p0s', pb, j)], writes=[('cacc', aj)])
                        S.op('pool', lambda j=j, aj=aj: G.tensor_tensor(
                            out=szT[:, pb, j, 0:N], in0=cacc[:, aj, 0:N], in1=szT[:, pb, j, 0:N], op=ALU.mult),
                            reads=[('cacc', aj)], writes=[('szT', pb, j)])

                    for j_ in range(8):
                        conv_items.append(j_)

                def tile_body(ti, t):
                    def ln_handoff():
                        if st_l.get('ln_tail') is not None:
                            st_l['ln_tail']()
                            st_l['ln_tail'] = None
                        S.op('sp', lambda t=t: SP.dma_start(out=tb[:, 0, :], in_=src[t * 128:(t + 1) * 128, :]),
                             reads=[('xd', l, t)], writes=[('tb', 0)], dma=('tbi', 0))
                    if attn:
                        def scores(h):
                            sbuf = h % 2
                            hp = h % 2
                            ch = h // 2
                            for j in range(5):
                                ks = (t - 4 + j) % 8
                                hasb = BIAS_ON_PE and j in (0, 3, 4)
                                S.op('pe', lambda sbuf=sbuf, j=j, ks=ks, hp=hp, ch=ch, hasb=hasb: T.matmul(
                                    PS[:, sbuf * 1024 + j * 128: sbuf * 1024 + (j + 1) * 128],
                                    lhsT=KTr[hp * 64:(hp + 1) * 64, ch, ks * 128:(ks + 1) * 128],
                                    rhs=QT[hp * 64:(hp + 1) * 64, pb, ch, ti * 128:(ti + 1) * 128],
                                    start=True, stop=(not hasb)),
                                    reads=[('K', ks), ('QT', pb)], writes=[BK(2 * sbuf), BK(2 * sbuf + 1)])
                                if hasb:
                                    if j == 0:
                                        brhs = mask0[:, :]
                                        bk = 'mask0'
                                    else:
                                        brhs = Ep[:, h, (j - 3) * 128:(j - 2) * 128]
                                        bk = 'Ep'
                                    S.op('pe', lambda sbuf=sbuf, j=j, brhs=brhs: T.matmul(
                                        PS[:, sbuf * 1024 + j * 128: sbuf * 1024 + (j + 1) * 128],
                                        lhsT=ident_bf[:, :], rhs=brhs, start=False, stop=True),
                                        reads=[bk, 'ident_bf'], writes=[BK(2 * sbuf), BK(2 * sbuf + 1)])
                            pbuf = h % 3
                            S.op('act', lambda sbuf=sbuf, pbuf=pbuf: A.activation(
                                out=PT[:, pbuf, :], in_=PS[:, sbuf * 1024: sbuf * 1024 + 640], func=AF.Exp),
                                writes=[BK(2 * sbuf), BK(2 * sbuf + 1), ('PT', pbuf)])
                            if not BIAS_ON_PE:
                                S.op('dve', lambda pbuf=pbuf, h=h: V.tensor_tensor(
                                    out=PT[:, pbuf, 384:640], in0=PT[:, pbuf, 384:640], in1=Ep[:, h, :], op=ALU.mult),
                                    reads=['Ep'], writes=[('PT', pbuf)])
                                S.op('dve', lambda pbuf=pbuf: V.memset(PT[0:64, pbuf, 64:128], 0.0),
                                     writes=[('PT', pbuf)])

                        pvb = {'b': None}

                        def pv(h):
                            hg = h % 4
                            if hg == 0:
                                pvb['b'] = 7
                            b = pvb['b']
                            pbuf = h % 3
                            for j in range(5):
                                vs = (t - 4 + j) % 8
                                S.op('pe', lambda b=b, hg=hg, j=j, vs=vs, pbuf=pbuf, h=h: T.matmul(
                                    bank(b, 65, hg * 65), lhsT=PT[:, pbuf, j * 128:(j + 1) * 128],
                                    rhs=Vr[:, vs, h, :], start=(j == 0), stop=(j == 4)),
                                    reads=[('PT', pbuf), ('V', vs)], writes=[BK(b)])
                            if hg == 3:
                                h0 = h - 3
                                gb = (h // 4) % 2
                                pv4 = bank(b, 260).rearrange("p (h d) -> p h d", h=4)
                                S.op('dve', lambda pv4=pv4: V.tensor_scalar(
                                    out=small[:, 0:4].unsqueeze(2), in0=pv4[:, :, 64:65], scalar1=1e-30, scalar2=None,
                                    op0=ALU.add), writes=[BK(b), 'rs'])
                                S.op('dve', lambda: V.reciprocal(out=small[:, 0:4], in_=small[:, 0:4]), writes=['rs'])
                                S.op('dve', lambda pv4=pv4, gb=gb: V.tensor_tensor(
                                    out=gtmp[:, gb, :].rearrange("p (h d) -> p h d", h=4), in0=pv4[:, :, 0:64],
                                    in1=small[:, 0:4].unsqueeze(2).to_broadcast([128, 4, 64]), op=ALU.mult),
                                    reads=['rs'], writes=[BK(b), ('gtmp', gb)])
                                S.op('pool', lambda gb=gb, h0=h0: G.tensor_tensor(
                                    out=siluz[:, pb, ti, h0 * 64:(h0 + 4) * 64], in0=gtmp[:, gb, :],
                                    in1=siluz[:, pb, ti, h0 * 64:(h0 + 4) * 64], op=ALU.mult),
                                    reads=[('gtmp', gb)], writes=[('siluz', pb, ti)])

                        scores(0)
                        scores(1)
                        for h in range(16):
                            if h + 2 < 16:
                                scores(h + 2)
                            fill()
                            pv(h)
                            if h == 3:
                                ln_handoff()
                    mbank = {}

                    def mscore(hm):
                        b = gen_bank()
                        mbank[hm] = b
                        for mb in range(2):
                            S.op('pe', lambda b=b, mb=mb, hm=hm: T.matmul(
                                bank(b, 128, mb * 128), lhsT=kmemT[:, hm, mb * 128:(mb + 1) * 128],
                                rhs=QmT[:, pb, hm, ti * 128:(ti + 1) * 128], start=True, stop=True),
                                reads=['kmemT', ('QmT', pb)], writes=[BK(b)])
                        mbuf = hm % 2
                        S.op('act', lambda b=b, mbuf=mbuf: A.activation(out=PTm[:, mbuf, :], in_=bank(b, 256), func=AF.Exp),
                             writes=[BK(b), ('PTm', mbuf)])

                    def mpv(hm):
                        b = mbank[hm]
                        mbuf = hm % 2
                        for mb in range(2):
                            S.op('pe', lambda b=b, mb=mb, hm=hm, mbuf=mbuf: T.matmul(
                                bank(b, 129, 256), lhsT=PTm[:, mbuf, mb * 128:(mb + 1) * 128],
                                rhs=vmem[:, mb, hm, :], start=(mb == 0), stop=(mb == 1)),
                                reads=[('PTm', mbuf), 'vmem'], writes=[BK(b)])
                        S.op('dve', lambda b=b: V.tensor_scalar(
                            out=small[:, 8:9], in0=bank(b, 1, 256 + 128), scalar1=1e-30, scalar2=None, op0=ALU.add),
                            writes=[BK(b), 'rsm'])
                        S.op('dve', lambda: V.reciprocal(out=small[:, 8:9], in_=small[:, 8:9]), writes=['rsm'])
                        S.op('dve', lambda b=b, hm=hm: V.scalar_tensor_tensor(
                            out=siluz[:, pb, ti, 1024 + hm * 128:1024 + (hm + 1) * 128], in0=bank(b, 128, 256),
                            scalar=small[:, 8:9], in1=siluz[:, pb, ti, 1024 + hm * 128:1024 + (hm + 1) * 128],
                            op0=ALU.mult, op1=ALU.mult),
                            reads=['rsm'], writes=[BK(b), ('siluz', pb, ti)])

                    mscore(0)
                    for hm in range(4):
                        if hm + 1 < 4:
                            mscore(hm + 1)
                        mpv(hm)
                        fill()
                        if hm == 0 and not attn:
                            ln_handoff()
                    if _os.environ.get("KDBG") and int(_os.environ["KDBG"]) == t:
                        S.op('sp', lambda: SP.dma_start(out=dbg_out, in_=siluz[:, pb, ti, :]),
                             reads=[('siluz', pb, ti)], writes=[('xd', l + 1, 0)], dma='gdbg')
                    if _os.environ.get("KDBG2") == 'PT' and int(_os.environ["KDBG"]) == t:
                        S.op('sp', lambda: SP.dma_start(out=dbg2[:, 0:1280].rearrange("p (a b) -> p a b", a=2), in_=PT[:, :, :]),
                             reads=[('PT', 0), ('PT', 1)], writes=[('xd', l + 1, 1)], dma='gdbg2')
                        S.op('sp', lambda: SP.dma_start(out=dbg2[:, 1280:1792].rearrange("p (a b) -> p a b", a=2), in_=PTm[:, :, :]),
                             reads=[('PTm', 0), ('PTm', 1)], writes=[('xd', l + 1, 2)], dma='gdbg3')
                    chunks = list(range(12)) if attn else list(range(8, 12))
                    for g0 in range(0, len(chunks), 4):
                        grp = chunks[g0:g0 + 4]
                        b = gen_bank()
                        psb = bank(b).bitcast(BF16)
                        for q, j in enumerate(grp):
                            S.op('pe', lambda q=q, j=j, psb=psb: T.transpose(
                                out=psb[:, q * 128:(q + 1) * 128], in_=siluz[:, pb, ti, j * 128:(j + 1) * 128],
                                identity=ident_bf[:, :]),
                                reads=[('siluz', pb, ti), 'ident_bf'], writes=[BK(b)])
                        j0 = grp[0]
                        S.op('dve', lambda psb=psb, j0=j0: V.tensor_copy(
                            out=yT[:, j0:j0 + 4, :], in_=psb[:, 0:512].rearrange("p (c t) -> p c t", c=4)),
                            writes=[BK(b), ('yT', j0 // 4)])
                    fill()
                    while conv_items:
                        conv_chunk(conv_items.pop(0))
                    tslot = 0
                    for half in range(2):
                        b = gen_bank()
                        for ec in range(12):
                            if attn or ec >= 8:
                                lh = yT[:, ec, :]
                                rk = [('yT', ec // 4)]
                            else:
                                lh = szT[:, pb, ec, ti * 128:(ti + 1) * 128]
                                rk = [('szT', pb, ec)]
                            S.op('pe', lambda b=b, ec=ec, half=half, lh=lh: T.matmul(
                                bank(b), lhsT=lh, rhs=w_out_sb[:, ec, half * 512:(half + 1) * 512],
                                start=(ec == 0), stop=(ec == 11)),
                                reads=rk + [('wout', half)], writes=[BK(b)])
                        S.op('dve', lambda b=b, half=half, tslot=tslot: V.scalar_tensor_tensor(
                            out=tb[:, tslot, half * 512:(half + 1) * 512], in0=tb[:, tslot, half * 512:(half + 1) * 512],
                            scalar=ALPHA, in1=bank(b), op0=ALU.mult, op1=ALU.add),
                            writes=[BK(b), ('tb', tslot)])
                        S.op('dve', lambda half=half, tslot=tslot: V.bn_stats(
                            out=stats[:, tslot, half, :], in_=tb[:, tslot, half * 512:(half + 1) * 512]),
                            reads=[('tb', tslot)], writes=[('stats', tslot)])
                    def ln_tail(t=t, tslot=tslot):
                        S.op('dve', lambda tslot=tslot: V.bn_aggr(
                            out=mv[:, tslot, 0:2], in_=stats[:, tslot, :, :].rearrange("p a b -> p (a b)")),
                            reads=[('stats', tslot)], writes=[('mv', tslot)])
                        S.op('dve', lambda tslot=tslot: V.tensor_scalar(
                            out=small[:, 20:21], in0=mv[:, tslot, 1:2], scalar1=EPS, scalar2=None, op0=ALU.add),
                            reads=[('mv', tslot)], writes=[('mvb', tslot)])
                        S.op('pool', lambda tslot=tslot: G.tensor_tensor(
                            out=mv[:, tslot, 2:3], in0=small[:, 20:21], in1=small[:, 17:18], op=ALU.pow),
                            reads=[('mvb', tslot), 'epsc'], writes=[('mvc', tslot)])
                        S.op('dve', lambda tslot=tslot: V.tensor_scalar(
                            out=tb[:, tslot, :], in0=tb[:, tslot, :], scalar1=mv[:, tslot, 0:1], scalar2=mv[:, tslot, 2:3],
                            op0=ALU.subtract, op1=ALU.mult),
                            reads=[('mv', tslot), ('mvc', tslot)], writes=[('tb', tslot)])
                        S.op('pool', lambda tslot=tslot: G.tensor_tensor(
                            out=tb[:, tslot, :], in0=tb[:, tslot, :], in1=lng[:, :], op=ALU.mult),
                            reads=['lng'], writes=[('tb', tslot)])
                        S.op('pool', lambda tslot=tslot: G.tensor_tensor(
                            out=tb[:, tslot, :], in0=tb[:, tslot, :], in1=lnb[:, :], op=ALU.add),
                            reads=['lnb'], writes=[('tb', tslot)])
                        if l == DEPTH - 1 and t < HALO:
                            return
                        r0 = dst_rows(t)
                        S.op('sp', lambda tslot=tslot, r0=r0: SP.dma_start(out=dst[r0:r0 + 128, :], in_=tb[:, tslot, :]),
                             reads=[('tb', tslot)], writes=[('xd', l + 1, t)], dma=('tbo', tslot))
                    st_l['ln_tail'] = ln_tail

                for ti_, t_ in enumerate(tiles):
                    tile_body(ti_, t_)
                    fill()
                for f_, a_ in nxt:
                    f_(*a_)

            def xload(bi, l=l, src=src, st_l=st_l, blocks=blocks):
                if bi >= len(blocks):
                    return
                for t in blocks[bi][0]:
                    slot = st_l['nload'] % NXS
                    st_l['nload'] += 1
                    st_l['xslot'][t] = slot
                    S.op('sp', lambda slot=slot, t=t: SP.dma_start(out=xs[:, slot, :], in_=src[t * 128:(t + 1) * 128, :]),
                         reads=[('xd', l, t)], writes=[('xs', slot)], dma=('xs', slot))

            if nb > 0:
                xload(0)
                xload(1)
                for f_, a_ in phase1(0):
                    f_(*a_)
            for bi in range(nb):
                nxt = phase1(bi + 1) if bi + 1 < nb else []
                xload(bi + 2)
                if bi + 1 == nb and li + 1 < len(layers):
                    load_weights(layers[li + 1])
                phase2(bi, nxt)
            if st_l.get('ln_tail') is not None:
                st_l['ln_tail']()
                st_l['ln_tail'] = None
            S.fence()
            if li + 1 < len(layers):
                load_wkv(layers[li + 1])
                load_w_out(layers[li + 1])

        outkeys = [('xd', last + 1, t) for t in range(NT)]
        S.op('sp', None, reads=outkeys)
        S.finalize(st)
    return nc, S


def _host_layout(inputs):
    x = np.asarray(inputs["x"], dtype=np.float32)
    mem = np.asarray(inputs["mem"], dtype=np.float32)
    rel_bias = np.asarray(inputs["rel_bias"], dtype=np.float32)
    conv_w = np.asarray(inputs["conv_w"], dtype=np.float32)
    k = np.arange(128)[:, None]
    q = np.arange(128)[None, :]
    idx3 = np.minimum(q - k + 128, 128) + 128
    idx4 = (q - k) + 128
    idx = np.concatenate([idx3, idx4], axis=1)
    bt = np.ascontiguousarray(np.transpose(rel_bias[:, :, idx], (0, 2, 1, 3)))
    chb = np.ascontiguousarray(np.broadcast_to(rel_bias[:, None, :, 256], (2, 128, 16)))
    convw = np.ascontiguousarray(np.transpose(conv_w.reshape(2, 3, 8, 128), (0, 3, 2, 1)))
    common = dict(
        ident=np.eye(128, dtype=np.float32),
        w_in=np.asarray(inputs["w_in"], dtype=np.float32),
        w_mem_kv=np.asarray(inputs["w_mem_kv"], dtype=np.float32),
        w_out=np.asarray(inputs["w_out"], dtype=np.float32),
        bt=bt, chb=chb, convw=convw,
        ln_g=np.asarray(inputs["ln_g"], dtype=np.float32),
        ln_b=np.asarray(inputs["ln_b"], dtype=np.float32),
    )
    per_core = []
    for c in range(8):
        b, half = c // 2, c % 2
        s0 = half * 4096
        w0 = s0 - HALO * 128
        xw = np.zeros((NT * 128, D), np.float32)
        lo = max(w0, 0)
        xw[lo - w0:] = x[b, lo:s0 + 4096]
        vt = np.zeros((NT * 128,), np.float32)
        vt[lo - w0:] = 1.0
        valid = np.ascontiguousarray(vt.reshape(NT, 128).T)
        vrow = np.ascontiguousarray(np.broadcast_to(vt.reshape(NT, 128)[None, :, 126:128], (128, NT, 2)))
        m = dict(common)
        m.update(xin=xw, valid=valid, vrow=vrow, mem=np.ascontiguousarray(mem[b]))
        per_core.append(m)
    return per_core


_CACHE = {}


def _get_program(layers):
    key = tuple(layers)
    if key not in _CACHE:
        _CACHE[key] = build_program(list(layers))[0]
    return _CACHE[key]


FUSED = True


def kernel(**inputs):
    per_core = _host_layout(inputs)
    groups = [[0, 1, 2, 3]] if FUSED else [[0], [1], [2], [3]]
    full = {k: per_core[0][k] for k in ("w_in", "w_mem_kv", "w_out", "ln_g", "ln_b")}
    for grp in groups:
        nc = _get_program(grp)
        sl = {k: np.ascontiguousarray(v[grp[0]:grp[-1] + 1]) for k, v in full.items()}
        maps = []
        for c in range(8):
            m = dict(per_core[c])
            m.update(sl)
            maps.append(m)
        res = run_bass_kernel_spmd(nc, maps, core_ids=list(range(8)))
        outs = [r["xout"] for r in res.results]
        if grp[-1] != DEPTH - 1:
            for c in range(8):
                per_core[c]["xin"] = outs[c]
    out = np.zeros((4, 8192, D), np.float32)
    for c in range(8):
        b, half = c // 2, c % 2
        out[b, half * 4096:(half + 1) * 4096] = outs[c]
    return out
```

```python
import math
from contextlib import ExitStack

import numpy as np
import concourse.bass as bass
import concourse.mybir as mybir
from concourse.bass_utils import run_bass_kernel_spmd

F32 = mybir.dt.float32
BF16 = mybir.dt.bfloat16
AF = mybir.ActivationFunctionType
ALU = mybir.AluOpType

D = 1024
NIN = 5120
NT = 42
HALO = 10
NMAIN = 32
DEPTH = 4
ALPHA = (2.0 * DEPTH) ** 0.25
EPS = 1e-5
NXS = 4
BIAS_ON_PE = False

ENG_ATTR = {'pe': 'tensor', 'act': 'scalar', 'dve': 'vector', 'pool': 'gpsimd', 'sp': 'sync'}


class Op:
    __slots__ = ('eng', 'fn', 'reads', 'writes', 'dma', 'waits', 'signal', 'ev', 'vc', 'sigcount')

    def __init__(self, eng, fn, reads, writes, dma):
        self.eng = eng
        self.fn = fn
        self.reads = reads
        self.writes = writes
        self.dma = dma
        self.waits = []
        self.signal = False
        self.ev = None
        self.vc = None
        self.sigcount = 0


class Sched:
    def __init__(self, nc, same_engine_sync=('act', 'dve', 'pool')):
        self.nc = nc
        self.ops = []
        self.same_sync = set(same_engine_sync)

    def op(self, eng, fn, reads=(), writes=(), dma=None):
        self.ops.append(Op(eng, fn, tuple(reads), tuple(writes), dma))

    def fence(self):
        shared = {}
        for e in ENG_ATTR:
            self.ops.append(Op(e, None, ('__fence__', shared), (), None))

    def finalize(self, stack):
        nc = self.nc
        state = {}
        vcs = {e: {} for e in ENG_ATTR}
        eng_count = {e: 0 for e in ENG_ATTR}
        dma_count = {}
        evop = {}
        dmavc = {}
        for op in self.ops:
            e = op.eng
            deps = {}
            if len(op.reads) == 2 and op.reads[0] == '__fence__':
                shared = op.reads[1]
                if not shared:
                    for e2, c2 in eng_count.items():
                        if c2 > 0:
                            shared[('e', e2)] = c2 - 1
                    for dk2, c2 in dma_count.items():
                        shared[dk2] = c2
                deps.update(shared)
                op.reads = ()
            for r in op.reads:
                st = state.get(r)
                if st:
                    for (k, i) in st[0]:
                        if deps.get(k, -1) < i:
                            deps[k] = i
            for w in op.writes:
                st = state.get(w)
                if st:
                    for (k, i) in st[0]:
                        if deps.get(k, -1) < i:
                            deps[k] = i
                    for (k, i) in st[1]:
                        if deps.get(k, -1) < i:
                            deps[k] = i
            vc = vcs[e]
            myk = ('e', e)
            for k, i in deps.items():
                if k == myk and e not in self.same_sync:
                    continue
                if vc.get(k, -1) >= i:
                    continue
                op.waits.append((k, i))
            for (k, i) in op.waits:
                if k[0] == 'e':
                    src = evop[(k, i)]
                    src.signal = True
                    svc = src.vc
                else:
                    svc = dmavc[(k, i)]
                for kk, ii in svc.items():
                    if vc.get(kk, -1) < ii:
                        vc[kk] = ii
                if vc.get(k, -1) < i:
                    vc[k] = i
            if op.fn is None:
                op.ev = None
                continue
            if op.dma is None:
                idx = eng_count[e]
                eng_count[e] = idx + 1
                op.ev = (myk, idx)
                evop[op.ev] = op
                snap = dict(vc)
                snap[myk] = idx
                op.vc = snap
                if e not in self.same_sync:
                    vc[myk] = idx
            else:
                dk = ('d', op.dma)
                c = dma_count.get(dk, 0) + 1
                dma_count[dk] = c
                op.ev = (dk, c)
                dmavc[op.ev] = dict(vc)
            for r in op.reads:
                st = state.setdefault(r, [[], []])
                rl = [x for x in st[1] if x[0] != op.ev[0]]
                rl.append(op.ev)
                st[1] = rl
            for w in op.writes:
                state[w] = [[op.ev], []]
        sems = {}
        for e in ENG_ATTR:
            sems[('e', e)] = stack.enter_context(nc.semaphore('sem_' + e))
        for dk in dma_count:
            sems[dk] = stack.enter_context(nc.semaphore('dsem_%d' % len(sems)))
        cnt = {e: 0 for e in ENG_ATTR}
        for op in self.ops:
            if op.dma is None and op.signal:
                cnt[op.eng] += 1
                op.sigcount = cnt[op.eng]
        for op in self.ops:
            engobj = getattr(nc, ENG_ATTR[op.eng])
            for (k, i) in op.waits:
                val = evop[(k, i)].sigcount if k[0] == 'e' else 16 * i
                engobj.wait_ge(sems[k], val)
            if op.fn is None:
                continue
            ins = op.fn()
            if op.dma is not None:
                ins.then_inc(sems[op.ev[0]], 16)
            elif op.signal:
                ins.then_inc(sems[('e', op.eng)], 1)
        self.stats = dict(nops=len(self.ops), sig=cnt, nsems=len(sems))


def layer_plan(l):
    if l == 0:
        blocks = [((0, 1), 'kv'), ((2, 3), 'kv')] + [((t, t + 1), 'full') for t in range(4, NT, 2)]
    elif l == 1:
        blocks = [((4,), 'cu'), ((5,), 'full')] + [((t, t + 1), 'full') for t in range(6, NT, 2)]
    elif l == 2:
        blocks = [((5, 6), 'kv'), ((7, 8), 'kv'), ((9,), 'full')] + \
                 [((t, t + 1), 'full') for t in range(10, NT, 2)]
    else:
        blocks = [((9,), 'cu')] + [((t, t + 1), 'full') for t in range(10, NT, 2)]
    return blocks


def build_program(layers, same_engine_sync=('dve', 'pool'), max_blocks=None):
    nc = bass.Bass("TRN2", target_bir_lowering=False, dynamic_dma_scratch_size=8192)
    last = layers[-1]
    final = (last == DEPTH - 1)
    dr = {}

    def din(name, shape):
        dr[name] = nc.dram_tensor(name, list(shape), F32, kind="ExternalInput").ap()

    din("xin", (NT * 128, D))
    din("valid", (128, NT))
    din("vrow", (128, NT, 2))
    din("mem", (256, D))
    din("ident", (128, 128))
    din("w_in", (len(layers), D, NIN))
    din("w_mem_kv", (len(layers), D, D))
    din("w_out", (len(layers), 1536, D))
    din("bt", (2, 128, 16, 256))
    din("chb", (2, 128, 16))
    din("convw", (2, 128, 8, 3))
    din("ln_g", (len(layers), D))
    din("ln_b", (len(layers), D))
    if final:
        xout = nc.dram_tensor("xout", [NMAIN * 128, D], F32, kind="ExternalOutput").ap()
    else:
        xout = nc.dram_tensor("xout", [NT * 128, D], F32, kind="ExternalOutput").ap()
    scratch = {}
    for l in layers[:-1]:
        scratch[l + 1] = nc.dram_tensor("xs%d" % (l + 1), [NT * 128, D], F32).ap()

    S = Sched(nc, same_engine_sync)
    with ExitStack() as st:
        def sb(name, shape, dt):
            return st.enter_context(nc.sbuf_tensor("sb_" + name, list(shape), dt))

        w_in_sb = sb("w_in_sb", (128, 8, NIN), BF16)
        w_out_sb = sb("w_out_sb", (128, 12, D), BF16)
        kmemT = sb("kmemT", (128, 4, 256), BF16)
        vmem = sb("vmem", (128, 2, 4, 129), BF16)
        valid_sb = sb("valid_sb", (128, NT), F32)
        vrow_sb = sb("vrow_sb", (128, NT, 2), F32)
        ident = sb("ident", (128, 128), F32)
        ident_bf = sb("ident_bf", (128, 128), BF16)
        lng = sb("lng", (128, D), F32)
        lnb = sb("lnb", (128, D), F32)
        xs = sb("xs", (128, NXS, D), F32)
        xT = sb("xT", (128, 8, 256), BF16)
        tb = sb("tb", (128, 1, D), F32)
        siluz = sb("siluz", (128, 2, 2, 1536), BF16)
        yT = sb("yT", (128, 12, 128), BF16)
        QmT = sb("QmT", (128, 2, 4, 256), BF16)
        PTm = sb("PTm", (128, 2, 256), BF16)
        small = sb("small", (128, 64), F32)
        stats = sb("stats", (128, 2, 2, 6), F32)
        mv = sb("mv", (128, 2, 4), F32)
        gtmp = sb("gtmp", (128, 2, 256), F32)
        chb_sb = sb("chb_sb", (128, 16), F32)
        mask0 = sb("mask0", (128, 128), BF16) if BIAS_ON_PE else None
        import os as _os
        dbg_out = nc.dram_tensor("dbg", [128, 1536], BF16, kind="ExternalOutput").ap() if _os.environ.get("KDBG") else None
        dbg2 = nc.dram_tensor("dbg2", [128, 2048], BF16, kind="ExternalOutput").ap() if _os.environ.get("KDBG2") else None
        cw_sb = sb("cw_sb", (128, 8, 3), F32)
        MIXB = 53248
        MIX = sb("MIX", (128, MIXB // 2), BF16)

        def mview(off, shape, dt):
            esz = 4 if dt == F32 else 2
            n = 1
            for d_ in shape:
                n *= d_
            v = MIX[:, off // 2: off // 2 + (n * esz) // 2]
            if dt == F32:
                v = v.bitcast(F32)
            if len(shape) == 1:
                return v
            names = "abcd"[:len(shape)]
            pat = "p (%s) -> p %s" % (" ".join(names), " ".join(names))
            kw = {names[i]: shape[i] for i in range(len(shape) - 1)}
            return v.rearrange(pat, **kw)

        QT = mview(0, (2, 8, 256), BF16)
        KTr = mview(8192, (8, 1024), BF16)
        Vr = mview(24576, (8, 16, 65), BF16)
        PT = mview(41216, (3, 640), BF16)
        Ep = mview(45056, (16, 256), BF16)
        cu = mview(0, (2, 8, 258), F32)
        p0s = mview(16512, (2, 8, 256), F32)
        p1s = mview(32896, (2, 256), F32)
        cacc = mview(34944, (2, 256), F32)
        szT = mview(36992, (2, 8, 256), BF16)
        memx = mview(0, (2, D), F32)
        memT = mview(8192, (8, 256), BF16)
        wkv = mview(12288, (8, D), BF16)
        btst = tb[:, :, :].rearrange("p a (h c) -> p (a h) c", c=256)

        PS = st.enter_context(nc.psum_tensor("PS", [128, 8 * 512], F32))
        print("SBUF bytes remaining per partition:", nc.sbuf_bytes_remaining)

        def bank(b, n=512, off=0):
            return PS[:, b * 512 + off: b * 512 + off + n]

        T, A, V, G, SP = nc.tensor, nc.scalar, nc.vector, nc.gpsimd, nc.sync
        BK = lambda b: ('bank', b)

        rot = {'i': 0}

        def gen_bank():
            b = 4 + (rot['i'] % 3)
            rot['i'] += 1
            return b

        S.op('sp', lambda: SP.dma_start(out=ident[:, :], in_=dr["ident"]), writes=['ident'], dma='ident')
        S.op('sp', lambda: SP.dma_start(out=valid_sb[:, :], in_=dr["valid"]), writes=['valid'], dma='valid')
        S.op('sp', lambda: SP.dma_start(out=vrow_sb[:, :, :], in_=dr["vrow"]), writes=['vrow'], dma='vrow')
        S.op('dve', lambda: V.tensor_copy(out=ident_bf[:, :], in_=ident[:, :]), reads=['ident'], writes=['ident_bf'])
        S.op('pool', lambda: G.memset(small[:, 16:17], EPS), writes=['epsc'])
        S.op('pool', lambda: G.memset(small[:, 17:18], -0.5), writes=['epsc'])

        WCH = ((1024, 3072), (0, 1024), (3072, 5120))

        def wkey(col):
            return ('win', 0 if 1024 <= col < 3072 else (1 if col < 1024 else 2))

        def load_weights(l):
            l = layers.index(l)
            for c, (c0, c1) in enumerate(WCH):
                S.op('pool', lambda l=l, c0=c0, c1=c1: G.dma_start(
                    out=w_in_sb[:, :, c0:c1],
                    in_=dr["w_in"][l, :, c0:c1].rearrange("(k p) n -> p k n", p=128)),
                    writes=[('win', c)], dma=('win', c))

        def load_w_out(l):
            l = layers.index(l)
            S.op('pool', lambda l=l: G.dma_start(
                out=w_out_sb[:, :, :], in_=dr["w_out"][l, :, :].rearrange("(k p) n -> p k n", p=128)),
                writes=[('wout', 0), ('wout', 1)], dma=('wout', 0))

        import os as _os2
        _PROBE2 = _os2.environ.get('KPROBE2', '')

        def load_wkv(l):
            l = layers.index(l)
            S.op('pool', lambda l=l: G.dma_start(
                out=wkv[:, :, :], in_=dr["w_mem_kv"][l, :, :].rearrange("(k p) n -> p k n", p=128)),
                writes=[('wkv', 0), ('wkv', 1)], dma=('wkv', 0))

        def kv_prologue(l):
            l = layers.index(l)
            S.op('sp', lambda: SP.dma_start(out=memx, in_=dr["mem"].rearrange("(j p) d -> p j d", p=128)),
                 writes=['memx'], dma='memx')
            if _PROBE2 == 'c1':
                return
            for j in range(2):
                for half in range(2):
                    b = gen_bank()
                    for q in range(4):
                        kc = half * 4 + q
                        S.op('pe', lambda b=b, q=q, kc=kc, j=j: T.transpose(
                            out=bank(b, 128, q * 128), in_=memx[:, j, kc * 128:(kc + 1) * 128], identity=ident[:, :]),
                            reads=['memx', 'ident'], writes=[BK(b)])
                    S.op('act', lambda b=b, half=half, j=j: A.copy(
                        out=memT[:, half * 4:half * 4 + 4, j * 128:(j + 1) * 128],
                        in_=bank(b).rearrange("p (c t) -> p c t", c=4)),
                        writes=[BK(b), ('memT', j, half)])
            if _PROBE2 == 'c2':
                return
            memT_keys = [('memT', j, half) for j in range(2) for half in range(2)]
            for h in range(4):
                b = gen_bank()
                for kc in range(8):
                    S.op('pe', lambda b=b, h=h, kc=kc: T.matmul(
                        bank(b, 256), lhsT=wkv[:, kc, h * 128:(h + 1) * 128], rhs=memT[:, kc, :],
                        start=(kc == 0), stop=(kc == 7)),
                        reads=[('wkv', 0)] + memT_keys, writes=[BK(b)])
                S.op('act', lambda b=b, h=h: A.copy(out=kmemT[:, h, :], in_=bank(b, 256)),
                     writes=[BK(b), 'kmemT'])
            if _PROBE2 == 'c3':
                return
            for mb in range(2):
                b = gen_bank()
                for kc in range(8):
                    S.op('pe', lambda b=b, mb=mb, kc=kc: T.matmul(
                        bank(b), lhsT=memT[:, kc, mb * 128:(mb + 1) * 128], rhs=wkv[:, kc, 512:1024],
                        start=(kc == 0), stop=(kc == 7)),
                        reads=[('wkv', 1)] + memT_keys, writes=[BK(b)])
                S.op('dve', lambda b=b, mb=mb: V.tensor_copy(
                    out=vmem[:, mb, :, 0:128], in_=bank(b).rearrange("p (h d) -> p h d", h=4)),
                    writes=[BK(b), 'vmem'])
                S.op('pool', lambda mb=mb: G.memset(vmem[:, mb, :, 128:129], 1.0), writes=['vmem'])

        import os as _os
        PROBE = _os.environ.get("KPROBE", "")
        if PROBE != "a":
            load_wkv(layers[0])
            load_weights(layers[0])
            load_w_out(layers[0])

        for li, l in enumerate(layers if PROBE not in ("a", "b") else []):
            attn = (l % 2 == 0)
            la = l // 2
            src = dr["xin"] if li == 0 else scratch[l]
            dst = xout if l == last else scratch[l + 1]

            def dst_rows(t, l=l):
                if l == DEPTH - 1:
                    return (t - HALO) * 128
                return t * 128

            kv_prologue(l)
            if PROBE == "c":
                break

            S.op('sp', lambda l=li: SP.dma_start(out=lng[:, :], in_=dr["ln_g"][l:l + 1, :].to_broadcast([128, D])),
                 writes=['lng'], dma='lng')
            S.op('sp', lambda l=li: SP.dma_start(out=lnb[:, :], in_=dr["ln_b"][l:l + 1, :].to_broadcast([128, D])),
                 writes=['lnb'], dma='lnb')
            if attn:
                S.op('sp', lambda la=la: SP.dma_start(out=chb_sb[:, :], in_=dr["chb"][la]), writes=['chb'], dma='chb')
                S.op('dve', lambda: V.tensor_scalar(out=small[:, 32:48], in0=chb_sb[:, :], scalar1=-1.0,
                                                    scalar2=None, op0=ALU.mult),
                     reads=['chb'], writes=['negchb'])
                for g in range(4):
                    S.op('sp', lambda la=la, g=g: SP.dma_start(out=btst, in_=dr["bt"][la, :, 4 * g:4 * g + 4, :]),
                         writes=[('tb', 0)], dma='btst')
                    for hh in range(4):
                        h = 4 * g + hh
                        S.op('act', lambda h=h, hh=hh: A.activation(
                            out=Ep[:, h, :], in_=btst[:, hh, :], func=(AF.Identity if BIAS_ON_PE else AF.Exp), bias=small[:, 32 + h:33 + h], scale=1.0),
                            reads=[('tb', 0), 'negchb'], writes=['Ep'])
                S.op('pool', lambda: G.memset(Ep[64:128, :, 128:192], (-30000.0 if BIAS_ON_PE else 0.0)), writes=['Ep'])
                if BIAS_ON_PE:
                    S.op('pool', lambda: G.memset(mask0[:, :], 0.0), writes=['mask0'])
                    S.op('pool', lambda: G.memset(mask0[0:64, 64:128], -30000.0), writes=['mask0'])
            else:
                S.op('sp', lambda la=la: SP.dma_start(out=cw_sb[:, :, :], in_=dr["convw"][la]), writes=['cw'], dma='cw')

            blocks = layer_plan(l)
            if max_blocks is not None:
                blocks = blocks[:max_blocks]
            nb = len(blocks)
            st_l = {'cu_prev': None, 'xslot': {}, 'nload': 0}
            plan_tiles = [t_ for tl_, _m in blocks for t_ in tl_]

            def phase1(bi, l=l, attn=attn, blocks=blocks, src=src, st_l=st_l):
                tiles, mode = blocks[bi]
                n = len(tiles)
                N = 128 * n
                pb = bi % 2
                items = []
                silu_items = []

                def xunit(ti, t):
                    slot = st_l['xslot'][t]
                    for half in range(2):
                        b = gen_bank()
                        for q in range(4):
                            kc = half * 4 + q
                            S.op('pe', lambda b=b, q=q, kc=kc, slot=slot: T.transpose(
                                out=bank(b, 128, q * 128), in_=xs[:, slot, kc * 128:(kc + 1) * 128], identity=ident[:, :]),
                                reads=[('xs', slot), 'ident'], writes=[BK(b)])
                        if half == 0:
                            S.op('act', lambda b=b, half=half, ti=ti: A.copy(
                                out=xT[:, half * 4:half * 4 + 4, ti * 128:(ti + 1) * 128],
                                in_=bank(b).rearrange("p (c t) -> p c t", c=4)),
                                writes=[BK(b), ('xT', ti, half)])
                        else:
                            S.op('dve', lambda b=b, half=half, ti=ti: V.tensor_copy(
                                out=xT[:, half * 4:half * 4 + 4, ti * 128:(ti + 1) * 128],
                                in_=bank(b).rearrange("p (c t) -> p c t", c=4)),
                                writes=[BK(b), ('xT', ti, half)])
                for ti_, t_ in enumerate(tiles):
                    items.append((xunit, (ti_, t_)))
                xTk = [('xT', ti, half) for ti in range(n) for half in range(2)]

                def tokmajor(ti, col0, evac):
                    b = gen_bank()
                    c = col0 // 512
                    for kc in range(8):
                        S.op('pe', lambda b=b, kc=kc, ti=ti, col0=col0: T.matmul(
                            bank(b), lhsT=xT[:, kc, ti * 128:(ti + 1) * 128], rhs=w_in_sb[:, kc, col0:col0 + 512],
                            start=(kc == 0), stop=(kc == 7)),
                            reads=[('xT', ti, 0), ('xT', ti, 1), wkey(col0)], writes=[BK(b)])
                    evac(b)

                def featmajor(col0, evac):
                    b = gen_bank()
                    c = col0 // 512
                    for kc in range(8):
                        S.op('pe', lambda b=b, kc=kc, col0=col0: T.matmul(
                            bank(b, N), lhsT=w_in_sb[:, kc, col0:col0 + 128], rhs=xT[:, kc, 0:N],
                            start=(kc == 0), stop=(kc == 7)),
                            reads=xTk + [wkey(col0)], writes=[BK(b)])
                    evac(b)

                if attn:
                    for ti, t in enumerate(tiles):
                        vs = t % 8
                        for hb in range(2):
                            def ev(b, t=t, vs=vs, hb=hb):
                                S.op('dve', lambda: V.tensor_scalar(
                                    out=Vr[:, vs, hb * 8:(hb + 1) * 8, 0:64],
                                    in0=bank(b).rearrange("p (h d) -> p h d", h=8),
                                    scalar1=valid_sb[:, t:t + 1], scalar2=None, op0=ALU.mult),
                                    reads=['valid'], writes=[BK(b), ('V', vs)])
                            items.append((tokmajor, (ti, 2048 + hb * 512, ev)))
                        S.op('pool', lambda vs=vs, t=t: G.tensor_copy(
                            out=Vr[:, vs, :, 64:65], in_=valid_sb[:, t:t + 1].unsqueeze(1).to_broadcast([128, 16, 1])),
                            reads=['valid', 'vmem', 'kmemT'], writes=[('V', vs)])
                    for j in range(8):
                        def ev(b, j=j):
                            for ti, t in enumerate(tiles):
                                ks = t % 8
                                if (j + ti) % 2 == 0:
                                    S.op('act', lambda ti=ti, ks=ks: A.copy(
                                        out=KTr[:, j, ks * 128:(ks + 1) * 128], in_=bank(b, 128, ti * 128)),
                                        writes=[BK(b), ('K', ks)])
                                else:
                                    S.op('dve', lambda ti=ti, ks=ks: V.tensor_copy(
                                        out=KTr[:, j, ks * 128:(ks + 1) * 128], in_=bank(b, 128, ti * 128)),
                                        writes=[BK(b), ('K', ks)])
                        items.append((featmajor, (1024 + j * 128, ev)))
                    if mode == 'full':
                        for j in range(8):
                            def ev(b, j=j):
                                S.op('act', lambda: A.mul(out=QT[:, pb, j, 0:N], in_=bank(b, N), mul=0.125),
                                     writes=[BK(b), ('QT', pb)])
                            items.append((featmajor, (j * 128, ev)))
                else:
                    cb = bi % 2
                    prev = st_l['cu_prev']
                    if prev is not None:
                        pcb, pN, pt = prev
                        S.op('pool', lambda cb=cb, pcb=pcb, pN=pN, pt=pt: G.tensor_tensor(
                            out=cu[:, cb, :, 0:2], in0=cu[:, pcb, :, pN:pN + 2],
                            in1=vrow_sb[:, pt:pt + 1, :].to_broadcast([128, 8, 2]), op=ALU.mult),
                            reads=[('cu', pcb), 'vrow'], writes=[('cu', cb)])
                    else:
                        S.op('pool', lambda cb=cb: G.memset(cu[:, cb, :, 0:2], 0.0), reads=['vmem', 'kmemT'], writes=[('cu', cb)])
                    for j in range(8):
                        pj = j % 2

                        def ev1(b, pj=pj):
                            S.op('act', lambda: A.copy(out=p1s[:, pj, 0:N], in_=bank(b, N)),
                                 writes=[BK(b), ('p1s', pj)])
                        items.append((featmajor, (1024 + j * 128, ev1)))

                        def ev2(b, j=j, pj=pj, cb=cb):
                            S.op('dve', lambda: V.tensor_tensor(out=cu[:, cb, j, 2:2 + N], in0=bank(b, N),
                                                                in1=p1s[:, pj, 0:N], op=ALU.mult),
                                 reads=[('p1s', pj)], writes=[BK(b), ('cu', cb)])
                        items.append((featmajor, (2048 + j * 128, ev2)))
                    st_l['cu_prev'] = (cb, N, tiles[-1])
                    if mode == 'full':
                        for j in range(8):
                            def ev(b, j=j):
                                S.op('act', lambda: A.copy(out=p0s[:, pb, j, 0:N], in_=bank(b, N)),
                                     writes=[BK(b), ('p0s', pb, j)])
                            items.append((featmajor, (j * 128, ev)))

                            def evz(b, j=j):
                                S.op('act', lambda: A.activation(out=szT[:, pb, j, 0:N], in_=bank(b, N), func=AF.Silu),
                                     writes=[BK(b), ('szT', pb, j)])
                            silu_items.append((featmajor, (3584 + j * 128, evz)))
                if mode == 'full':
                    for j in range(4):
                        def ev(b, j=j):
                            S.op('act', lambda: A.mul(out=QmT[:, pb, j, 0:N], in_=bank(b, N), mul=1.0 / math.sqrt(128.0)),
                                 writes=[BK(b), ('QmT', pb)])
                        items.append((featmajor, (3072 + j * 128, ev)))
                    zblocks = (0, 1, 2) if attn else (2,)
                    for ti, t in enumerate(tiles):
                        for zb in zblocks:
                            def ev(b, ti=ti, zb=zb):
                                S.op('act', lambda: A.activation(out=siluz[:, pb, ti, zb * 512:(zb + 1) * 512],
                                                                 in_=bank(b), func=AF.Silu),
                                     writes=[BK(b), ('siluz', pb, ti)])
                            silu_items.append((tokmajor, (ti, 3584 + zb * 512, ev)))
                st_l['nsilu'] = len(silu_items)
                return items + silu_items

            def phase2(bi, nxt, l=l, attn=attn, blocks=blocks, dst=dst, src=src, st_l=st_l, plan_tiles=plan_tiles):
                tiles, mode = blocks[bi]
                if mode != 'full':
                    for f_, a_ in nxt:
                        f_(*a_)
                    return
                pts = {'left': (22 if attn else 6) * len(tiles)}
                nsilu = [st_l.get('nsilu', 0)]

                def fill():
                    for _ in range(2):
                        if conv_items:
                            conv_chunk(conv_items.pop(0))
                    left = max(pts['left'], 1)
                    k = -(-len(nxt) // left)
                    for _ in range(min(k, len(nxt))):
                        f_, a_ = nxt.pop(0)
                        f_(*a_)
                    if 0 < len(nxt) < nsilu[0]:
                        while nxt:
                            f_, a_ = nxt.pop(0)
                            f_(*a_)
                    pts['left'] -= 1
                _d2 = _os.environ.get("KDBG2")
                if _d2 and int(_os.environ["KDBG"]) == tiles[0]:
                    if _d2 == 'xT':
                        S.op('sp', lambda: SP.dma_start(out=dbg2, in_=xT[:, :, :].rearrange("p a b -> p (a b)")),
                             reads=[('xT', 0, 0), ('xT', 0, 1), ('xT', 1, 0), ('xT', 1, 1)], writes=[('xd', l + 1, 1)], dma='gdbg2')
                    elif _d2 == 'w':
                        S.op('sp', lambda: SP.dma_start(out=dbg2, in_=w_in_sb[:, 0, 3072:5120]),
                             reads=[('win', 1), ('win', 2)], writes=[('xd', l + 1, 1)], dma='gdbg2')
                    elif _d2 == 'QT':
                        S.op('sp', lambda: SP.dma_start(out=dbg2.rearrange("p (a b) -> p a b", a=8), in_=QT[:, bi % 2, :, :]),
                             reads=[('QT', bi % 2)], writes=[('xd', l + 1, 1)], dma='gdbg2')
                    elif _d2 == 'K':
                        S.op('sp', lambda: SP.dma_start(out=dbg2[:, 0:1024].rearrange("p (a b) -> p a b", a=8),
                                                        in_=KTr[:, :, (tiles[0] % 8) * 128:(tiles[0] % 8) * 128 + 128]),
                             reads=[('K', tiles[0] % 8)], writes=[('xd', l + 1, 1)], dma='gdbg2')
                    elif _d2 == 'QmT':
                        S.op('sp', lambda: SP.dma_start(out=dbg2[:, 0:1024].rearrange("p (a b) -> p a b", a=4), in_=QmT[:, bi % 2, :, :]),
                             reads=[('QmT', bi % 2)], writes=[('xd', l + 1, 1)], dma='gdbg2')
                    elif _d2 == 'V':
                        S.op('sp', lambda: SP.dma_start(out=dbg2[:, 0:1040].rearrange("p (a b) -> p a b", a=16), in_=Vr[:, tiles[0] % 8, :, :]),
                             reads=[('V', tiles[0] % 8)], writes=[('xd', l + 1, 1)], dma='gdbg2')
                    elif _d2 == 'sz':
                        S.op('sp', lambda: SP.dma_start(out=dbg2[:, 0:1536], in_=siluz[:, bi % 2, 0, :]),
                             reads=[('siluz', bi % 2, 0)], writes=[('xd', l + 1, 1)], dma='gdbg2')
                n = len(tiles)
                N = 128 * n
                pb = bi % 2
                conv_items = []
                if not attn:
                    cb = bi % 2

                    def conv_chunk(j):
                        aj = j % 2
                        S.op('dve', lambda j=j, aj=aj: V.tensor_scalar(
                            out=cacc[:, aj, 0:N], in0=cu[:, cb, j, 0:N], scalar1=cw_sb[:, j, 0:1], scalar2=None,
                            op0=ALU.mult), reads=[('cu', cb), 'cw'], writes=[('cacc', aj)])
                        for k in (1, 2):
                            S.op('dve', lambda j=j, aj=aj, k=k: V.scalar_tensor_tensor(
                                out=cacc[:, aj, 0:N], in0=cu[:, cb, j, k:k + N], scalar=cw_sb[:, j, k:k + 1],
                                in1=cacc[:, aj, 0:N], op0=ALU.mult, op1=ALU.add),
                                reads=[('cu', cb), 'cw'], writes=[('cacc', aj)])
                        S.op('pool', lambda j=j, aj=aj: G.tensor_tensor(
                            out=cacc[:, aj, 0:N], in0=cacc[:, aj, 0:N], in1=p0s[:, pb, j, 0:N], op=ALU.mult),
                            reads=[('p0s', pb, j)], writes=[('cacc', aj)])
                        S.op('pool', lambda j=j, aj=aj: G.tensor_tensor(
                            out=szT[:, pb, j, 0:N], in0=cacc[:, aj, 0:N], in1=szT[:, pb, j, 0:N], op=ALU.mult),
                            reads=[('cacc', aj)], writes=[('szT', pb, j)])

                    for j_ in range(8):
                        conv_items.append(j_)

                def tile_body(ti, t):
                    def ln_handoff():
                        if st_l.get('ln_tail') is not None:
                            st_l['ln_tail']()
                            st_l['ln_tail'] = None
                        S.op('sp', lambda t=t: SP.dma_start(out=tb[:, 0, :], in_=src[t * 128:(t + 1) * 128, :]),
                             reads=[('xd', l, t)], writes=[('tb', 0)], dma=('tbi', 0))
                    if attn:
                        def scores(h):
                            sbuf = h % 2
                            hp = h % 2
                            ch = h // 2
                            for j in range(5):
                                ks = (t - 4 + j) % 8
                                hasb = BIAS_ON_PE and j in (0, 3, 4)
                                S.op('pe', lambda sbuf=sbuf, j=j, ks=ks, hp=hp, ch=ch, hasb=hasb: T.matmul(
                                    PS[:, sbuf * 1024 + j * 128: sbuf * 1024 + (j + 1) * 128],
                                    lhsT=KTr[hp * 64:(hp + 1) * 64, ch, ks * 128:(ks + 1) * 128],
                                    rhs=QT[hp * 64:(hp + 1) * 64, pb, ch, ti * 128:(ti + 1) * 128],
                                    start=True, stop=(not hasb)),
                                    reads=[('K', ks), ('QT', pb)], writes=[BK(2 * sbuf), BK(2 * sbuf + 1)])
                                if hasb:
                                    if j == 0:
                                        brhs = mask0[:, :]
                                        bk = 'mask0'
                                    else:
                                        brhs = Ep[:, h, (j - 3) * 128:(j - 2) * 128]
                                        bk = 'Ep'
                                    S.op('pe', lambda sbuf=sbuf, j=j, brhs=brhs: T.matmul(
                                        PS[:, sbuf * 1024 + j * 128: sbuf * 1024 + (j + 1) * 128],
                                        lhsT=ident_bf[:, :], rhs=brhs, start=False, stop=True),
                                        reads=[bk, 'ident_bf'], writes=[BK(2 * sbuf), BK(2 * sbuf + 1)])
                            pbuf = h % 3
                            S.op('act', lambda sbuf=sbuf, pbuf=pbuf: A.activation(
                                out=PT[:, pbuf, :], in_=PS[:, sbuf * 1024: sbuf * 1024 + 640], func=AF.Exp),
                                writes=[BK(2 * sbuf), BK(2 * sbuf + 1), ('PT', pbuf)])
                            if not BIAS_ON_PE:
                                S.op('dve', lambda pbuf=pbuf, h=h: V.tensor_tensor(
                                    out=PT[:, pbuf, 384:640], in0=PT[:, pbuf, 384:640], in1=Ep[:, h, :], op=ALU.mult),
                                    reads=['Ep'], writes=[('PT', pbuf)])
                                S.op('dve', lambda pbuf=pbuf: V.memset(PT[0:64, pbuf, 64:128], 0.0),
                                     writes=[('PT', pbuf)])

                        pvb = {'b': None}

                        def pv(h):
                            hg = h % 4
                            if hg == 0:
                                pvb['b'] = 7
                            b = pvb['b']
                            pbuf = h % 3
                            for j in range(5):
                                vs = (t - 4 + j) % 8
                                S.op('pe', lambda b=b, hg=hg, j=j, vs=vs, pbuf=pbuf, h=h: T.matmul(
                                    bank(b, 65, hg * 65), lhsT=PT[:, pbuf, j * 128:(j + 1) * 128],
                                    rhs=Vr[:, vs, h, :], start=(j == 0), stop=(j == 4)),
                                    reads=[('PT', pbuf), ('V', vs)], writes=[BK(b)])
                            if hg == 3:
                                h0 = h - 3
                                gb = (h // 4) % 2
                                pv4 = bank(b, 260).rearrange("p (h d) -> p h d", h=4)
                                S.op('dve', lambda pv4=pv4: V.tensor_scalar(
                                    out=small[:, 0:4].unsqueeze(2), in0=pv4[:, :, 64:65], scalar1=1e-30, scalar2=None,
                                    op0=ALU.add), writes=[BK(b), 'rs'])
                                S.op('dve', lambda: V.reciprocal(out=small[:, 0:4], in_=small[:, 0:4]), writes=['rs'])
                                S.op('dve', lambda pv4=pv4, gb=gb: V.tensor_tensor(
                                    out=gtmp[:, gb, :].rearrange("p (h d) -> p h d", h=4), in0=pv4[:, :, 0:64],
                                    in1=small[:, 0:4].unsqueeze(2).to_broadcast([128, 4, 64]), op=ALU.mult),
                                    reads=['rs'], writes=[BK(b), ('gtmp', gb)])
                                S.op('pool', lambda gb=gb, h0=h0: G.tensor_tensor(
                                    out=siluz[:, pb, ti, h0 * 64:(h0 + 4) * 64], in0=gtmp[:, gb, :],
                                    in1=siluz[:, pb, ti, h0 * 64:(h0 + 4) * 64], op=ALU.mult),
                                    reads=[('gtmp', gb)], writes=[('siluz', pb, ti)])

                        scores(0)
                        scores(1)
                        for h in range(16):
                            if h + 2 < 16:
                                scores(h + 2)
                            fill()
                            pv(h)
                            if h == 3:
                                ln_handoff()
                    mbank = {}

                    def mscore(hm):
                        b = gen_bank()
                        mbank[hm] = b
                        for mb in range(2):
                            S.op('pe', lambda b=b, mb=mb, hm=hm: T.matmul(
                                bank(b, 128, mb * 128), lhsT=kmemT[:, hm, mb * 128:(mb + 1) * 128],
                                rhs=QmT[:, pb, hm, ti * 128:(ti + 1) * 128], start=True, stop=True),
                                reads=['kmemT', ('QmT', pb)], writes=[BK(b)])
                        mbuf = hm % 2
                        S.op('act', lambda b=b, mbuf=mbuf: A.activation(out=PTm[:, mbuf, :], in_=bank(b, 256), func=AF.Exp),
                             writes=[BK(b), ('PTm', mbuf)])

                    def mpv(hm):
                        b = mbank[hm]
                        mbuf = hm % 2
                        for mb in range(2):
                            S.op('pe', lambda b=b, mb=mb, hm=hm, mbuf=mbuf: T.matmul(
                                bank(b, 129, 256), lhsT=PTm[:, mbuf, mb * 128:(mb + 1) * 128],
                                rhs=vmem[:, mb, hm, :], start=(mb == 0), stop=(mb == 1)),
                                reads=[('PTm', mbuf), 'vmem'], writes=[BK(b)])
                        S.op('dve', lambda b=b: V.tensor_scalar(
                            out=small[:, 8:9], in0=bank(b, 1, 256 + 128), scalar1=1e-30, scalar2=None, op0=ALU.add),
                            writes=[BK(b), 'rsm'])
                        S.op('dve', lambda: V.reciprocal(out=small[:, 8:9], in_=small[:, 8:9]), writes=['rsm'])
                        S.op('dve', lambda b=b, hm=hm: V.scalar_tensor_tensor(
                            out=siluz[:, pb, ti, 1024 + hm * 128:1024 + (hm + 1) * 128], in0=bank(b, 128, 256),
                            scalar=small[:, 8:9], in1=siluz[:, pb, ti, 1024 + hm * 128:1024 + (hm + 1) * 128],
                            op0=ALU.mult, op1=ALU.mult),
                            reads=['rsm'], writes=[BK(b), ('siluz', pb, ti)])

                    mscore(0)
                    for hm in range(4):
                        if hm + 1 < 4:
                            mscore(hm + 1)
                        mpv(hm)
                        fill()
                        if hm == 0 and not attn:
                            ln_handoff()
                    if _os.environ.get("KDBG") and int(_os.environ["KDBG"]) == t:
                        S.op('sp', lambda: SP.dma_start(out=dbg_out, in_=siluz[:, pb, ti, :]),
                             reads=[('siluz', pb, ti)], writes=[('xd', l + 1, 0)], dma='gdbg')
                    if _os.environ.get("KDBG2") == 'PT' and int(_os.environ["KDBG"]) == t:
                        S.op('sp', lambda: SP.dma_start(out=dbg2[:, 0:1280].rearrange("p (a b) -> p a b", a=2), in_=PT[:, :, :]),
                             reads=[('PT', 0), ('PT', 1)], writes=[('xd', l + 1, 1)], dma='gdbg2')
                        S.op('sp', lambda: SP.dma_start(out=dbg2[:, 1280:1792].rearrange("p (a b) -> p a b", a=2), in_=PTm[:, :, :]),
                             reads=[('PTm', 0), ('PTm', 1)], writes=[('xd', l + 1, 2)], dma='gdbg3')
                    chunks = list(range(12)) if attn else list(range(8, 12))
                    for g0 in range(0, len(chunks), 4):
                        grp = chunks[g0:g0 + 4]
                        b = gen_bank()
                        psb = bank(b).bitcast(BF16)
                        for q, j in enumerate(grp):
                            S.op('pe', lambda q=q, j=j, psb=psb: T.transpose(
                                out=psb[:, q * 128:(q + 1) * 128], in_=siluz[:, pb, ti, j * 128:(j + 1) * 128],
                                identity=ident_bf[:, :]),
                                reads=[('siluz', pb, ti), 'ident_bf'], writes=[BK(b)])
                        j0 = grp[0]
                        S.op('dve', lambda psb=psb, j0=j0: V.tensor_copy(
                            out=yT[:, j0:j0 + 4, :], in_=psb[:, 0:512].rearrange("p (c t) -> p c t", c=4)),
                            writes=[BK(b), ('yT', j0 // 4)])
                    fill()
                    while conv_items:
                        conv_chunk(conv_items.pop(0))
                    tslot = 0
                    for half in range(2):
                        b = gen_bank()
                        for ec in range(12):
                            if attn or ec >= 8:
                                lh = yT[:, ec, :]
                                rk = [('yT', ec // 4)]
                            else:
                                lh = szT[:, pb, ec, ti * 128:(ti + 1) * 128]
                                rk = [('szT', pb, ec)]
                            S.op('pe', lambda b=b, ec=ec, half=half, lh=lh: T.matmul(
                                bank(b), lhsT=lh, rhs=w_out_sb[:, ec, half * 512:(half + 1) * 512],
                                start=(ec == 0), stop=(ec == 11)),
                                reads=rk + [('wout', half)], writes=[BK(b)])
                        S.op('dve', lambda b=b, half=half, tslot=tslot: V.scalar_tensor_tensor(
                            out=tb[:, tslot, half * 512:(half + 1) * 512], in0=tb[:, tslot, half * 512:(half + 1) * 512],
                            scalar=ALPHA, in1=bank(b), op0=ALU.mult, op1=ALU.add),
                            writes=[BK(b), ('tb', tslot)])
                        S.op('dve', lambda half=half, tslot=tslot: V.bn_stats(
                            out=stats[:, tslot, half, :], in_=tb[:, tslot, half * 512:(half + 1) * 512]),
                            reads=[('tb', tslot)], writes=[('stats', tslot)])
                    def ln_tail(t=t, tslot=tslot):
                        S.op('dve', lambda tslot=tslot: V.bn_aggr(
                            out=mv[:, tslot, 0:2], in_=stats[:, tslot, :, :].rearrange("p a b -> p (a b)")),
                            reads=[('stats', tslot)], writes=[('mv', tslot)])
                        S.op('dve', lambda tslot=tslot: V.tensor_scalar(
                            out=small[:, 20:21], in0=mv[:, tslot, 1:2], scalar1=EPS, scalar2=None, op0=ALU.add),
                            reads=[('mv', tslot)], writes=[('mvb', tslot)])
                        S.op('pool', lambda tslot=tslot: G.tensor_tensor(
                            out=mv[:, tslot, 2:3], in0=small[:, 20:21], in1=small[:, 17:18], op=ALU.pow),
                            reads=[('mvb', tslot), 'epsc'], writes=[('mvc', tslot)])
                        S.op('dve', lambda tslot=tslot: V.tensor_scalar(
                            out=tb[:, tslot, :], in0=tb[:, tslot, :], scalar1=mv[:, tslot, 0:1], scalar2=mv[:, tslot, 2:3],
                            op0=ALU.subtract, op1=ALU.mult),
                            reads=[('mv', tslot), ('mvc', tslot)], writes=[('tb', tslot)])
                        S.op('pool', lambda tslot=tslot: G.tensor_tensor(
                            out=tb[:, tslot, :], in0=tb[:, tslot, :], in1=lng[:, :], op=ALU.mult),
                            reads=['lng'], writes=[('tb', tslot)])
                        S.op('pool', lambda tslot=tslot: G.tensor_tensor(
                            out=tb[:, tslot, :], in0=tb[:, tslot, :], in1=lnb[:, :], op=ALU.add),
                            reads=['lnb'], writes=[('tb', tslot)])
                        if l == DEPTH - 1 and t < HALO:
                            return
                        r0 = dst_rows(t)
                        S.op('sp', lambda tslot=tslot, r0=r0: SP.dma_start(out=dst[r0:r0 + 128, :], in_=tb[:, tslot, :]),
                             reads=[('tb', tslot)], writes=[('xd', l + 1, t)], dma=('tbo', tslot))
                    st_l['ln_tail'] = ln_tail

                for ti_, t_ in enumerate(tiles):
                    tile_body(ti_, t_)
                    fill()
                for f_, a_ in nxt:
                    f_(*a_)

            def xload(bi, l=l, src=src, st_l=st_l, blocks=blocks):
                if bi >= len(blocks):
                    return
                for t in blocks[bi][0]:
                    slot = st_l['nload'] % NXS
                    st_l['nload'] += 1
                    st_l['xslot'][t] = slot
                    S.op('sp', lambda slot=slot, t=t: SP.dma_start(out=xs[:, slot, :], in_=src[t * 128:(t + 1) * 128, :]),
                         reads=[('xd', l, t)], writes=[('xs', slot)], dma=('xs', slot))

            if nb > 0:
                xload(0)
                xload(1)
                for f_, a_ in phase1(0):
                    f_(*a_)
            for bi in range(nb):
                nxt = phase1(bi + 1) if bi + 1 < nb else []
                xload(bi + 2)
                if bi + 1 == nb and li + 1 < len(layers):
                    load_weights(layers[li + 1])
                phase2(bi, nxt)
            if st_l.get('ln_tail') is not None:
                st_l['ln_tail']()
                st_l['ln_tail'] = None
            S.fence()
            if li + 1 < len(layers):
                load_wkv(layers[li + 1])
                load_w_out(layers[li + 1])

        outkeys = [('xd', last + 1, t) for t in range(NT)]
        S.op('sp', None, reads=outkeys)
        S.finalize(st)
    return nc, S


def _host_layout(inputs):
    x = np.asarray(inputs["x"], dtype=np.float32)
    mem = np.asarray(inputs["mem"], dtype=np.float32)
    rel_bias = np.asarray(inputs["rel_bias"], dtype=np.float32)
    conv_w = np.asarray(inputs["conv_w"], dtype=np.float32)
    k = np.arange(128)[:, None]
    q = np.arange(128)[None, :]
    idx3 = np.minimum(q - k + 128, 128) + 128
    idx4 = (q - k) + 128
    idx = np.concatenate([idx3, idx4], axis=1)
    bt = np.ascontiguousarray(np.transpose(rel_bias[:, :, idx], (0, 2, 1, 3)))
    chb = np.ascontiguousarray(np.broadcast_to(rel_bias[:, None, :, 256], (2, 128, 16)))
    convw = np.ascontiguousarray(np.transpose(conv_w.reshape(2, 3, 8, 128), (0, 3, 2, 1)))
    common = dict(
        ident=np.eye(128, dtype=np.float32),
        w_in=np.asarray(inputs["w_in"], dtype=np.float32),
        w_mem_kv=np.asarray(inputs["w_mem_kv"], dtype=np.float32),
        w_out=np.asarray(inputs["w_out"], dtype=np.float32),
        bt=bt, chb=chb, convw=convw,
        ln_g=np.asarray(inputs["ln_g"], dtype=np.float32),
        ln_b=np.asarray(inputs["ln_b"], dtype=np.float32),
    )
    per_core = []
    for c in range(8):
        b, half = c // 2, c % 2
        s0 = half * 4096
        w0 = s0 - HALO * 128
        xw = np.zeros((NT * 128, D), np.float32)
        lo = max(w0, 0)
        xw[lo - w0:] = x[b, lo:s0 + 4096]
        vt = np.zeros((NT * 128,), np.float32)
        vt[lo - w0:] = 1.0
        valid = np.ascontiguousarray(vt.reshape(NT, 128).T)
        vrow = np.ascontiguousarray(np.broadcast_to(vt.reshape(NT, 128)[None, :, 126:128], (128, NT, 2)))
        m = dict(common)
        m.update(xin=xw, valid=valid, vrow=vrow, mem=np.ascontiguousarray(mem[b]))
        per_core.append(m)
    return per_core


_CACHE = {}


def _get_program(layers):
    key = tuple(layers)
    if key not in _CACHE:
        _CACHE[key] = build_program(list(layers))[0]
    return _CACHE[key]


FUSED = True


def kernel(**inputs):
    per_core = _host_layout(inputs)
    groups = [[0, 1, 2, 3]] if FUSED else [[0], [1], [2], [3]]
    full = {k: per_core[0][k] for k in ("w_in", "w_mem_kv", "w_out", "ln_g", "ln_b")}
    for grp in groups:
        nc = _get_program(grp)
        sl = {k: np.ascontiguousarray(v[grp[0]:grp[-1] + 1]) for k, v in full.items()}
        maps = []
        for c in range(8):
            m = dict(per_core[c])
            m.update(sl)
            maps.append(m)
        res = run_bass_kernel_spmd(nc, maps, core_ids=list(range(8)))
        outs = [r["xout"] for r in res.results]
        if grp[-1] != DEPTH - 1:
            for c in range(8):
                per_core[c]["xin"] = outs[c]
    out = np.zeros((4, 8192, D), np.float32)
    for c in range(8):
        b, half = c // 2, c % 2
        out[b, half * 4096:(half + 1) * 4096] = outs[c]
    return out
```

```python
import math
from contextlib import ExitStack

import numpy as np
import concourse.bass as bass
import concourse.mybir as mybir
from concourse.bass_utils import run_bass_kernel_spmd

F32 = mybir.dt.float32
BF16 = mybir.dt.bfloat16
AF = mybir.ActivationFunctionType
ALU = mybir.AluOpType

D = 1024
NIN = 5120
NT = 42
HALO = 10
NMAIN = 32
DEPTH = 4
ALPHA = (2.0 * DEPTH) ** 0.25
EPS = 1e-5
NXS = 4
BIAS_ON_PE = False

ENG_ATTR = {'pe': 'tensor', 'act': 'scalar', 'dve': 'vector', 'pool': 'gpsimd', 'sp': 'sync'}


class Op:
    __slots__ = ('eng', 'fn', 'reads', 'writes', 'dma', 'waits', 'signal', 'ev', 'vc', 'sigcount')

    def __init__(self, eng, fn, reads, writes, dma):
        self.eng = eng
        self.fn = fn
        self.reads = reads
        self.writes = writes
        self.dma = dma
        self.waits = []
        self.signal = False
        self.ev = None
        self.vc = None
        self.sigcount = 0


class Sched:
    def __init__(self, nc, same_engine_sync=('act', 'dve', 'pool')):
        self.nc = nc
        self.ops = []
        self.same_sync = set(same_engine_sync)

    def op(self, eng, fn, reads=(), writes=(), dma=None):
        self.ops.append(Op(eng, fn, tuple(reads), tuple(writes), dma))

    def fence(self):
        shared = {}
        for e in ENG_ATTR:
            self.ops.append(Op(e, None, ('__fence__', shared), (), None))

    def finalize(self, stack):
        nc = self.nc
        state = {}
        vcs = {e: {} for e in ENG_ATTR}
        eng_count = {e: 0 for e in ENG_ATTR}
        dma_count = {}
        evop = {}
        dmavc = {}
        for op in self.ops:
            e = op.eng
            deps = {}
            if len(op.reads) == 2 and op.reads[0] == '__fence__':
                shared = op.reads[1]
                if not shared:
                    for e2, c2 in eng_count.items():
                        if c2 > 0:
                            shared[('e', e2)] = c2 - 1
                    for dk2, c2 in dma_count.items():
                        shared[dk2] = c2
                deps.update(shared)
                op.reads = ()
            for r in op.reads:
                st = state.get(r)
                if st:
                    for (k, i) in st[0]:
                        if deps.get(k, -1) < i:
                            deps[k] = i
            for w in op.writes:
                st = state.get(w)
                if st:
                    for (k, i) in st[0]:
                        if deps.get(k, -1) < i:
                            deps[k] = i
                    for (k, i) in st[1]:
                        if deps.get(k, -1) < i:
                            deps[k] = i
            vc = vcs[e]
            myk = ('e', e)
            for k, i in deps.items():
                if k == myk and e not in self.same_sync:
                    continue
                if vc.get(k, -1) >= i:
                    continue
                op.waits.append((k, i))
            for (k, i) in op.waits:
                if k[0] == 'e':
                    src = evop[(k, i)]
                    src.signal = True
                    svc = src.vc
                else:
                    svc = dmavc[(k, i)]
                for kk, ii in svc.items():
                    if vc.get(kk, -1) < ii:
                        vc[kk] = ii
                if vc.get(k, -1) < i:
                    vc[k] = i
            if op.fn is None:
                op.ev = None
                continue
            if op.dma is None:
                idx = eng_count[e]
                eng_count[e] = idx + 1
                op.ev = (myk, idx)
                evop[op.ev] = op
                snap = dict(vc)
                snap[myk] = idx
                op.vc = snap
                if e not in self.same_sync:
                    vc[myk] = idx
            else:
                dk = ('d', op.dma)
                c = dma_count.get(dk, 0) + 1
                dma_count[dk] = c
                op.ev = (dk, c)
                dmavc[op.ev] = dict(vc)
            for r in op.reads:
                st = state.setdefault(r, [[], []])
                rl = [x for x in st[1] if x[0] != op.ev[0]]
                rl.append(op.ev)
                st[1] = rl
            for w in op.writes:
                state[w] = [[op.ev], []]
        sems = {}
        for e in ENG_ATTR:
            sems[('e', e)] = stack.enter_context(nc.semaphore('sem_' + e))
        for dk in dma_count:
            sems[dk] = stack.enter_context(nc.semaphore('dsem_%d' % len(sems)))
        cnt = {e: 0 for e in ENG_ATTR}
        for op in self.ops:
            if op.dma is None and op.signal:
                cnt[op.eng] += 1
                op.sigcount = cnt[op.eng]
        for op in self.ops:
            engobj = getattr(nc, ENG_ATTR[op.eng])
            for (k, i) in op.waits:
                val = evop[(k, i)].sigcount if k[0] == 'e' else 16 * i
                engobj.wait_ge(sems[k], val)
            if op.fn is None:
                continue
            ins = op.fn()
            if op.dma is not None:
                ins.then_inc(sems[op.ev[0]], 16)
            elif op.signal:
                ins.then_inc(sems[('e', op.eng)], 1)
        self.stats = dict(nops=len(self.ops), sig=cnt, nsems=len(sems))


def layer_plan(l):
    if l == 0:
        blocks = [((0, 1), 'kv'), ((2, 3), 'kv')] + [((t, t + 1), 'full') for t in range(4, NT, 2)]
    elif l == 1:
        blocks = [((4,), 'cu'), ((5,), 'full')] + [((t, t + 1), 'full') for t in range(6, NT, 2)]
    elif l == 2:
        blocks = [((5, 6), 'kv'), ((7, 8), 'kv'), ((9,), 'full')] + \
                 [((t, t + 1), 'full') for t in range(10, NT, 2)]
    else:
        blocks = [((9,), 'cu')] + [((t, t + 1), 'full') for t in range(10, NT, 2)]
    return blocks


def build_program(layers, same_engine_sync=('dve', 'pool'), max_blocks=None):
    nc = bass.Bass("TRN2", target_bir_lowering=False, dynamic_dma_scratch_size=8192)
    last = layers[-1]
    final = (last == DEPTH - 1)
    dr = {}

    def din(name, shape):
        dr[name] = nc.dram_tensor(name, list(shape), F32, kind="ExternalInput").ap()

    din("xin", (NT * 128, D))
    din("valid", (128, NT))
    din("vrow", (128, NT, 2))
    din("mem", (256, D))
    din("ident", (128, 128))
    din("w_in", (len(layers), D, NIN))
    din("w_mem_kv", (len(layers), D, D))
    din("w_out", (len(layers), 1536, D))
    din("bt", (2, 128, 16, 256))
    din("chb", (2, 128, 16))
    din("convw", (2, 128, 8, 3))
    din("ln_g", (len(layers), D))
    din("ln_b", (len(layers), D))
    if final:
        xout = nc.dram_tensor("xout", [NMAIN * 128, D], F32, kind="ExternalOutput").ap()
    else:
        xout = nc.dram_tensor("xout", [NT * 128, D], F32, kind="ExternalOutput").ap()
    scratch = {}
    for l in layers[:-1]:
        scratch[l + 1] = nc.dram_tensor("xs%d" % (l + 1), [NT * 128, D], F32).ap()

    S = Sched(nc, same_engine_sync)
    with ExitStack() as st:
        def sb(name, shape, dt):
            return st.enter_context(nc.sbuf_tensor("sb_" + name, list(shape), dt))

        w_in_sb = sb("w_in_sb", (128, 8, NIN), BF16)
        w_out_sb = sb("w_out_sb", (128, 12, D), BF16)
        kmemT = sb("kmemT", (128, 4, 256), BF16)
        vmem = sb("vmem", (128, 2, 4, 129), BF16)
        valid_sb = sb("valid_sb", (128, NT), F32)
        vrow_sb = sb("vrow_sb", (128, NT, 2), F32)
        ident = sb("ident", (128, 128), F32)
        ident_bf = sb("ident_bf", (128, 128), BF16)
        lng = sb("lng", (128, D), F32)
        lnb = sb("lnb", (128, D), F32)
        xs = sb("xs", (128, NXS, D), F32)
        xT = sb("xT", (128, 8, 256), BF16)
        tb = sb("tb", (128, 1, D), F32)
        siluz = sb("siluz", (128, 2, 2, 1536), BF16)
        yT = sb("yT", (128, 12, 128), BF16)
        QmT = sb("QmT", (128, 2, 4, 256), BF16)
        PTm = sb("PTm", (128, 2, 256), BF16)
        small = sb("small", (128, 64), F32)
        stats = sb("stats", (128, 2, 2, 6), F32)
        mv = sb("mv", (128, 2, 4), F32)
        gtmp = sb("gtmp", (128, 2, 256), F32)
        chb_sb = sb("chb_sb", (128, 16), F32)
        mask0 = sb("mask0", (128, 128), BF16) if BIAS_ON_PE else None
        import os as _os
        dbg_out = nc.dram_tensor("dbg", [128, 1536], BF16, kind="ExternalOutput").ap() if _os.environ.get("KDBG") else None
        dbg2 = nc.dram_tensor("dbg2", [128, 2048], BF16, kind="ExternalOutput").ap() if _os.environ.get("KDBG2") else None
        cw_sb = sb("cw_sb", (128, 8, 3), F32)
        MIXB = 53248
        MIX = sb("MIX", (128, MIXB // 2), BF16)

        def mview(off, shape, dt):
            esz = 4 if dt == F32 else 2
            n = 1
            for d_ in shape:
                n *= d_
            v = MIX[:, off // 2: off // 2 + (n * esz) // 2]
            if dt == F32:
                v = v.bitcast(F32)
            if len(shape) == 1:
                return v
            names = "abcd"[:len(shape)]
            pat = "p (%s) -> p %s" % (" ".join(names), " ".join(names))
            kw = {names[i]: shape[i] for i in range(len(shape) - 1)}
            return v.rearrange(pat, **kw)

        QT = mview(0, (2, 8, 256), BF16)
        KTr = mview(8192, (8, 1024), BF16)
        Vr = mview(24576, (8, 16, 65), BF16)
        PT = mview(41216, (3, 640), BF16)
        Ep = mview(45056, (16, 256), BF16)
        cu = mview(0, (2, 8, 258), F32)
        p0s = mview(16512, (2, 8, 256), F32)
        p1s = mview(32896, (2, 256), F32)
        cacc = mview(34944, (2, 256), F32)
        szT = mview(36992, (2, 8, 256), BF16)
        memx = mview(0, (2, D), F32)
        memT = mview(8192, (8, 256), BF16)
        wkv = mview(12288, (8, D), BF16)
        btst = tb[:, :, :].rearrange("p a (h c) -> p (a h) c", c=256)

        PS = st.enter_context(nc.psum_tensor("PS", [128, 8 * 512], F32))
        print("SBUF bytes remaining per partition:", nc.sbuf_bytes_remaining)

        def bank(b, n=512, off=0):
            return PS[:, b * 512 + off: b * 512 + off + n]

        T, A, V, G, SP = nc.tensor, nc.scalar, nc.vector, nc.gpsimd, nc.sync
        BK = lambda b: ('bank', b)

        rot = {'i': 0}

        def gen_bank():
            b = 4 + (rot['i'] % rot.get('n', 3))
            rot['i'] += 1
            return b

        S.op('sp', lambda: SP.dma_start(out=ident[:, :], in_=dr["ident"]), writes=['ident'], dma='ident')
        S.op('sp', lambda: SP.dma_start(out=valid_sb[:, :], in_=dr["valid"]), writes=['valid'], dma='valid')
        S.op('sp', lambda: SP.dma_start(out=vrow_sb[:, :, :], in_=dr["vrow"]), writes=['vrow'], dma='vrow')
        S.op('dve', lambda: V.tensor_copy(out=ident_bf[:, :], in_=ident[:, :]), reads=['ident'], writes=['ident_bf'])
        S.op('pool', lambda: G.memset(small[:, 16:17], EPS), writes=['epsc'])
        S.op('pool', lambda: G.memset(small[:, 17:18], -0.5), writes=['epsc'])

        WCH = ((1024, 3072), (0, 1024), (3072, 5120))

        def wkey(col):
            return ('win', 0 if 1024 <= col < 3072 else (1 if col < 1024 else 2))

        def load_weights(l):
            l = layers.index(l)
            for c, (c0, c1) in enumerate(WCH):
                S.op('pool', lambda l=l, c0=c0, c1=c1: G.dma_start(
                    out=w_in_sb[:, :, c0:c1],
                    in_=dr["w_in"][l, :, c0:c1].rearrange("(k p) n -> p k n", p=128)),
                    writes=[('win', c)], dma=('win', c))

        def load_w_out(l):
            l = layers.index(l)
            S.op('pool', lambda l=l: G.dma_start(
                out=w_out_sb[:, :, :], in_=dr["w_out"][l, :, :].rearrange("(k p) n -> p k n", p=128)),
                writes=[('wout', 0), ('wout', 1)], dma=('wout', 0))

        import os as _os2
        _PROBE2 = _os2.environ.get('KPROBE2', '')

        def load_wkv(l):
            l = layers.index(l)
            S.op('pool', lambda l=l: G.dma_start(
                out=wkv[:, :, :], in_=dr["w_mem_kv"][l, :, :].rearrange("(k p) n -> p k n", p=128)),
                writes=[('wkv', 0), ('wkv', 1)], dma=('wkv', 0))

        def kv_prologue(l):
            l = layers.index(l)
            S.op('sp', lambda: SP.dma_start(out=memx, in_=dr["mem"].rearrange("(j p) d -> p j d", p=128)),
                 writes=['memx'], dma='memx')
            if _PROBE2 == 'c1':
                return
            for j in range(2):
                for half in range(2):
                    b = gen_bank()
                    for q in range(4):
                        kc = half * 4 + q
                        S.op('pe', lambda b=b, q=q, kc=kc, j=j: T.transpose(
                            out=bank(b, 128, q * 128), in_=memx[:, j, kc * 128:(kc + 1) * 128], identity=ident[:, :]),
                            reads=['memx', 'ident'], writes=[BK(b)])
                    S.op('act', lambda b=b, half=half, j=j: A.copy(
                        out=memT[:, half * 4:half * 4 + 4, j * 128:(j + 1) * 128],
                        in_=bank(b).rearrange("p (c t) -> p c t", c=4)),
                        writes=[BK(b), ('memT', j, half)])
            if _PROBE2 == 'c2':
                return
            memT_keys = [('memT', j, half) for j in range(2) for half in range(2)]
            for h in range(4):
                b = gen_bank()
                for kc in range(8):
                    S.op('pe', lambda b=b, h=h, kc=kc: T.matmul(
                        bank(b, 256), lhsT=wkv[:, kc, h * 128:(h + 1) * 128], rhs=memT[:, kc, :],
                        start=(kc == 0), stop=(kc == 7)),
                        reads=[('wkv', 0)] + memT_keys, writes=[BK(b)])
                S.op('act', lambda b=b, h=h: A.copy(out=kmemT[:, h, :], in_=bank(b, 256)),
                     writes=[BK(b), 'kmemT'])
            if _PROBE2 == 'c3':
                return
            for mb in range(2):
                b = gen_bank()
                for kc in range(8):
                    S.op('pe', lambda b=b, mb=mb, kc=kc: T.matmul(
                        bank(b), lhsT=memT[:, kc, mb * 128:(mb + 1) * 128], rhs=wkv[:, kc, 512:1024],
                        start=(kc == 0), stop=(kc == 7)),
                        reads=[('wkv', 1)] + memT_keys, writes=[BK(b)])
                S.op('dve', lambda b=b, mb=mb: V.tensor_copy(
                    out=vmem[:, mb, :, 0:128], in_=bank(b).rearrange("p (h d) -> p h d", h=4)),
                    writes=[BK(b), 'vmem'])
                S.op('pool', lambda mb=mb: G.memset(vmem[:, mb, :, 128:129], 1.0), writes=['vmem'])

        import os as _os
        PROBE = _os.environ.get("KPROBE", "")
        if PROBE != "a":
            load_wkv(layers[0])
            load_weights(layers[0])
            load_w_out(layers[0])

        for li, l in enumerate(layers if PROBE not in ("a", "b") else []):
            attn = (l % 2 == 0)
            la = l // 2
            rot['n'] = 3 if attn else 4
            src = dr["xin"] if li == 0 else scratch[l]
            dst = xout if l == last else scratch[l + 1]

            def dst_rows(t, l=l):
                if l == DEPTH - 1:
                    return (t - HALO) * 128
                return t * 128

            kv_prologue(l)
            if PROBE == "c":
                break

            S.op('sp', lambda l=li: SP.dma_start(out=lng[:, :], in_=dr["ln_g"][l:l + 1, :].to_broadcast([128, D])),
                 writes=['lng'], dma='lng')
            S.op('sp', lambda l=li: SP.dma_start(out=lnb[:, :], in_=dr["ln_b"][l:l + 1, :].to_broadcast([128, D])),
                 writes=['lnb'], dma='lnb')
            if attn:
                S.op('sp', lambda la=la: SP.dma_start(out=chb_sb[:, :], in_=dr["chb"][la]), writes=['chb'], dma='chb')
                S.op('dve', lambda: V.tensor_scalar(out=small[:, 32:48], in0=chb_sb[:, :], scalar1=-1.0,
                                                    scalar2=None, op0=ALU.mult),
                     reads=['chb'], writes=['negchb'])
                for g in range(4):
                    S.op('sp', lambda la=la, g=g: SP.dma_start(out=btst, in_=dr["bt"][la, :, 4 * g:4 * g + 4, :]),
                         writes=[('tb', 0)], dma='btst')
                    for hh in range(4):
                        h = 4 * g + hh
                        S.op('act', lambda h=h, hh=hh: A.activation(
                            out=Ep[:, h, :], in_=btst[:, hh, :], func=(AF.Identity if BIAS_ON_PE else AF.Exp), bias=small[:, 32 + h:33 + h], scale=1.0),
                            reads=[('tb', 0), 'negchb'], writes=['Ep'])
                S.op('pool', lambda: G.memset(Ep[64:128, :, 128:192], (-30000.0 if BIAS_ON_PE else 0.0)), writes=['Ep'])
                if BIAS_ON_PE:
                    S.op('pool', lambda: G.memset(mask0[:, :], 0.0), writes=['mask0'])
                    S.op('pool', lambda: G.memset(mask0[0:64, 64:128], -30000.0), writes=['mask0'])
            else:
                S.op('sp', lambda la=la: SP.dma_start(out=cw_sb[:, :, :], in_=dr["convw"][la]), writes=['cw'], dma='cw')

            blocks = layer_plan(l)
            if max_blocks is not None:
                blocks = blocks[:max_blocks]
            nb = len(blocks)
            st_l = {'cu_prev': None, 'xslot': {}, 'nload': 0}
            plan_tiles = [t_ for tl_, _m in blocks for t_ in tl_]

            def phase1(bi, l=l, attn=attn, blocks=blocks, src=src, st_l=st_l):
                tiles, mode = blocks[bi]
                n = len(tiles)
                N = 128 * n
                pb = bi % 2
                items = []
                silu_items = []

                def xunit(ti, t):
                    slot = st_l['xslot'][t]
                    for half in range(2):
                        b = gen_bank()
                        for q in range(4):
                            kc = half * 4 + q
                            S.op('pe', lambda b=b, q=q, kc=kc, slot=slot: T.transpose(
                                out=bank(b, 128, q * 128), in_=xs[:, slot, kc * 128:(kc + 1) * 128], identity=ident[:, :]),
                                reads=[('xs', slot), 'ident'], writes=[BK(b)])
                        if half == 0:
                            S.op('act', lambda b=b, half=half, ti=ti: A.copy(
                                out=xT[:, half * 4:half * 4 + 4, ti * 128:(ti + 1) * 128],
                                in_=bank(b).rearrange("p (c t) -> p c t", c=4)),
                                writes=[BK(b), ('xT', ti, half)])
                        else:
                            S.op('dve', lambda b=b, half=half, ti=ti: V.tensor_copy(
                                out=xT[:, half * 4:half * 4 + 4, ti * 128:(ti + 1) * 128],
                                in_=bank(b).rearrange("p (c t) -> p c t", c=4)),
                                writes=[BK(b), ('xT', ti, half)])
                for ti_, t_ in enumerate(tiles):
                    items.append((xunit, (ti_, t_)))
                xTk = [('xT', ti, half) for ti in range(n) for half in range(2)]

                def tokmajor(ti, col0, evac):
                    b = gen_bank()
                    c = col0 // 512
                    for kc in range(8):
                        S.op('pe', lambda b=b, kc=kc, ti=ti, col0=col0: T.matmul(
                            bank(b), lhsT=xT[:, kc, ti * 128:(ti + 1) * 128], rhs=w_in_sb[:, kc, col0:col0 + 512],
                            start=(kc == 0), stop=(kc == 7)),
                            reads=[('xT', ti, 0), ('xT', ti, 1), wkey(col0)], writes=[BK(b)])
                    evac(b)

                def featmajor(col0, evac):
                    b = gen_bank()
                    c = col0 // 512
                    for kc in range(8):
                        S.op('pe', lambda b=b, kc=kc, col0=col0: T.matmul(
                            bank(b, N), lhsT=w_in_sb[:, kc, col0:col0 + 128], rhs=xT[:, kc, 0:N],
                            start=(kc == 0), stop=(kc == 7)),
                            reads=xTk + [wkey(col0)], writes=[BK(b)])
                    evac(b)

                if attn:
                    for ti, t in enumerate(tiles):
                        vs = t % 8
                        for hb in range(2):
                            def ev(b, t=t, vs=vs, hb=hb):
                                S.op('dve', lambda: V.tensor_scalar(
                                    out=Vr[:, vs, hb * 8:(hb + 1) * 8, 0:64],
                                    in0=bank(b).rearrange("p (h d) -> p h d", h=8),
                                    scalar1=valid_sb[:, t:t + 1], scalar2=None, op0=ALU.mult),
                                    reads=['valid'], writes=[BK(b), ('V', vs)])
                            items.append((tokmajor, (ti, 2048 + hb * 512, ev)))
                        S.op('pool', lambda vs=vs, t=t: G.tensor_copy(
                            out=Vr[:, vs, :, 64:65], in_=valid_sb[:, t:t + 1].unsqueeze(1).to_broadcast([128, 16, 1])),
                            reads=['valid', 'vmem', 'kmemT'], writes=[('V', vs)])
                    for j in range(8):
                        def ev(b, j=j):
                            for ti, t in enumerate(tiles):
                                ks = t % 8
                                if (j + ti) % 2 == 0:
                                    S.op('act', lambda ti=ti, ks=ks: A.copy(
                                        out=KTr[:, j, ks * 128:(ks + 1) * 128], in_=bank(b, 128, ti * 128)),
                                        writes=[BK(b), ('K', ks)])
                                else:
                                    S.op('dve', lambda ti=ti, ks=ks: V.tensor_copy(
                                        out=KTr[:, j, ks * 128:(ks + 1) * 128], in_=bank(b, 128, ti * 128)),
                                        writes=[BK(b), ('K', ks)])
                        items.append((featmajor, (1024 + j * 128, ev)))
                    if mode == 'full':
                        for j in range(8):
                            def ev(b, j=j):
                                S.op('act', lambda: A.mul(out=QT[:, pb, j, 0:N], in_=bank(b, N), mul=0.125),
                                     writes=[BK(b), ('QT', pb)])
                            items.append((featmajor, (j * 128, ev)))
                else:
                    cb = bi % 2
                    prev = st_l['cu_prev']
                    if prev is not None:
                        pcb, pN, pt = prev
                        S.op('pool', lambda cb=cb, pcb=pcb, pN=pN, pt=pt: G.tensor_tensor(
                            out=cu[:, cb, :, 0:2], in0=cu[:, pcb, :, pN:pN + 2],
                            in1=vrow_sb[:, pt:pt + 1, :].to_broadcast([128, 8, 2]), op=ALU.mult),
                            reads=[('cu', pcb), 'vrow'], writes=[('cu', cb)])
                    else:
                        S.op('pool', lambda cb=cb: G.memset(cu[:, cb, :, 0:2], 0.0), reads=['vmem', 'kmemT'], writes=[('cu', cb)])
                    for j in range(8):
                        pj = j % 2

                        def ev1(b, pj=pj):
                            S.op('act', lambda: A.copy(out=p1s[:, pj, 0:N], in_=bank(b, N)),
                                 writes=[BK(b), ('p1s', pj)])
                        items.append((featmajor, (1024 + j * 128, ev1)))

                        def ev2(b, j=j, pj=pj, cb=cb):
                            S.op('dve', lambda: V.tensor_tensor(out=cu[:, cb, j, 2:2 + N], in0=bank(b, N),
                                                                in1=p1s[:, pj, 0:N], op=ALU.mult),
                                 reads=[('p1s', pj)], writes=[BK(b), ('cu', cb)])
                        items.append((featmajor, (2048 + j * 128, ev2)))
                    st_l['cu_prev'] = (cb, N, tiles[-1])
                    if mode == 'full':
                        for j in range(8):
                            def ev(b, j=j):
                                S.op('act', lambda: A.copy(out=p0s[:, pb, j, 0:N], in_=bank(b, N)),
                                     writes=[BK(b), ('p0s', pb, j)])
                            items.append((featmajor, (j * 128, ev)))

                            def evz(b, j=j):
                                S.op('act', lambda: A.activation(out=szT[:, pb, j, 0:N], in_=bank(b, N), func=AF.Silu),
                                     writes=[BK(b), ('szT', pb, j)])
                            silu_items.append((featmajor, (3584 + j * 128, evz)))
                if mode == 'full':
                    for j in range(4):
                        def ev(b, j=j):
                            S.op('act', lambda: A.mul(out=QmT[:, pb, j, 0:N], in_=bank(b, N), mul=1.0 / math.sqrt(128.0)),
                                 writes=[BK(b), ('QmT', pb)])
                        items.append((featmajor, (3072 + j * 128, ev)))
                    zblocks = (0, 1, 2) if attn else (2,)
                    for ti, t in enumerate(tiles):
                        for zb in zblocks:
                            def ev(b, ti=ti, zb=zb):
                                S.op('act', lambda: A.activation(out=siluz[:, pb, ti, zb * 512:(zb + 1) * 512],
                                                                 in_=bank(b), func=AF.Silu),
                                     writes=[BK(b), ('siluz', pb, ti)])
                            silu_items.append((tokmajor, (ti, 3584 + zb * 512, ev)))
                st_l['nsilu'] = len(silu_items)
                return items + silu_items

            def phase2(bi, nxt, l=l, attn=attn, blocks=blocks, dst=dst, src=src, st_l=st_l, plan_tiles=plan_tiles):
                tiles, mode = blocks[bi]
                if mode != 'full':
                    for f_, a_ in nxt:
                        f_(*a_)
                    return
                pts = {'left': (22 if attn else 6) * len(tiles)}
                nsilu = [st_l.get('nsilu', 0)]

                def fill():
                    for _ in range(2):
                        if conv_items:
                            conv_chunk(conv_items.pop(0))
                    left = max(pts['left'], 1)
                    k = -(-len(nxt) // left)
                    for _ in range(min(k, len(nxt))):
                        f_, a_ = nxt.pop(0)
                        f_(*a_)
                    if 0 < len(nxt) < nsilu[0]:
                        while nxt:
                            f_, a_ = nxt.pop(0)
                            f_(*a_)
                    pts['left'] -= 1
                _d2 = _os.environ.get("KDBG2")
                if _d2 and int(_os.environ["KDBG"]) == tiles[0]:
                    if _d2 == 'xT':
                        S.op('sp', lambda: SP.dma_start(out=dbg2, in_=xT[:, :, :].rearrange("p a b -> p (a b)")),
                             reads=[('xT', 0, 0), ('xT', 0, 1), ('xT', 1, 0), ('xT', 1, 1)], writes=[('xd', l + 1, 1)], dma='gdbg2')
                    elif _d2 == 'w':
                        S.op('sp', lambda: SP.dma_start(out=dbg2, in_=w_in_sb[:, 0, 3072:5120]),
                             reads=[('win', 1), ('win', 2)], writes=[('xd', l + 1, 1)], dma='gdbg2')
                    elif _d2 == 'QT':
                        S.op('sp', lambda: SP.dma_start(out=dbg2.rearrange("p (a b) -> p a b", a=8), in_=QT[:, bi % 2, :, :]),
                             reads=[('QT', bi % 2)], writes=[('xd', l + 1, 1)], dma='gdbg2')
                    elif _d2 == 'K':
                        S.op('sp', lambda: SP.dma_start(out=dbg2[:, 0:1024].rearrange("p (a b) -> p a b", a=8),
                                                        in_=KTr[:, :, (tiles[0] % 8) * 128:(tiles[0] % 8) * 128 + 128]),
                             reads=[('K', tiles[0] % 8)], writes=[('xd', l + 1, 1)], dma='gdbg2')
                    elif _d2 == 'QmT':
                        S.op('sp', lambda: SP.dma_start(out=dbg2[:, 0:1024].rearrange("p (a b) -> p a b", a=4), in_=QmT[:, bi % 2, :, :]),
                             reads=[('QmT', bi % 2)], writes=[('xd', l + 1, 1)], dma='gdbg2')
                    elif _d2 == 'V':
                        S.op('sp', lambda: SP.dma_start(out=dbg2[:, 0:1040].rearrange("p (a b) -> p a b", a=16), in_=Vr[:, tiles[0] % 8, :, :]),
                             reads=[('V', tiles[0] % 8)], writes=[('xd', l + 1, 1)], dma='gdbg2')
                    elif _d2 == 'sz':
                        S.op('sp', lambda: SP.dma_start(out=dbg2[:, 0:1536], in_=siluz[:, bi % 2, 0, :]),
                             reads=[('siluz', bi % 2, 0)], writes=[('xd', l + 1, 1)], dma='gdbg2')
                n = len(tiles)
                N = 128 * n
                pb = bi % 2
                conv_items = []
                if not attn:
                    cb = bi % 2

                    def conv_chunk(j):
                        aj = j % 2
                        S.op('dve', lambda j=j, aj=aj: V.tensor_scalar(
                            out=cacc[:, aj, 0:N], in0=cu[:, cb, j, 0:N], scalar1=cw_sb[:, j, 0:1], scalar2=None,
                            op0=ALU.mult), reads=[('cu', cb), 'cw'], writes=[('cacc', aj)])
                        for k in (1, 2):
                            S.op('dve', lambda j=j, aj=aj, k=k: V.scalar_tensor_tensor(
                                out=cacc[:, aj, 0:N], in0=cu[:, cb, j, k:k + N], scalar=cw_sb[:, j, k:k + 1],
                                in1=cacc[:, aj, 0:N], op0=ALU.mult, op1=ALU.add),
                                reads=[('cu', cb), 'cw'], writes=[('cacc', aj)])
                        S.op('pool', lambda j=j, aj=aj: G.tensor_tensor(
                            out=cacc[:, aj, 0:N], in0=cacc[:, aj, 0:N], in1=p0s[:, pb, j, 0:N], op=ALU.mult),
                            reads=[('p0s', pb, j)], writes=[('cacc', aj)])
                        S.op('pool', lambda j=j, aj=aj: G.tensor_tensor(
                            out=szT[:, pb, j, 0:N], in0=cacc[:, aj, 0:N], in1=szT[:, pb, j, 0:N], op=ALU.mult),
                            reads=[('cacc', aj)], writes=[('szT', pb, j)])

                    for j_ in range(8):
                        conv_items.append(j_)

                def tile_body(ti, t):
                    def ln_handoff():
                        if st_l.get('ln_tail') is not None:
                            st_l['ln_tail']()
                            st_l['ln_tail'] = None
                        S.op('sp', lambda t=t: SP.dma_start(out=tb[:, 0, :], in_=src[t * 128:(t + 1) * 128, :]),
                             reads=[('xd', l, t)], writes=[('tb', 0)], dma=('tbi', 0))
                    if attn:
                        def scores(h):
                            sbuf = h % 2
                            hp = h % 2
                            ch = h // 2
                            for j in range(5):
                                ks = (t - 4 + j) % 8
                                hasb = BIAS_ON_PE and j in (0, 3, 4)
                                S.op('pe', lambda sbuf=sbuf, j=j, ks=ks, hp=hp, ch=ch, hasb=hasb: T.matmul(
                                    PS[:, sbuf * 1024 + j * 128: sbuf * 1024 + (j + 1) * 128],
                                    lhsT=KTr[hp * 64:(hp + 1) * 64, ch, ks * 128:(ks + 1) * 128],
                                    rhs=QT[hp * 64:(hp + 1) * 64, pb, ch, ti * 128:(ti + 1) * 128],
                                    start=True, stop=(not hasb)),
                                    reads=[('K', ks), ('QT', pb)], writes=[BK(2 * sbuf), BK(2 * sbuf + 1)])
                                if hasb:
                                    if j == 0:
                                        brhs = mask0[:, :]
                                        bk = 'mask0'
                                    else:
                                        brhs = Ep[:, h, (j - 3) * 128:(j - 2) * 128]
                                        bk = 'Ep'
                                    S.op('pe', lambda sbuf=sbuf, j=j, brhs=brhs: T.matmul(
                                        PS[:, sbuf * 1024 + j * 128: sbuf * 1024 + (j + 1) * 128],
                                        lhsT=ident_bf[:, :], rhs=brhs, start=False, stop=True),
                                        reads=[bk, 'ident_bf'], writes=[BK(2 * sbuf), BK(2 * sbuf + 1)])
                            pbuf = h % 3
                            S.op('act', lambda sbuf=sbuf, pbuf=pbuf: A.activation(
                                out=PT[:, pbuf, :], in_=PS[:, sbuf * 1024: sbuf * 1024 + 640], func=AF.Exp),
                                writes=[BK(2 * sbuf), BK(2 * sbuf + 1), ('PT', pbuf)])
                            if not BIAS_ON_PE:
                                S.op('dve', lambda pbuf=pbuf, h=h: V.tensor_tensor(
                                    out=PT[:, pbuf, 384:640], in0=PT[:, pbuf, 384:640], in1=Ep[:, h, :], op=ALU.mult),
                                    reads=['Ep'], writes=[('PT', pbuf)])
                                S.op('dve', lambda pbuf=pbuf: V.memset(PT[0:64, pbuf, 64:128], 0.0),
                                     writes=[('PT', pbuf)])

                        pvb = {'b': None}

                        def pv(h):
                            hg = h % 4
                            if hg == 0:
                                pvb['b'] = 7
                            b = pvb['b']
                            pbuf = h % 3
                            for j in range(5):
                                vs = (t - 4 + j) % 8
                                S.op('pe', lambda b=b, hg=hg, j=j, vs=vs, pbuf=pbuf, h=h: T.matmul(
                                    bank(b, 65, hg * 65), lhsT=PT[:, pbuf, j * 128:(j + 1) * 128],
                                    rhs=Vr[:, vs, h, :], start=(j == 0), stop=(j == 4)),
                                    reads=[('PT', pbuf), ('V', vs)], writes=[BK(b)])
                            if hg == 3:
                                h0 = h - 3
                                gb = (h // 4) % 2
                                pv4 = bank(b, 260).rearrange("p (h d) -> p h d", h=4)
                                S.op('dve', lambda pv4=pv4: V.tensor_scalar(
                                    out=small[:, 0:4].unsqueeze(2), in0=pv4[:, :, 64:65], scalar1=1e-30, scalar2=None,
                                    op0=ALU.add), writes=[BK(b), 'rs'])
                                S.op('dve', lambda: V.reciprocal(out=small[:, 0:4], in_=small[:, 0:4]), writes=['rs'])
                                S.op('dve', lambda pv4=pv4, gb=gb: V.tensor_tensor(
                                    out=gtmp[:, gb, :].rearrange("p (h d) -> p h d", h=4), in0=pv4[:, :, 0:64],
                                    in1=small[:, 0:4].unsqueeze(2).to_broadcast([128, 4, 64]), op=ALU.mult),
                                    reads=['rs'], writes=[BK(b), ('gtmp', gb)])
                                S.op('pool', lambda gb=gb, h0=h0: G.tensor_tensor(
                                    out=siluz[:, pb, ti, h0 * 64:(h0 + 4) * 64], in0=gtmp[:, gb, :],
                                    in1=siluz[:, pb, ti, h0 * 64:(h0 + 4) * 64], op=ALU.mult),
                                    reads=[('gtmp', gb)], writes=[('siluz', pb, ti)])

                        scores(0)
                        scores(1)
                        for h in range(16):
                            if h + 2 < 16:
                                scores(h + 2)
                            fill()
                            pv(h)
                            if h == 3:
                                ln_handoff()
                    mbank = {}

                    def mscore(hm):
                        b = gen_bank()
                        mbank[hm] = b
                        for mb in range(2):
                            S.op('pe', lambda b=b, mb=mb, hm=hm: T.matmul(
                                bank(b, 128, mb * 128), lhsT=kmemT[:, hm, mb * 128:(mb + 1) * 128],
                                rhs=QmT[:, pb, hm, ti * 128:(ti + 1) * 128], start=True, stop=True),
                                reads=['kmemT', ('QmT', pb)], writes=[BK(b)])
                        mbuf = hm % 2
                        S.op('act', lambda b=b, mbuf=mbuf: A.activation(out=PTm[:, mbuf, :], in_=bank(b, 256), func=AF.Exp),
                             writes=[BK(b), ('PTm', mbuf)])

                    def mpv(hm):
                        b = mbank[hm]
                        mbuf = hm % 2
                        for mb in range(2):
                            S.op('pe', lambda b=b, mb=mb, hm=hm, mbuf=mbuf: T.matmul(
                                bank(b, 129, 256), lhsT=PTm[:, mbuf, mb * 128:(mb + 1) * 128],
                                rhs=vmem[:, mb, hm, :], start=(mb == 0), stop=(mb == 1)),
                                reads=[('PTm', mbuf), 'vmem'], writes=[BK(b)])
                        S.op('dve', lambda b=b: V.tensor_scalar(
                            out=small[:, 8:9], in0=bank(b, 1, 256 + 128), scalar1=1e-30, scalar2=None, op0=ALU.add),
                            writes=[BK(b), 'rsm'])
                        S.op('dve', lambda: V.reciprocal(out=small[:, 8:9], in_=small[:, 8:9]), writes=['rsm'])
                        S.op('dve', lambda b=b, hm=hm: V.scalar_tensor_tensor(
                            out=siluz[:, pb, ti, 1024 + hm * 128:1024 + (hm + 1) * 128], in0=bank(b, 128, 256),
                            scalar=small[:, 8:9], in1=siluz[:, pb, ti, 1024 + hm * 128:1024 + (hm + 1) * 128],
                            op0=ALU.mult, op1=ALU.mult),
                            reads=['rsm'], writes=[BK(b), ('siluz', pb, ti)])

                    mscore(0)
                    for hm in range(4):
                        if hm + 1 < 4:
                            mscore(hm + 1)
                        mpv(hm)
                        fill()
                        if hm == 0 and not attn:
                            ln_handoff()
                    if _os.environ.get("KDBG") and int(_os.environ["KDBG"]) == t:
                        S.op('sp', lambda: SP.dma_start(out=dbg_out, in_=siluz[:, pb, ti, :]),
                             reads=[('siluz', pb, ti)], writes=[('xd', l + 1, 0)], dma='gdbg')
                    if _os.environ.get("KDBG2") == 'PT' and int(_os.environ["KDBG"]) == t:
                        S.op('sp', lambda: SP.dma_start(out=dbg2[:, 0:1280].rearrange("p (a b) -> p a b", a=2), in_=PT[:, :, :]),
                             reads=[('PT', 0), ('PT', 1)], writes=[('xd', l + 1, 1)], dma='gdbg2')
                        S.op('sp', lambda: SP.dma_start(out=dbg2[:, 1280:1792].rearrange("p (a b) -> p a b", a=2), in_=PTm[:, :, :]),
                             reads=[('PTm', 0), ('PTm', 1)], writes=[('xd', l + 1, 2)], dma='gdbg3')
                    chunks = list(range(12)) if attn else list(range(8, 12))
                    for g0 in range(0, len(chunks), 4):
                        grp = chunks[g0:g0 + 4]
                        b = gen_bank()
                        psb = bank(b).bitcast(BF16)
                        for q, j in enumerate(grp):
                            S.op('pe', lambda q=q, j=j, psb=psb: T.transpose(
                                out=psb[:, q * 128:(q + 1) * 128], in_=siluz[:, pb, ti, j * 128:(j + 1) * 128],
                                identity=ident_bf[:, :]),
                                reads=[('siluz', pb, ti), 'ident_bf'], writes=[BK(b)])
                        j0 = grp[0]
                        S.op('dve', lambda psb=psb, j0=j0: V.tensor_copy(
                            out=yT[:, j0:j0 + 4, :], in_=psb[:, 0:512].rearrange("p (c t) -> p c t", c=4)),
                            writes=[BK(b), ('yT', j0 // 4)])
                    fill()
                    while conv_items:
                        conv_chunk(conv_items.pop(0))
                    tslot = 0
                    for half in range(2):
                        b = gen_bank()
                        for ec in range(12):
                            if attn or ec >= 8:
                                lh = yT[:, ec, :]
                                rk = [('yT', ec // 4)]
                            else:
                                lh = szT[:, pb, ec, ti * 128:(ti + 1) * 128]
                                rk = [('szT', pb, ec)]
                            S.op('pe', lambda b=b, ec=ec, half=half, lh=lh: T.matmul(
                                bank(b), lhsT=lh, rhs=w_out_sb[:, ec, half * 512:(half + 1) * 512],
                                start=(ec == 0), stop=(ec == 11)),
                                reads=rk + [('wout', half)], writes=[BK(b)])
                        S.op('dve', lambda b=b, half=half, tslot=tslot: V.scalar_tensor_tensor(
                            out=tb[:, tslot, half * 512:(half + 1) * 512], in0=tb[:, tslot, half * 512:(half + 1) * 512],
                            scalar=ALPHA, in1=bank(b), op0=ALU.mult, op1=ALU.add),
                            writes=[BK(b), ('tb', tslot)])
                        S.op('dve', lambda half=half, tslot=tslot: V.bn_stats(
                            out=stats[:, tslot, half, :], in_=tb[:, tslot, half * 512:(half + 1) * 512]),
                            reads=[('tb', tslot)], writes=[('stats', tslot)])
                    def ln_tail(t=t, tslot=tslot):
                        S.op('dve', lambda tslot=tslot: V.bn_aggr(
                            out=mv[:, tslot, 0:2], in_=stats[:, tslot, :, :].rearrange("p a b -> p (a b)")),
                            reads=[('stats', tslot)], writes=[('mv', tslot)])
                        S.op('dve', lambda tslot=tslot: V.tensor_scalar(
                            out=small[:, 20:21], in0=mv[:, tslot, 1:2], scalar1=EPS, scalar2=None, op0=ALU.add),
                            reads=[('mv', tslot)], writes=[('mvb', tslot)])
                        S.op('pool', lambda tslot=tslot: G.tensor_tensor(
                            out=mv[:, tslot, 2:3], in0=small[:, 20:21], in1=small[:, 17:18], op=ALU.pow),
                            reads=[('mvb', tslot), 'epsc'], writes=[('mvc', tslot)])
                        S.op('dve', lambda tslot=tslot: V.tensor_scalar(
                            out=tb[:, tslot, :], in0=tb[:, tslot, :], scalar1=mv[:, tslot, 0:1], scalar2=mv[:, tslot, 2:3],
                            op0=ALU.subtract, op1=ALU.mult),
                            reads=[('mv', tslot), ('mvc', tslot)], writes=[('tb', tslot)])
                        S.op('pool', lambda tslot=tslot: G.tensor_tensor(
                            out=tb[:, tslot, :], in0=tb[:, tslot, :], in1=lng[:, :], op=ALU.mult),
                            reads=['lng'], writes=[('tb', tslot)])
                        S.op('pool', lambda tslot=tslot: G.tensor_tensor(
                            out=tb[:, tslot, :], in0=tb[:, tslot, :], in1=lnb[:, :], op=ALU.add),
                            reads=['lnb'], writes=[('tb', tslot)])
                        if l == DEPTH - 1 and t < HALO:
                            return
                        r0 = dst_rows(t)
                        S.op('sp', lambda tslot=tslot, r0=r0: SP.dma_start(out=dst[r0:r0 + 128, :], in_=tb[:, tslot, :]),
                             reads=[('tb', tslot)], writes=[('xd', l + 1, t)], dma=('tbo', tslot))
                    st_l['ln_tail'] = ln_tail

                for ti_, t_ in enumerate(tiles):
                    tile_body(ti_, t_)
                    fill()
                for f_, a_ in nxt:
                    f_(*a_)

            def xload(bi, l=l, src=src, st_l=st_l, blocks=blocks):
                if bi >= len(blocks):
                    return
                for t in blocks[bi][0]:
                    slot = st_l['nload'] % NXS
                    st_l['nload'] += 1
                    st_l['xslot'][t] = slot
                    S.op('sp', lambda slot=slot, t=t: SP.dma_start(out=xs[:, slot, :], in_=src[t * 128:(t + 1) * 128, :]),
                         reads=[('xd', l, t)], writes=[('xs', slot)], dma=('xs', slot))

            if nb > 0:
                xload(0)
                xload(1)
                for f_, a_ in phase1(0):
                    f_(*a_)
            for bi in range(nb):
                nxt = phase1(bi + 1) if bi + 1 < nb else []
                xload(bi + 2)
                if bi + 1 == nb and li + 1 < len(layers):
                    load_weights(layers[li + 1])
                phase2(bi, nxt)
            if st_l.get('ln_tail') is not None:
                st_l['ln_tail']()
                st_l['ln_tail'] = None
            S.fence()
            if li + 1 < len(layers):
                load_wkv(layers[li + 1])
                load_w_out(layers[li + 1])

        outkeys = [('xd', last + 1, t) for t in range(NT)]
        S.op('sp', None, reads=outkeys)
        S.finalize(st)
    return nc, S


def _host_layout(inputs):
    x = np.asarray(inputs["x"], dtype=np.float32)
    mem = np.asarray(inputs["mem"], dtype=np.float32)
    rel_bias = np.asarray(inputs["rel_bias"], dtype=np.float32)
    conv_w = np.asarray(inputs["conv_w"], dtype=np.float32)
    k = np.arange(128)[:, None]
    q = np.arange(128)[None, :]
    idx3 = np.minimum(q - k + 128, 128) + 128
    idx4 = (q - k) + 128
    idx = np.concatenate([idx3, idx4], axis=1)
    bt = np.ascontiguousarray(np.transpose(rel_bias[:, :, idx], (0, 2, 1, 3)))
    chb = np.ascontiguousarray(np.broadcast_to(rel_bias[:, None, :, 256], (2, 128, 16)))
    convw = np.ascontiguousarray(np.transpose(conv_w.reshape(2, 3, 8, 128), (0, 3, 2, 1)))
    common = dict(
        ident=np.eye(128, dtype=np.float32),
        w_in=np.asarray(inputs["w_in"], dtype=np.float32),
        w_mem_kv=np.asarray(inputs["w_mem_kv"], dtype=np.float32),
        w_out=np.asarray(inputs["w_out"], dtype=np.float32),
        bt=bt, chb=chb, convw=convw,
        ln_g=np.asarray(inputs["ln_g"], dtype=np.float32),
        ln_b=np.asarray(inputs["ln_b"], dtype=np.float32),
    )
    per_core = []
    for c in range(8):
        b, half = c // 2, c % 2
        s0 = half * 4096
        w0 = s0 - HALO * 128
        xw = np.zeros((NT * 128, D), np.float32)
        lo = max(w0, 0)
        xw[lo - w0:] = x[b, lo:s0 + 4096]
        vt = np.zeros((NT * 128,), np.float32)
        vt[lo - w0:] = 1.0
        valid = np.ascontiguousarray(vt.reshape(NT, 128).T)
        vrow = np.ascontiguousarray(np.broadcast_to(vt.reshape(NT, 128)[None, :, 126:128], (128, NT, 2)))
        m = dict(common)
        m.update(xin=xw, valid=valid, vrow=vrow, mem=np.ascontiguousarray(mem[b]))
        per_core.append(m)
    return per_core


_CACHE = {}


def _get_program(layers):
    key = tuple(layers)
    if key not in _CACHE:
        _CACHE[key] = build_program(list(layers))[0]
    return _CACHE[key]


FUSED = True


def kernel(**inputs):
    per_core = _host_layout(inputs)
    groups = [[0, 1, 2, 3]] if FUSED else [[0], [1], [2], [3]]
    full = {k: per_core[0][k] for k in ("w_in", "w_mem_kv", "w_out", "ln_g", "ln_b")}
    for grp in groups:
        nc = _get_program(grp)
        sl = {k: np.ascontiguousarray(v[grp[0]:grp[-1] + 1]) for k, v in full.items()}
        maps = []
        for c in range(8):
            m = dict(per_core[c])
            m.update(sl)
            maps.append(m)
        res = run_bass_kernel_spmd(nc, maps, core_ids=list(range(8)))
        outs = [r["xout"] for r in res.results]
        if grp[-1] != DEPTH - 1:
            for c in range(8):
                per_core[c]["xin"] = outs[c]
    out = np.zeros((4, 8192, D), np.float32)
    for c in range(8):
        b, half = c // 2, c % 2
        out[b, half * 4096:(half + 1) * 4096] = outs[c]
    return out
```
